# Optimizing a Trainium2 kernel written in Bass

```python
import jax, jax.numpy as jnp
from jax import lax
import numpy as np

D_MODEL = 4096
BATCH = 4
SEQ = 2048
DEPTH = 1

CHUNK = 64
Q_BLOCK = 128
N_MEM = 256
MIX_WIDTH = D_MODEL
FOX_WIDTH = MIX_WIDTH // 2
FOX_HEAD_DIM = 128
FOX_HEADS = FOX_WIDTH // FOX_HEAD_DIM
RWKV_WIDTH = MIX_WIDTH - FOX_WIDTH
RWKV_HEAD_DIM = 64
RWKV_HEADS = RWKV_WIDTH // RWKV_HEAD_DIM
DECAY_LORA = 96
ICLR_LORA = 96
GATE_LORA = 256
FOX_COLS = 3 * FOX_WIDTH + FOX_HEADS
RWKV_SHIFT_COLS = 3 * RWKV_WIDTH + DECAY_LORA + ICLR_LORA + GATE_LORA
IN_COLS = FOX_COLS + RWKV_SHIFT_COLS
XATTN_HEADS = 4
XATTN_HEAD_DIM = D_MODEL // XATTN_HEADS
D_FF = 11008
RMS_EPS = 1e-6
RWKV_GN_EPS = 64e-5
NEG_INF = -1e30

kernel_name = 'fox_rwkv7_macaron_sandwich_memory_layer'


def rms_norm(x, g):
    xf = x.astype(jnp.float32)
    y = xf * lax.rsqrt(jnp.mean(xf * xf, axis=-1, keepdims=True) + RMS_EPS)
    return (y * g.astype(jnp.float32)).astype(x.dtype)


def swiglu(h, w_gate, w_up, w_down):
    u = jax.nn.silu(jnp.einsum('bsd,df->bsf', h, w_gate)) * jnp.einsum('bsd,df->bsf', h, w_up)
    return jnp.einsum('bsf,fd->bsd', u, w_down)


def fox_attention(q, k, v, logf):
    seq = q.shape[1]
    c = jnp.cumsum(logf, axis=1).transpose(0, 2, 1)
    scale = FOX_HEAD_DIM ** -0.5
    outs = []
    for blk in range(seq // Q_BLOCK):
        q0, q1 = blk * Q_BLOCK, (blk + 1) * Q_BLOCK
        s = jnp.einsum('bqhd,bkhd->bhqk', q[:, q0:q1], k[:, :q1]).astype(jnp.float32) * scale
        s = s + c[:, :, q0:q1, None] - c[:, :, None, :q1]
        causal = (q0 + jnp.arange(Q_BLOCK))[:, None] >= jnp.arange(q1)[None, :]
        p = jax.nn.softmax(jnp.where(causal, s, NEG_INF), axis=-1)
        outs.append(jnp.einsum('bhqk,bkhd->bqhd', p.astype(v.dtype), v[:, :q1]))
    return jnp.concatenate(outs, axis=1)


def rwkv7_recurrence(r, decay, k, v, kk, a):
    b, _, h, n = r.shape

    def step(state, inp):
        r_t, w_t, k_t, v_t, kk_t, a_t = inp
        s_kk = jnp.einsum('bhvk,bhk->bhv', state, kk_t)
        state = (state * w_t[:, :, None, :]
                 - s_kk[..., None] * (kk_t * a_t)[:, :, None, :]
                 + v_t[..., None] * k_t[:, :, None, :])
        return state, jnp.einsum('bhvk,bhk->bhv', state, r_t)

    xs = tuple(jnp.swapaxes(t.astype(jnp.float32), 0, 1) for t in (r, decay, k, v, kk, a))
    state0 = jnp.zeros((b, h, n, n), jnp.float32)
    _, y = lax.scan(step, state0, xs)
    return jnp.swapaxes(y, 0, 1)


def parallel_mixer(h, w_in, fox_f_bias, rwkv_mu, rwkv_w0, rwkv_w_up, rwkv_a0, rwkv_a_up,
                   rwkv_g_up, rwkv_k_k, rwkv_k_a, rwkv_r_k, rwkv_ln_w, rwkv_ln_b, w_out):
    b, s, _ = h.shape
    proj = jnp.einsum('bsd,dc->bsc', h, w_in)
    fox_proj, rwkv_proj = proj[..., :FOX_COLS], proj[..., FOX_COLS:]

    q, k, v, f_logit = jnp.split(fox_proj, [FOX_WIDTH, 2 * FOX_WIDTH, 3 * FOX_WIDTH], axis=-1)
    fox_heads = lambda t: t.reshape(b, s, FOX_HEADS, FOX_HEAD_DIM)
    logf = jax.nn.log_sigmoid((f_logit + fox_f_bias).astype(jnp.float32))
    y_fox = fox_attention(fox_heads(q), fox_heads(k), fox_heads(v), logf).reshape(b, s, FOX_WIDTH)

    prev = jnp.pad(rwkv_proj, ((0, 0), (1, 0), (0, 0)))[:, :-1]
    z = rwkv_proj + rwkv_mu * (prev - rwkv_proj)
    r, kr, vr, xw, xa, xg = jnp.split(
        z, [RWKV_WIDTH, 2 * RWKV_WIDTH, 3 * RWKV_WIDTH, 3 * RWKV_WIDTH + DECAY_LORA,
            3 * RWKV_WIDTH + DECAY_LORA + ICLR_LORA], axis=-1)
    w_log = -jax.nn.softplus(-(rwkv_w0 + jnp.tanh(xw) @ rwkv_w_up).astype(jnp.float32)) - 0.5
    decay = jnp.exp(-jnp.exp(w_log))
    a = jax.nn.sigmoid((rwkv_a0 + xa @ rwkv_a_up).astype(jnp.float32))
    g = (jax.nn.sigmoid(xg) @ rwkv_g_up).astype(jnp.float32)
    hn = lambda t: t.astype(jnp.float32).reshape(b, s, RWKV_HEADS, RWKV_HEAD_DIM)
    pn = lambda p: p.astype(jnp.float32).reshape(RWKV_HEADS, RWKV_HEAD_DIM)
    rf, kf, vf, decay, a = hn(r), hn(kr), hn(vr), hn(decay), hn(a)
    kk = kf * pn(rwkv_k_k)
    kk = kk / jnp.maximum(jnp.sqrt(jnp.sum(kk * kk, axis=-1, keepdims=True)), 1e-12)
    kf = kf * (1.0 + (a - 1.0) * pn(rwkv_k_a))
    y = rwkv7_recurrence(rf, decay, kf, vf, kk, a)
    mean = jnp.mean(y, axis=-1, keepdims=True)
    var = jnp.mean(jnp.square(y - mean), axis=-1, keepdims=True)
    yn = ((y - mean) * lax.rsqrt(var + RWKV_GN_EPS)).reshape(b, s, RWKV_WIDTH)
    yn = yn * rwkv_ln_w.astype(jnp.float32) + rwkv_ln_b.astype(jnp.float32)
    bonus = (jnp.sum(rf * kf * pn(rwkv_r_k), axis=-1, keepdims=True) * vf).reshape(b, s, RWKV_WIDTH)
    y_rwkv = ((yn + bonus) * g).astype(h.dtype)

    return jnp.einsum('bsc,cd->bsd', jnp.concatenate([y_fox, y_rwkv], axis=-1), w_out)


def cross_attention(h, m, wq, wk, wv, wo):
    b, s, _ = h.shape
    q = jnp.einsum('bsd,dc->bsc', h, wq).reshape(b, s, XATTN_HEADS, XATTN_HEAD_DIM)
    k = jnp.einsum('bmd,dc->bmc', m, wk).reshape(b, m.shape[1], XATTN_HEADS, XATTN_HEAD_DIM)
    v = jnp.einsum('bmd,dc->bmc', m, wv).reshape(b, m.shape[1], XATTN_HEADS, XATTN_HEAD_DIM)
    sc = jnp.einsum('bqhd,bkhd->bhqk', q, k).astype(jnp.float32) * (XATTN_HEAD_DIM ** -0.5)
    p = jax.nn.softmax(sc, axis=-1).astype(v.dtype)
    o = jnp.einsum('bhqk,bkhd->bqhd', p, v).reshape(b, s, D_MODEL)
    return jnp.einsum('bsc,cd->bsd', o, wo)


def setup_inputs(seed: int = 0) -> dict:
    key = jax.random.key(seed)
    ks = iter(jax.random.split(key, 48))
    nrm = lambda shape, scale: scale * jax.random.normal(next(ks), shape, jnp.float32)
    uni = lambda shape, lo, hi: jax.random.uniform(next(ks), shape, jnp.float32, lo, hi)
    gain = lambda width: 1.0 + nrm((DEPTH, width), 0.05)
    L, D = DEPTH, D_MODEL
    return {
        'x': nrm((BATCH, SEQ, D), 1.0),
        'mem': nrm((BATCH, N_MEM, D), 1.0),
        'ffn1_pre_g': gain(D),
        'ffn1_w_gate': nrm((L, D, D_FF), D ** -0.5),
        'ffn1_w_up': nrm((L, D, D_FF), D ** -0.5),
        'ffn1_w_down': nrm((L, D_FF, D), D_FF ** -0.5),
        'ffn1_post_g': gain(D),
        'mix_pre_g': gain(D),
        'w_in': nrm((L, D, IN_COLS), D ** -0.5),
        'fox_f_bias': uni((L, FOX_HEADS), 1.0, 5.0),
        'rwkv_mu': uni((L, RWKV_SHIFT_COLS), 0.0, 1.0),
        'rwkv_w0': uni((L, RWKV_WIDTH), -6.0, -1.0),
        'rwkv_w_up': nrm((L, DECAY_LORA, RWKV_WIDTH), 0.1),
        'rwkv_a0': nrm((L, RWKV_WIDTH), 0.1),
        'rwkv_a_up': nrm((L, ICLR_LORA, RWKV_WIDTH), ICLR_LORA ** -0.5),
        'rwkv_g_up': nrm((L, GATE_LORA, RWKV_WIDTH), GATE_LORA ** -0.5),
        'rwkv_k_k': 0.85 + nrm((L, RWKV_WIDTH), 0.02),
        'rwkv_k_a': 1.0 + nrm((L, RWKV_WIDTH), 0.02),
        'rwkv_r_k': nrm((L, RWKV_WIDTH), 0.1),
        'rwkv_ln_w': gain(RWKV_WIDTH),
        'rwkv_ln_b': nrm((L, RWKV_WIDTH), 0.01),
        'w_out': nrm((L, MIX_WIDTH, D), MIX_WIDTH ** -0.5),
        'mix_post_g': gain(D),
        'xattn_pre_g': gain(D),
        'mem_norm_g': gain(D),
        'xattn_wq': nrm((L, D, D), D ** -0.5),
        'xattn_wk': nrm((L, D, D), D ** -0.5),
        'xattn_wv': nrm((L, D, D), D ** -0.5),
        'xattn_wo': nrm((L, D, D), D ** -0.5),
        'xattn_post_g': gain(D),
        'ffn2_pre_g': gain(D),
        'ffn2_w_gate': nrm((L, D, D_FF), D ** -0.5),
        'ffn2_w_up': nrm((L, D, D_FF), D ** -0.5),
        'ffn2_w_down': nrm((L, D_FF, D), D_FF ** -0.5),
        'ffn2_post_g': gain(D),
    }


def reference(x, mem, ffn1_pre_g, ffn1_w_gate, ffn1_w_up, ffn1_w_down, ffn1_post_g,
              mix_pre_g, w_in, fox_f_bias, rwkv_mu, rwkv_w0, rwkv_w_up, rwkv_a0, rwkv_a_up,
              rwkv_g_up, rwkv_k_k, rwkv_k_a, rwkv_r_k, rwkv_ln_w, rwkv_ln_b, w_out, mix_post_g,
              xattn_pre_g, mem_norm_g, xattn_wq, xattn_wk, xattn_wv, xattn_wo, xattn_post_g,
              ffn2_pre_g, ffn2_w_gate, ffn2_w_up, ffn2_w_down, ffn2_post_g):
    for l in range(DEPTH):
        x = x + 0.5 * rms_norm(swiglu(rms_norm(x, ffn1_pre_g[l]), ffn1_w_gate[l], ffn1_w_up[l],
                                      ffn1_w_down[l]), ffn1_post_g[l])
        mixed = parallel_mixer(rms_norm(x, mix_pre_g[l]), w_in[l], fox_f_bias[l], rwkv_mu[l],
                               rwkv_w0[l], rwkv_w_up[l], rwkv_a0[l], rwkv_a_up[l], rwkv_g_up[l],
                               rwkv_k_k[l], rwkv_k_a[l], rwkv_r_k[l], rwkv_ln_w[l], rwkv_ln_b[l],
                               w_out[l])
        x = x + rms_norm(mixed, mix_post_g[l])
        xa = cross_attention(rms_norm(x, xattn_pre_g[l]), rms_norm(mem, mem_norm_g[l]),
                             xattn_wq[l], xattn_wk[l], xattn_wv[l], xattn_wo[l])
        x = x + rms_norm(xa, xattn_post_g[l])
        x = x + 0.5 * rms_norm(swiglu(rms_norm(x, ffn2_pre_g[l]), ffn2_w_gate[l], ffn2_w_up[l],
                                      ffn2_w_down[l]), ffn2_post_g[l])
    return x
```

```python
import numpy as np
from concourse.bass_utils import run_bass_kernel_spmd
import concourse.bass as bass
import concourse.mybir as mybir

F32 = mybir.dt.float32
BF16 = mybir.dt.bfloat16
AF = mybir.ActivationFunctionType
ALU = mybir.AluOpType
_uid = [0]


def un(s):
    _uid[0] += 1
    return f"{s}_{_uid[0]}"


class A:
    def __init__(self, nc):
        from contextlib import ExitStack
        self.nc = nc
        self.st = ExitStack()

    def __enter__(self):
        self.st.__enter__()
        return self

    def __exit__(self, *a):
        return self.st.__exit__(*a)

    def sb(self, name, shape, dt):
        return self.st.enter_context(self.nc.sbuf_tensor(un(name), shape, dt))

    def ps(self, name, shape, dt=None):
        return self.st.enter_context(self.nc.psum_tensor(un(name), shape, dt if dt is not None else F32))


class Ev:
    def __init__(self, nc, name, step=1):
        self.sem = nc.alloc_semaphore(name)
        self.n = 0
        self.step = step

    def mark(self, ins):
        ins.then_inc(self.sem, self.step)
        self.n += self.step
        return (self, self.n)


class Cx:
    def __init__(self, nc):
        self.nc = nc
        self.pe = Ev(nc, "e_pe")
        self.act = Ev(nc, "e_act")
        self.dve = Ev(nc, "e_dve")
        self.pool = Ev(nc, "e_pool")
        self.waited = {}
        self._n = 0
        self.dma_evs = []

    def ev(self, name, step=16):
        self._n += 1
        return Ev(self.nc, f"{name}_{self._n}", step)

    def wait(self, eng, tok):
        if tok is None:
            return
        ev, v = tok
        if v <= 0:
            return
        key = (id(eng), id(ev))
        if self.waited.get(key, 0) >= v:
            return
        self.waited[key] = v
        eng.wait_ge(ev.sem, v)

    def evof(self, eng):
        nc = self.nc
        if eng is nc.tensor:
            return self.pe
        if eng is nc.scalar:
            return self.act
        if eng is nc.vector:
            return self.dve
        if eng is nc.gpsimd:
            return self.pool
        raise ValueError

    def op(self, eng, ins, deps=()):
        return self.evof(eng).mark(ins)


def gemm(cx, act, KC, T, Ws, n_tiles, ngt, wbufs, wsems, ps_sets, epilogue, pre_tok=None, wstate=None):
    nc = cx.nc
    nW = len(Ws)
    NG = ngt * 128
    n_groups = (n_tiles + ngt - 1) // ngt
    if wstate is None:
        wstate = {}
    wfree = wstate.setdefault("wfree", [None, None])
    gi0 = wstate.get("gi", 0)
    ps_free = [None, None]
    nth = (T + 511) // 512
    tile_i = 0
    last_tok = None
    for g in range(n_groups):
        b = (gi0 + g) % 2
        gt = min(ngt, n_tiles - g * ngt)
        ncols = gt * 128
        wv = wbufs[b][:, 0:nW * KC * NG].rearrange("p (w k n) -> p w k n", w=nW, k=KC)
        cx.wait(nc.gpsimd, wfree[b])
        for wi, W in enumerate(Ws):
            src = W.rearrange("(k p) n -> p k n", p=128)[:, :, g * NG:g * NG + ncols]
            wtok = wsems[b].mark(nc.gpsimd.dma_start(out=wv[:, wi, :, 0:ncols], in_=src))
        cx.wait(nc.tensor, wtok)
        if g == 0:
            cx.wait(nc.tensor, pre_tok)
        for jt in range(gt):
            j = g * ngt + jt
            s = tile_i % 2
            cx.wait(nc.tensor, ps_free[s])
            ins = None
            for wi in range(nW):
                for k in range(KC):
                    for th in range(nth):
                        t0 = th * 512
                        t1 = min(T, t0 + 512)
                        ins = nc.tensor.matmul(ps_sets[s][wi][:, t0:t1], lhsT=wv[:, wi, k, jt * 128:(jt + 1) * 128],
                                               rhs=act[:, k, t0:t1], start=(k == 0), stop=(k == KC - 1))
            pe_tok = cx.pe.mark(ins)
            last_tok = pe_tok
            ps_free[s] = epilogue(j, ps_sets[s], pe_tok)
            tile_i += 1
        wfree[b] = last_tok
    wstate["gi"] = gi0 + n_groups
    return last_tok


def psum_fence(cx):
    nc = cx.nc
    cx.wait(nc.tensor, (cx.act, cx.act.n))
    cx.wait(nc.tensor, (cx.dve, cx.dve.n))


def barrier(cx, engines=None):
    nc = cx.nc
    engs = engines or [nc.tensor, nc.scalar, nc.vector, nc.gpsimd, nc.sync]
    evs = [cx.pe, cx.act, cx.dve, cx.pool] + cx.dma_evs
    for e in engs:
        for ev in evs:
            cx.wait(e, (ev, ev.n))


def dma_ev(cx, name):
    ev = Ev(cx.nc, f"{name}_{len(cx.dma_evs)}", 16)
    cx.dma_evs.append(ev)
    return ev


def rstd_from_stats(cx, out_sb, st_ps, D, eps, pre_tok):
    nc = cx.nc
    cx.wait(nc.vector, pre_tok)
    t = cx.dve.mark(nc.vector.tensor_scalar(out=out_sb, in0=st_ps, scalar1=1.0 / D, scalar2=eps, op0=ALU.mult, op1=ALU.add))
    cx.wait(nc.vector, t)
    cx.wait(nc.scalar, t)
    t = cx.act.mark(nc.scalar.activation(out=out_sb, in_=out_sb, func=AF.Sqrt))
    cx.wait(nc.vector, t)
    t = cx.dve.mark(nc.vector.reciprocal(out=out_sb, in_=out_sb))
    cx.wait(nc.vector, t)
    return t


def norm_in_tokmajor(cx, x_dram, Tt, DC, g_sb, act_out, xT_dram, C, eps, pre_tok=None):
    nc = cx.nc
    D = DC * 128
    ntt = Tt // 128
    ncg = DC // 4
    with (nc.sbuf_tensor(un("n0_xt0"), [128, D], F32) as xt0, nc.sbuf_tensor(un("n0_xt1"), [128, D], F32) as xt1,
          nc.sbuf_tensor(un("n0_xTt"), [128, DC, 128], F32) as xTt,
          nc.sbuf_tensor(un("n0_sq0"), [128, 4, 128], F32) as sq0, nc.sbuf_tensor(un("n0_sq1"), [128, 4, 128], F32) as sq1,
          nc.sbuf_tensor(un("n0_rstd"), [128, 128], F32) as rstd,
          nc.psum_tensor(un("n0_pT0"), [128, 4, 128], F32) as pT0, nc.psum_tensor(un("n0_pT1"), [128, 4, 128], F32) as pT1,
          nc.psum_tensor(un("n0_st"), [128, 128], F32) as stp):
        xts = [xt0, xt1]
        sqs = [sq0, sq1]
        pTs = [pT0, pT1]
        ld = [dma_ev(cx, "n0ld0"), dma_ev(cx, "n0ld1")]
        st = dma_ev(cx, "n0st")
        xt_free = [pre_tok, pre_tok]
        pT_free = [None, None]
        sq_free = [None, None]
        xTt_free = [pre_tok, None]
        stp_free = None
        ld_tok = [None, None]

        def issue_load(tt):
            b = tt % 2
            cx.wait(nc.sync, xt_free[b])
            ld_tok[b] = ld[b].mark(nc.sync.dma_start(out=xts[b][:], in_=x_dram[tt * 128:(tt + 1) * 128, :]))

        issue_load(0)
        u = 0
        for tt in range(ntt):
            b = tt % 2
            if tt + 1 < ntt:
                issue_load(tt + 1)
            cx.wait(nc.tensor, ld_tok[b])
            pending = None
            a_tok = None
            for cg in range(ncg):
                s = u % 2
                u += 1
                cx.wait(nc.tensor, pT_free[s])
                for c4 in range(4):
                    c = cg * 4 + c4
                    ins = nc.tensor.transpose(pTs[s][:, c4, :], xts[b][:, c * 128:(c + 1) * 128], C["ident_f"][:])
                p_tok = cx.pe.mark(ins)
                if pending is not None:
                    pa_tok, ps_, pcg = pending
                    cx.wait(nc.tensor, pa_tok)
                    if pcg == 0:
                        cx.wait(nc.tensor, stp_free)
                    for c4 in range(4):
                        ins = nc.tensor.matmul(stp[:, :], lhsT=C["ones_f"][:], rhs=sqs[ps_][:, c4, :],
                                               start=(pcg == 0 and c4 == 0), stop=False)
                    sq_free[ps_] = cx.pe.mark(ins)
                cx.wait(nc.scalar, p_tok)
                if cg == 0:
                    cx.wait(nc.scalar, xTt_free[0])
                    cx.wait(nc.scalar, xTt_free[1])
                nc.scalar.copy(out=xTt[:, cg * 4:(cg + 1) * 4, :], in_=pTs[s][:])
                cx.wait(nc.scalar, sq_free[s])
                a_tok = cx.act.mark(nc.scalar.activation(out=sqs[s][:], in_=pTs[s][:], func=AF.Square))
                pT_free[s] = a_tok
                pending = (a_tok, s, cg)
            xt_free[b] = p_tok
            pa_tok, ps_, pcg = pending
            cx.wait(nc.tensor, pa_tok)
            if pcg == 0:
                cx.wait(nc.tensor, stp_free)
            for c4 in range(4):
                ins = nc.tensor.matmul(stp[:, :], lhsT=C["ones_f"][:], rhs=sqs[ps_][:, c4, :],
                                       start=(pcg == 0 and c4 == 0), stop=(c4 == 3))
            s_tok = cx.pe.mark(ins)
            sq_free[ps_] = s_tok
            r_tok = rstd_from_stats(cx, rstd[:], stp[:, :], D, eps, s_tok)
            stp_free = r_tok
            cx.wait(nc.vector, a_tok)
            for c in range(DC):
                ins = nc.vector.scalar_tensor_tensor(out=act_out[:, c, tt * 128:(tt + 1) * 128], in0=xTt[:, c, :],
                                                     scalar=g_sb[:, c:c + 1], in1=rstd[:], op0=ALU.mult, op1=ALU.mult)
            d_tok = cx.dve.mark(ins)
            xTt_free[0] = d_tok
            if xT_dram is not None:
                cx.wait(nc.sync, a_tok)
                xTt_free[1] = st.mark(nc.sync.dma_start(
                    out=xT_dram.rearrange("(c p) t -> p c t", p=128)[:, :, tt * 128:(tt + 1) * 128], in_=xTt[:]))
        barrier(cx)


def ffn_up(cx, act, KC, T, Wg, Wu, FC, uT_dram, wbufs, wsems, wstate, pre_tok=None):
    nc = cx.nc
    with (nc.sbuf_tensor(un("fu_sg0"), [128, T], F32) as sg0, nc.sbuf_tensor(un("fu_sg1"), [128, T], F32) as sg1,
          nc.sbuf_tensor(un("fu_u0"), [128, T], BF16) as u0, nc.sbuf_tensor(un("fu_u1"), [128, T], BF16) as u1,
          nc.psum_tensor(un("fu_pg0"), [128, T], F32) as pg0, nc.psum_tensor(un("fu_pu0"), [128, T], F32) as pu0,
          nc.psum_tensor(un("fu_pg1"), [128, T], F32) as pg1, nc.psum_tensor(un("fu_pu1"), [128, T], F32) as pu1):
        sgs = [sg0, sg1]
        us = [u0, u1]
        st = [dma_ev(cx, "fust0"), dma_ev(cx, "fust1")]
        sg_free = [None, None]
        u_free = [None, None]

        def epi(j, pss, pe_tok):
            s = j % 2
            cx.wait(nc.scalar, pe_tok)
            cx.wait(nc.scalar, sg_free[s])
            a_tok = cx.act.mark(nc.scalar.activation(out=sgs[s][:], in_=pss[0][:], func=AF.Silu))
            cx.wait(nc.vector, a_tok)
            cx.wait(nc.vector, u_free[s])
            d_tok = cx.dve.mark(nc.vector.tensor_tensor(out=us[s][:], in0=sgs[s][:], in1=pss[1][:], op=ALU.mult))
            sg_free[s] = d_tok
            cx.wait(nc.sync, d_tok)
            u_free[s] = st[s].mark(nc.sync.dma_start(out=uT_dram[j * 128:(j + 1) * 128, :], in_=us[s][:]))
            return d_tok

        ngt = max(1, min(FC, (16384 // (2 * KC)) // 128))
        gemm(cx, act, KC, T, [Wg, Wu], FC, ngt, wbufs, wsems, [[pg0[:], pu0[:]], [pg1[:], pu1[:]]], epi,
             pre_tok=pre_tok, wstate=wstate)
        barrier(cx)


def gemm_raw_stats(cx, act, KC, T, t_off, W, DC, rawT_dram, st_ps, C, wbufs, wsems, wstate, ngt, pre_tok=None, tagn=""):
    nc = cx.nc
    nth = (T + 511) // 512
    with (nc.sbuf_tensor(un("gr_raw0") + tagn, [128, T], F32) as r0, nc.sbuf_tensor(un("gr_raw1") + tagn, [128, T], F32) as r1,
          nc.sbuf_tensor(un("gr_sq0") + tagn, [128, T], F32) as q0, nc.sbuf_tensor(un("gr_sq1") + tagn, [128, T], F32) as q1,
          nc.psum_tensor(un("gr_p0") + tagn, [128, T], F32) as p0, nc.psum_tensor(un("gr_p1") + tagn, [128, T], F32) as p1):
        rs = [r0, r1]
        qs = [q0, q1]
        st = [dma_ev(cx, "grst0"), dma_ev(cx, "grst1")]
        r_free = [None, None]
        q_free = [None, None]
        pend = []

        def stats_mm(j, s, a_tok):
            cx.wait(nc.tensor, a_tok)
            for th in range(nth):
                t0, t1 = th * 512, min(T, th * 512 + 512)
                ins = nc.tensor.matmul(st_ps[:, t_off + t0:t_off + t1], lhsT=C["ones_f"][:], rhs=qs[s][:, t0:t1],
                                       start=(j == 0), stop=(j == DC - 1))
            q_free[s] = cx.pe.mark(ins)

        def epi(j, pss, pe_tok):
            s = j % 2
            if pend:
                stats_mm(*pend.pop())
            cx.wait(nc.scalar, pe_tok)
            cx.wait(nc.scalar, r_free[s])
            a0 = cx.act.mark(nc.scalar.copy(out=rs[s][:], in_=pss[0][:]))
            cx.wait(nc.scalar, q_free[s])
            a_tok = cx.act.mark(nc.scalar.activation(out=qs[s][:], in_=pss[0][:], func=AF.Square))
            cx.wait(nc.sync, a0)
            r_free[s] = st[s].mark(nc.sync.dma_start(out=rawT_dram[j * 128:(j + 1) * 128, t_off:t_off + T], in_=rs[s][:]))
            pend.append((j, s, a_tok))
            return a_tok

        gemm(cx, act, KC, T, [W], DC, ngt, wbufs, wsems, [[p0[:]], [p1[:]]], epi, pre_tok=pre_tok, wstate=wstate)
        stats_mm(*pend.pop())
        barrier(cx)


def gemm_raw_acc(cx, act, KC, T, W, DC, rawT_dram, st_ps, C, wbufs, wsems, wstate, ngt, pre_tok=None, second=False):
    nc = cx.nc
    nth = (T + 511) // 512
    with A(nc) as al:
        rs = [al.sb("ga_r0", [128, T], F32), al.sb("ga_r1", [128, T], F32)]
        ps = [al.ps("ga_p0", [128, T]), al.ps("ga_p1", [128, T])]
        st = [dma_ev(cx, "gast0"), dma_ev(cx, "gast1")]
        r_free = [None, None]
        if second:
            qs = [al.sb("ga_q0", [128, T], F32), al.sb("ga_q1", [128, T], F32)]
            pb = [al.sb("ga_pb0", [128, T], F32), al.sb("ga_pb1", [128, T], F32)]
            ldp = [dma_ev(cx, "gald0"), dma_ev(cx, "gald1")]
            q_free = [None, None]
            pb_free = [None, None]
            pend = []

            def stats_mm(j, s, a_tok):
                cx.wait(nc.tensor, a_tok)
                for th in range(nth):
                    t0, t1 = th * 512, min(T, th * 512 + 512)
                    ins = nc.tensor.matmul(st_ps[:, t0:t1], lhsT=C["ones_f"][:], rhs=qs[s][:, t0:t1], start=(j == 0), stop=(j == DC - 1))
                q_free[s] = cx.pe.mark(ins)

        def epi(j, pss, pe_tok):
            s = j % 2
            if not second:
                cx.wait(nc.scalar, pe_tok)
                cx.wait(nc.scalar, r_free[s])
                a0 = cx.act.mark(nc.scalar.copy(out=rs[s][:], in_=pss[0][:]))
                cx.wait(nc.sync, a0)
                r_free[s] = st[s].mark(nc.sync.dma_start(out=rawT_dram[j * 128:(j + 1) * 128, :], in_=rs[s][:]))
                return a0
            if pend:
                stats_mm(*pend.pop())
            cx.wait(nc.sync, pb_free[s])
            l_tok = ldp[s].mark(nc.sync.dma_start(out=pb[s][:], in_=rawT_dram[j * 128:(j + 1) * 128, :]))
            cx.wait(nc.vector, pe_tok)
            cx.wait(nc.vector, l_tok)
            cx.wait(nc.vector, r_free[s])
            d0 = cx.dve.mark(nc.vector.tensor_tensor(out=rs[s][:], in0=pss[0][:], in1=pb[s][:], op=ALU.add))
            pb_free[s] = d0
            cx.wait(nc.scalar, d0)
            cx.wait(nc.scalar, q_free[s])
            a_tok = cx.act.mark(nc.scalar.activation(out=qs[s][:], in_=rs[s][:], func=AF.Square))
            cx.wait(nc.sync, a_tok)
            r_free[s] = st[s].mark(nc.sync.dma_start(out=rawT_dram[j * 128:(j + 1) * 128, :], in_=rs[s][:]))
            pend.append((j, s, a_tok))
            return d0

        gemm(cx, act, KC, T, [W], DC, ngt, wbufs, wsems, [[ps[0][:]], [ps[1][:]]], epi, pre_tok=pre_tok, wstate=wstate)
        if second:
            stats_mm(*pend.pop())
        barrier(cx)


def ffn_down(cx, actbuf, FC, T, Wd, DC, uT_dram, rawT_dram, st_ps, C, wbufs, wsems, wstate):
    nc = cx.nc
    ld = dma_ev(cx, "fdld")
    KH = (FC + 1) // 2
    k0 = 0
    for ph in range(2):
        kc = min(KH, FC - k0)
        act = actbuf[:, 0:kc * T].rearrange("p (k t) -> p k t", k=kc)
        tok = ld.mark(nc.sync.dma_start(out=act, in_=uT_dram[k0 * 128:(k0 + kc) * 128, :].rearrange("(k p) t -> p k t", p=128)))
        ngt = max(1, min(DC, (16384 // kc) // 128))
        gemm_raw_acc(cx, act, kc, T, Wd[k0 * 128:(k0 + kc) * 128, :], DC, rawT_dram, st_ps, C, wbufs, wsems, wstate, ngt,
                     pre_tok=tok, second=(ph == 1))
        k0 += kc


def residual_pass(cx, rawT_dram, st_ps, xT_dram, T, DC, gpost_sb, gpre_sb, act_out, C, eps, st2_ps, out_dram=None):
    nc = cx.nc
    D = DC * 128
    nth = (T + 511) // 512
    final = out_dram is not None
    NBF = 4
    with A(nc) as al:
        rbs = [al.sb(f"rp_r{i}", [128, T], F32) for i in range(NBF)]
        xbs = [al.sb(f"rp_x{i}", [128, T], F32) for i in range(NBF)]
        qbs = [al.sb(f"rp_q{i}", [128, T], F32) for i in range(2)]
        rsa, rsb = al.sb("rp_rsa", [128, T], F32), al.sb("rp_rsb", [128, T], F32)
        pTs = [al.ps("rp_pT0", [128, 4, 128]), al.ps("rp_pT1", [128, 4, 128])]
        ld = [dma_ev(cx, f"rpld{i}") for i in range(NBF)]
        st = [dma_ev(cx, f"rpst{i}") for i in range(NBF)]
        ra_tok = rstd_from_stats(cx, rsa[:], st_ps[:, 0:T], D, eps, None)
        r_free = [None] * NBF
        x_free = [[None, None] for _ in range(NBF)]
        q_free = [None, None]
        pT_free = [None, None]
        xT_v = xT_dram.rearrange("(c p) t -> c p t", p=128)
        raw_v = rawT_dram.rearrange("(c p) t -> c p t", p=128)
        u = 0
        for c in range(DC):
            b = c % NBF
            qi = c % 2
            cx.wait(nc.sync, r_free[b])
            cx.wait(nc.sync, x_free[b][0])
            cx.wait(nc.sync, x_free[b][1])
            ld[b].mark(nc.sync.dma_start(out=rbs[b][:], in_=raw_v[c]))
            ltok = ld[b].mark(nc.sync.dma_start(out=xbs[b][:], in_=xT_v[c]))
            cx.wait(nc.vector, ltok)
            t1 = cx.dve.mark(nc.vector.scalar_tensor_tensor(out=rbs[b][:], in0=rbs[b][:], scalar=gpost_sb[:, c:c + 1], in1=rsa[:],
                                                            op0=ALU.mult, op1=ALU.mult))
            cx.wait(nc.vector, t1)
            d_tok = cx.dve.mark(nc.vector.tensor_tensor(out=xbs[b][:], in0=rbs[b][:], in1=xbs[b][:], op=ALU.add))
            r_free[b] = d_tok
            if not final:
                cx.wait(nc.sync, d_tok)
                x_free[b][0] = st[b].mark(nc.sync.dma_start(out=xT_v[c], in_=xbs[b][:]))
                cx.wait(nc.scalar, d_tok)
                cx.wait(nc.scalar, q_free[qi])
                a_tok = cx.act.mark(nc.scalar.activation(out=qbs[qi][:], in_=xbs[b][:], func=AF.Square))
                x_free[b][1] = a_tok
                cx.wait(nc.tensor, a_tok)
                for th in range(nth):
                    t0, t1_ = th * 512, min(T, th * 512 + 512)
                    ins = nc.tensor.matmul(st2_ps[:, t0:t1_], lhsT=C["ones_f"][:], rhs=qbs[qi][:, t0:t1_],
                                           start=(c == 0), stop=(c == DC - 1))
                q_free[qi] = cx.pe.mark(ins)
            else:
                cx.wait(nc.tensor, d_tok)
                ov = out_dram.rearrange("(tt p) d -> p tt d", p=128)
                for t4 in range(T // 512):
                    s = u % 2
                    u += 1
                    cx.wait(nc.tensor, pT_free[s])
                    for i4 in range(4):
                        tt = t4 * 4 + i4
                        ins = nc.tensor.transpose(pTs[s][:, i4, :], xbs[b][:, tt * 128:(tt + 1) * 128], C["ident_f"][:])
                    p_tok = cx.pe.mark(ins)
                    x_free[b][1] = p_tok
                    cx.wait(nc.scalar, p_tok)
                    qs = qbs[s][:, 0:512].rearrange("p (a d) -> p a d", a=4)
                    cx.wait(nc.scalar, q_free[s])
                    a_tok = cx.act.mark(nc.scalar.copy(out=qs, in_=pTs[s][:]))
                    pT_free[s] = a_tok
                    cx.wait(nc.sync, a_tok)
                    q_free[s] = st[s].mark(nc.sync.dma_start(out=ov[:, t4 * 4:(t4 + 1) * 4, c * 128:(c + 1) * 128], in_=qs))
        if final:
            barrier(cx)
            return
        barrier(cx)
        rb_tok = rstd_from_stats(cx, rsb[:], st2_ps[:, 0:T], D, eps, None)
        x_free = [None] * NBF
        for c in range(DC):
            b = c % NBF
            cx.wait(nc.sync, x_free[b])
            ltok = ld[b].mark(nc.sync.dma_start(out=xbs[b][:], in_=xT_v[c]))
            cx.wait(nc.vector, ltok)
            x_free[b] = cx.dve.mark(nc.vector.scalar_tensor_tensor(out=act_out[:, c, :], in0=xbs[b][:], scalar=gpre_sb[:, c:c + 1],
                                                                   in1=rsb[:], op0=ALU.mult, op1=ALU.mult))
        barrier(cx)


def xattn_kv_phase(cx, DC, NM, mem_dram, gmem_sb, wk, wv, kT_d, vtok_d, C, eps, wbufs, wsems, wstate):
    nc = cx.nc
    D = DC * 128
    NMC = NM // 128
    with A(nc) as al:
        mT, kT, vtok = al.sb("xa_mT", [128, DC, NM], BF16), al.sb("xa_kT", [128, DC, NM], BF16), al.sb("xa_vtok", [128, NMC, D], BF16)
        norm_in_tokmajor(cx, mem_dram, NM, DC, gmem_sb, mT[:], None, C, eps)
        with A(nc) as a2:
            vTs = [a2.sb("xa_vT0", [128, NM], BF16), a2.sb("xa_vT1", [128, NM], BF16)]
            pk0, pv0 = a2.ps("xa_pk0", [128, NM]), a2.ps("xa_pv0", [128, NM])
            pk1, pv1 = a2.ps("xa_pk1", [128, NM]), a2.ps("xa_pv1", [128, NM])
            pts = [a2.ps("xa_pt0", [128, NMC, 128], BF16), a2.ps("xa_pt1", [128, NMC, 128], BF16)]
            vT_free = [None, None]
            pt_free = [None, None]
            pend = []

            def do_tr(j, s, a_tok):
                cx.wait(nc.tensor, a_tok)
                cx.wait(nc.tensor, pt_free[s])
                for mc in range(NMC):
                    ins = nc.tensor.transpose(pts[s][:, mc, :], vTs[s][:, mc * 128:(mc + 1) * 128], C["ident_b"][:])
                p_tok = cx.pe.mark(ins)
                vT_free[s] = p_tok
                cx.wait(nc.vector, p_tok)
                pt_free[s] = cx.dve.mark(nc.vector.tensor_copy(out=vtok[:, :, j * 128:(j + 1) * 128], in_=pts[s][:]))

            def epi_kv(j, pss, pe_tok):
                s = j % 2
                if pend:
                    do_tr(*pend.pop())
                cx.wait(nc.scalar, pe_tok)
                nc.scalar.copy(out=kT[:, j, :], in_=pss[0][:])
                cx.wait(nc.scalar, vT_free[s])
                a_tok = cx.act.mark(nc.scalar.copy(out=vTs[s][:], in_=pss[1][:]))
                pend.append((j, s, a_tok))
                return a_tok

            ngt = max(1, min(DC, (16384 // (2 * DC)) // 128))
            gemm(cx, mT[:], DC, NM, [wk, wv], DC, ngt, wbufs, wsems, [[pk0[:], pv0[:]], [pk1[:], pv1[:]]], epi_kv, wstate=wstate)
            do_tr(*pend.pop())
            barrier(cx)
        st = dma_ev(cx, "xakvst")
        st.mark(nc.sync.dma_start(out=kT_d, in_=kT[:]))
        st.mark(nc.sync.dma_start(out=vtok_d, in_=vtok[:]))
        barrier(cx)


def xattn_phase(cx, actbuf, T, DC, H, NM, kT_d, vtok_d, wq, qT_dram, C, wbufs, wsems, wstate):
    nc = cx.nc
    D = DC * 128
    HD = D // H
    HDC = HD // 128
    NMC = NM // 128
    act = actbuf[:, 0:DC * T].rearrange("p (k t) -> p k t", k=DC)
    scale = float(HD) ** -0.5
    with A(nc) as al:
        kT, vtok = al.sb("xa_kT", [128, DC, NM], BF16), al.sb("xa_vtok", [128, NMC, D], BF16)
        ldkv = dma_ev(cx, "xakvld")
        ldkv.mark(nc.sync.dma_start(out=kT[:], in_=kT_d))
        ldkv.mark(nc.sync.dma_start(out=vtok[:], in_=vtok_d))
        with (nc.sbuf_tensor(un("xa_q0"), [128, T], BF16) as q0, nc.sbuf_tensor(un("xa_q1"), [128, T], BF16) as q1,
              nc.psum_tensor(un("xa_pq0"), [128, T], F32) as pq0, nc.psum_tensor(un("xa_pq1"), [128, T], F32) as pq1):
            qs = [q0, q1]
            st = [dma_ev(cx, "xaq0"), dma_ev(cx, "xaq1")]
            q_free = [None, None]

            def epi_q(j, pss, pe_tok):
                s = j % 2
                cx.wait(nc.scalar, pe_tok)
                cx.wait(nc.scalar, q_free[s])
                a_tok = cx.act.mark(nc.scalar.copy(out=qs[s][:], in_=pss[0][:]))
                cx.wait(nc.sync, a_tok)
                q_free[s] = st[s].mark(nc.sync.dma_start(out=qT_dram[j * 128:(j + 1) * 128, :], in_=qs[s][:]))
                return a_tok

            ngt = max(1, min(DC, (16384 // DC) // 128))
            gemm(cx, act, DC, T, [wq], DC, ngt, wbufs, wsems, [[pq0[:]], [pq1[:]]], epi_q, wstate=wstate)
            barrier(cx)
        NTH = T // 512
        with (nc.sbuf_tensor(un("xa_qh0"), [128, HDC, T], BF16) as qh0,
              nc.sbuf_tensor(un("xa_pT0"), [128, NMC, 512], BF16) as pT0, nc.sbuf_tensor(un("xa_pT1"), [128, NMC, 512], BF16) as pT1,
              nc.sbuf_tensor(un("xa_rd0"), [128, 512], F32) as rd0,
              nc.psum_tensor(un("xa_ps0"), [128, NMC, 512], F32) as ps0, nc.psum_tensor(un("xa_ps1"), [128, NMC, 512], F32) as ps1,
              nc.psum_tensor(un("xa_pd0"), [128, 512], F32) as pd0, nc.psum_tensor(un("xa_pd1"), [128, 512], F32) as pd1,
              nc.psum_tensor(un("xa_po0"), [128, 512], F32) as po0, nc.psum_tensor(un("xa_po1"), [128, 512], F32) as po1):
            qhs, pTs, rds, pss_, pds, pos = [qh0, qh0], [pT0, pT1], [rd0, rd0], [ps0, ps1], [pd0, pd1], [po0, po1]
            ld = [dma_ev(cx, "xaqh0"), dma_ev(cx, "xaqh1")]
            qh_free = [None, None]
            ps_free = [None, None]
            pT_free = [None, None]
            pd_free = [None, None]
            rd_free = [None, None]
            po_free = [None, None]
            it = 0
            oi = 0
            for h in range(H):
                hb = 0
                cx.wait(nc.sync, qh_free[hb])
                l_tok = ld[hb].mark(nc.sync.dma_start(out=qhs[hb][:], in_=qT_dram.rearrange("(c p) t -> p c t", p=128)[:, h * HDC:(h + 1) * HDC, :]))
                cx.wait(nc.tensor, l_tok)
                for th in range(NTH):
                    s = it % 2
                    it += 1
                    tsl = slice(th * 512, (th + 1) * 512)
                    cx.wait(nc.tensor, ps_free[s])
                    for mc in range(NMC):
                        for dc in range(HDC):
                            ins = nc.tensor.matmul(pss_[s][:, mc, :], lhsT=kT[:, h * HDC + dc, mc * 128:(mc + 1) * 128],
                                                   rhs=qhs[hb][:, dc, tsl], start=(dc == 0), stop=(dc == HDC - 1))
                    s_tok = cx.pe.mark(ins)
                    cx.wait(nc.scalar, s_tok)
                    cx.wait(nc.scalar, pT_free[s])
                    e_tok = cx.act.mark(nc.scalar.activation(out=pTs[s][:], in_=pss_[s][:], func=AF.Exp, scale=scale))
                    ps_free[s] = e_tok
                    cx.wait(nc.tensor, e_tok)
                    cx.wait(nc.tensor, pd_free[s])
                    for mc in range(NMC):
                        ins = nc.tensor.matmul(pds[s][:], lhsT=C["ones_b"][:], rhs=pTs[s][:, mc, :], start=(mc == 0), stop=(mc == NMC - 1))
                    d_tok = cx.pe.mark(ins)
                    cx.wait(nc.vector, d_tok)
                    cx.wait(nc.vector, rd_free[s])
                    r_tok = cx.dve.mark(nc.vector.reciprocal(out=rds[s][:], in_=pds[s][:]))
                    pd_free[s] = r_tok
                    cx.wait(nc.vector, r_tok)
                    for dc in range(HDC):
                        o = oi % 2
                        oi += 1
                        cx.wait(nc.tensor, po_free[o])
                        for mc in range(NMC):
                            ins = nc.tensor.matmul(pos[o][:], lhsT=vtok[:, mc, h * HD + dc * 128:h * HD + (dc + 1) * 128],
                                                   rhs=pTs[s][:, mc, :], start=(mc == 0), stop=(mc == NMC - 1))
                        o_tok = cx.pe.mark(ins)
                        cx.wait(nc.vector, o_tok)
                        po_free[o] = cx.dve.mark(nc.vector.tensor_tensor(out=act[:, h * HDC + dc, tsl], in0=pos[o][:], in1=rds[s][:], op=ALU.mult))
                    pT_free[s] = o_tok
                    rd_free[s] = po_free[o]
                qh_free[hb] = s_tok
            barrier(cx)


def proj_raw_stats(cx, actbuf, T, DC, wo, rawT_dram, st_ps, C, wbufs, wsems, wstate):
    act = actbuf[:, 0:DC * T].rearrange("p (k t) -> p k t", k=DC)
    ngt = max(1, min(DC, (16384 // DC) // 128))
    gemm_raw_stats(cx, act, DC, T, 0, wo, DC, rawT_dram, st_ps, C, wbufs, wsems, wstate, ngt)


def inproj_phase(cx, actbuf, hT_dram, S, DC, w_in, NTF, NTR, projF, projR, wbufs, wsems, wstate, halves=None):
    nc = cx.nc
    TT = min(S, 1024)
    ld = dma_ev(cx, "ipld")
    with (nc.sbuf_tensor(un("ip_b0"), [128, TT], BF16) as b0, nc.sbuf_tensor(un("ip_b1"), [128, TT], BF16) as b1,
          nc.sbuf_tensor(un("ip_f0"), [128, TT], F32) as f0, nc.sbuf_tensor(un("ip_f1"), [128, TT], F32) as f1,
          nc.psum_tensor(un("ip_p0"), [128, TT], F32) as p0, nc.psum_tensor(un("ip_p1"), [128, TT], F32) as p1):
        bs, fs = [b0, b1], [f0, f1]
        st = [dma_ev(cx, "ipst0"), dma_ev(cx, "ipst1")]
        free = [None, None]
        for th in range(S // TT):
            act = actbuf[:, 0:DC * TT].rearrange("p (k t) -> p k t", k=DC)
            if halves is not None:
                k0 = 0
                for piece in halves[th]:
                    nk = piece.shape[0] // 128
                    tok = ld.mark(nc.sync.dma_start(out=act[:, k0:k0 + nk, :], in_=piece.rearrange("(k p) t -> p k t", p=128)))
                    k0 += nk
            else:
                src = hT_dram.rearrange("(k p) t -> p k t", p=128)[:, :, th * TT:(th + 1) * TT]
                tok = ld.mark(nc.sync.dma_start(out=act, in_=src))

            def epi(j, pss, pe_tok, th=th):
                s = j % 2
                cx.wait(nc.scalar, pe_tok)
                cx.wait(nc.scalar, free[s])
                if j < NTF:
                    a_tok = cx.act.mark(nc.scalar.copy(out=bs[s][:], in_=pss[0][:]))
                    cx.wait(nc.sync, a_tok)
                    free[s] = st[s].mark(nc.sync.dma_start(out=projF[j * 128:(j + 1) * 128, th * TT:(th + 1) * TT], in_=bs[s][:]))
                else:
                    a_tok = cx.act.mark(nc.scalar.copy(out=fs[s][:], in_=pss[0][:]))
                    cx.wait(nc.sync, a_tok)
                    jj = j - NTF
                    free[s] = st[s].mark(nc.sync.dma_start(out=projR[jj * 128:(jj + 1) * 128, th * TT:(th + 1) * TT], in_=fs[s][:]))
                return a_tok

            ngt = max(1, min(NTF + NTR, (16384 // DC) // 128))
            gemm(cx, act, DC, TT, [w_in], NTF + NTR, ngt, wbufs, wsems, [[p0[:]], [p1[:]]], epi, pre_tok=tok, wstate=wstate)
            barrier(cx)


def fox_phase(cx, S, NFH, projF, fT_dram, fbias_sb, yT_dram, C):
    nc = cx.nc
    FW = NFH * 128
    NB = S // 128
    NP = S // 512
    scale = 128.0 ** -0.5
    NEG = -1.0e30
    with A(nc) as al0:
        cT, onesS = al0.sb("fx_c", [128, S], F32), al0.sb("fx_ones", [128, S], F32)
        r1 = al0.sb("fx_r1", [128, S], F32)
        selA, selB = al0.sb("fx_selA", [128, 8, 128], F32), al0.sb("fx_selB", [128, 8, 128], F32)
        sel3 = al0.sb("fx_sel3", [128, 8, 128], BF16)
        C3 = al0.sb("fx_C3", [128, S], BF16)
        lo_t = al0.sb("fx_lo", [128, S], BF16)
        ncs = al0.sb("fx_ncs", [128, NB, 8], F32)
        mask = al0.sb("fx_mask", [128, 4, 512], F32)
        nbias = al0.sb("fx_nb", [128, 1], F32)
        ld0 = dma_ev(cx, "fxld")
        mv = dma_ev(cx, "fxmv")
        t0 = ld0.mark(nc.sync.dma_start(out=cT[:], in_=fT_dram))
        d0 = cx.dve.mark(nc.vector.memset(onesS[:], 1.0))
        d0 = cx.dve.mark(nc.vector.tensor_scalar(out=nbias[:], in0=fbias_sb, scalar1=-1.0, scalar2=None, op0=ALU.mult))
        cx.wait(nc.gpsimd, d0)
        ones3 = onesS[:, 0:1024].rearrange("p (h m) -> p h m", h=8)
        nc.gpsimd.affine_select(out=selA[:], in_=ones3, pattern=[[-1, 8], [0, 128]], compare_op=ALU.is_equal, fill=0.0, base=0, channel_multiplier=1)
        g1 = cx.pool.mark(nc.gpsimd.affine_select(out=selB[:], in_=ones3, pattern=[[-1, 8], [0, 128]], compare_op=ALU.is_equal, fill=0.0, base=-8,
                                                  channel_multiplier=1))
        cx.wait(nc.vector, g1)
        d1 = cx.dve.mark(nc.vector.tensor_tensor(out=selA[:], in0=selA[:], in1=selB[:], op=ALU.add))
        cx.wait(nc.gpsimd, d1)
        g2 = cx.pool.mark(nc.gpsimd.affine_select(out=selB[:], in_=ones3, pattern=[[-1, 8], [0, 128]], compare_op=ALU.is_equal, fill=0.0, base=-16,
                                                  channel_multiplier=1))
        cx.wait(nc.vector, g2)
        d1 = cx.dve.mark(nc.vector.tensor_tensor(out=sel3[:], in0=selA[:], in1=selB[:], op=ALU.add))
        pm = cx.pool.mark(nc.gpsimd.memset(mask[:], 0.0))
        cx.wait(nc.gpsimd, pm)
        p_tok = cx.pool.mark(nc.gpsimd.affine_select(out=mask[:], in_=mask[:], pattern=[[-128, 4], [1, 512]], compare_op=ALU.is_ge, fill=NEG,
                                                     base=0, channel_multiplier=-1))
        cx.wait(nc.scalar, t0)
        cx.wait(nc.scalar, d0)
        a_tok = cx.act.mark(nc.scalar.activation(out=cT[:], in_=cT[:], func=AF.Exp, scale=-1.0, bias=nbias[:]))
        cx.wait(nc.vector, a_tok)
        d_tok = cx.dve.mark(nc.vector.tensor_scalar(out=cT[:], in0=cT[:], scalar1=1.0, scalar2=None, op0=ALU.add))
        cx.wait(nc.scalar, d_tok)
        a_tok = cx.act.mark(nc.scalar.activation(out=cT[:], in_=cT[:], func=AF.Ln))
        cx.wait(nc.vector, a_tok)
        d_tok = cx.dve.mark(nc.vector.tensor_scalar(out=cT[:], in0=cT[:], scalar1=-1.0, scalar2=None, op0=ALU.mult))
        cx.wait(nc.vector, d_tok)
        d_tok = cx.dve.mark(nc.vector.tensor_tensor_scan(out=cT[:], data0=onesS[:], data1=cT[:], initial=0.0, op0=ALU.mult, op1=ALU.add))
        cx.wait(nc.tensor, d_tok)
        cx.wait(nc.tensor, p_tok)
        with A(nc) as alp:
            ptc = alp.ps("fx_pt", [128, NB, 8])
            for blk in range(NB):
                ins = nc.tensor.transpose(ptc[:, blk, :], cT[0:8, blk * 128:(blk + 1) * 128], C["ident_f"][0:8, 0:8])
            t_tok = cx.pe.mark(ins)
            cx.wait(nc.vector, t_tok)
            d_tok = cx.dve.mark(nc.vector.tensor_scalar(out=ncs[:], in0=ptc[:], scalar1=-1.0, scalar2=None, op0=ALU.mult))
            sv = Ser(cx, d_tok)
            V_ = nc.vector
            sv.run(V_, lambda: V_.tensor_scalar(out=r1[:], in0=cT[:], scalar1=1.0 / scale, scalar2=None, op0=ALU.mult))
            sv.run(V_, lambda: V_.tensor_copy(out=C3[:], in_=r1[:]))
            sv.run(V_, lambda: V_.tensor_tensor(out=r1[:], in0=r1[:], in1=C3[:], op=ALU.subtract))
            sv.run(V_, lambda: V_.tensor_copy(out=lo_t[:], in_=r1[:]))
            cx.wait(nc.sync, sv.last)
            m1 = mv.mark(nc.sync.dma_start(out=C3[8:16, :], in_=lo_t[0:8, :]))
            sv.run(V_, lambda: V_.tensor_tensor(out=r1[:], in0=r1[:], in1=lo_t[:], op=ALU.subtract))
            sv.run(V_, lambda: V_.tensor_copy(out=lo_t[:], in_=r1[:]), extra=[m1])
            cx.wait(nc.sync, sv.last)
            mv.mark(nc.sync.dma_start(out=C3[16:24, :], in_=lo_t[0:8, :]))
            barrier(cx)
        with A(nc) as al:
            qh, kh, vh = al.sb("fx_q", [128, S], BF16), al.sb("fx_k", [128, S], BF16), al.sb("fx_v", [128, S], BF16)
            vtok = al.sb("fx_vt", [128, NB, 128], BF16)
            tm0, tm1 = al.sb("fx_t0", [128, 512], F32), al.sb("fx_t1", [128, 512], F32)
            pp0, pp1 = al.sb("fx_p0", [128, 512], BF16), al.sb("fx_p1", [128, 512], BF16)
            rd = al.sb("fx_rd", [128, 512], F32)
            y0, y1 = al.sb("fx_y0", [128, 512], BF16), al.sb("fx_y1", [128, 512], BF16)
            ps0, ps1 = al.ps("fx_ps0", [128, 512]), al.ps("fx_ps1", [128, 512])
            pd, po = al.ps("fx_pd", [128, 512]), al.ps("fx_po", [128, 512])
            pv = al.ps("fx_pv", [128, 4, 128], BF16)
            tms, pps, pss_, ys = [tm0, tm1], [pp0, pp1], [ps0, ps1], [y0, y1]
            ldh = dma_ev(cx, "fxldh")
            sty = [dma_ev(cx, "fxsty0"), dma_ev(cx, "fxsty1")]
            it = 0
            yi = 0
            ps_free = [None, None]
            tm_free = [None, None]
            pp_free = [None, None]
            y_free = [None, None]
            acc_free = None
            rd_free = None
            for h in range(NFH):
                barrier(cx)
                ldh.mark(nc.sync.dma_start(out=qh[:], in_=projF[h * 128:(h + 1) * 128, :]))
                ldh.mark(nc.sync.dma_start(out=kh[:], in_=projF[FW + h * 128:FW + (h + 1) * 128, :]))
                l_tok = ldh.mark(nc.sync.dma_start(out=vh[:], in_=projF[2 * FW + h * 128:2 * FW + (h + 1) * 128, :]))
                cx.wait(nc.tensor, l_tok)
                for b4 in range(NB // 4):
                    for i4 in range(4):
                        blk = b4 * 4 + i4
                        ins = nc.tensor.transpose(pv[:, i4, :], vh[:, blk * 128:(blk + 1) * 128], C["ident_b"][:])
                    v_tok = cx.pe.mark(ins)
                    cx.wait(nc.scalar, v_tok)
                    a_tok = cx.act.mark(nc.scalar.copy(out=vtok[:, b4 * 4:(b4 + 1) * 4, :], in_=pv[:]))
                    cx.wait(nc.tensor, a_tok)
                for qp in range(NP):
                    qsl = slice(qp * 512, (qp + 1) * 512)
                    nkb = qp * 4 + 4
                    cx.wait(nc.tensor, acc_free)
                    pend = None
                    for kb in range(nkb):
                        s = it % 2
                        it += 1
                        cx.wait(nc.tensor, ps_free[s])
                        nc.tensor.matmul(pss_[s][:], lhsT=kh[:, kb * 128:(kb + 1) * 128], rhs=qh[:, qsl], start=True, stop=False)
                        s_tok = cx.pe.mark(nc.tensor.matmul(pss_[s][:], lhsT=sel3[0:24, h, :], rhs=C3[0:24, qsl], start=False, stop=True))
                        if pend is not None:
                            pe_, ps_, pkb = pend
                            cx.wait(nc.tensor, pe_)
                            nc.tensor.matmul(pd[:], lhsT=C["ones_b"][:], rhs=pps[ps_][:], start=(pkb == 0), stop=False)
                            pp_free[ps_] = cx.pe.mark(nc.tensor.matmul(po[:], lhsT=vtok[:, pkb, :], rhs=pps[ps_][:], start=(pkb == 0), stop=False))
                        off = kb - qp * 4
                        if off >= 0:
                            cx.wait(nc.vector, s_tok)
                            cx.wait(nc.vector, tm_free[s])
                            d_tok = cx.dve.mark(nc.vector.tensor_tensor(out=tms[s][:], in0=pss_[s][:], in1=mask[:, off, :], op=ALU.add))
                            ps_free[s] = d_tok
                            cx.wait(nc.scalar, d_tok)
                            cx.wait(nc.scalar, pp_free[s])
                            e_tok = cx.act.mark(nc.scalar.activation(out=pps[s][:], in_=tms[s][:], func=AF.Exp, bias=ncs[:, kb, h:h + 1], scale=scale))
                            tm_free[s] = e_tok
                        else:
                            cx.wait(nc.scalar, s_tok)
                            cx.wait(nc.scalar, pp_free[s])
                            e_tok = cx.act.mark(nc.scalar.activation(out=pps[s][:], in_=pss_[s][:], func=AF.Exp, bias=ncs[:, kb, h:h + 1], scale=scale))
                            ps_free[s] = e_tok
                        pend = (e_tok, s, kb)
                    pe_, ps_, pkb = pend
                    cx.wait(nc.tensor, pe_)
                    nc.tensor.matmul(pd[:], lhsT=C["ones_b"][:], rhs=pps[ps_][:], start=(pkb == 0), stop=True)
                    o_tok = cx.pe.mark(nc.tensor.matmul(po[:], lhsT=vtok[:, pkb, :], rhs=pps[ps_][:], start=(pkb == 0), stop=True))
                    pp_free[ps_] = o_tok
                    cx.wait(nc.vector, o_tok)
                    cx.wait(nc.vector, rd_free)
                    r_tok = cx.dve.mark(nc.vector.reciprocal(out=rd[:], in_=pd[:]))
                    cx.wait(nc.vector, r_tok)
                    ys_ = yi % 2
                    yi += 1
                    cx.wait(nc.vector, y_free[ys_])
                    y_tok = cx.dve.mark(nc.vector.tensor_tensor(out=ys[ys_][:], in0=po[:], in1=rd[:], op=ALU.mult))
                    acc_free = y_tok
                    rd_free = y_tok
                    cx.wait(nc.sync, y_tok)
                    y_free[ys_] = sty[ys_].mark(nc.sync.dma_start(out=yT_dram[h * 128:(h + 1) * 128, qsl], in_=ys[ys_][:]))
            barrier(cx)


class Ser:
    def __init__(self, cx, tok=None):
        self.cx = cx
        self.last = tok

    def run(self, eng, fn, extra=()):
        cx = self.cx
        cx.wait(eng, self.last)
        for t in extra:
            cx.wait(eng, t)
        self.last = cx.evof(eng).mark(fn())
        return self.last


class Dep:
    def __init__(self, cx):
        self.cx = cx
        self.w = {}
        self.r = {}

    def _pre(self, eng, reads, writes, extra):
        cx = self.cx
        for k in reads:
            cx.wait(eng, self.w.get(k))
        for k in writes:
            cx.wait(eng, self.w.get(k))
            for t in self.r.get(k, ()):
                cx.wait(eng, t)
        for t in extra:
            cx.wait(eng, t)

    def _post(self, tok, reads, writes):
        for k in reads:
            self.r.setdefault(k, []).append(tok)
        for k in writes:
            self.w[k] = tok
            self.r[k] = []
        return tok

    def run(self, eng, fn, reads=(), writes=(), extra=()):
        self._pre(eng, reads, writes, extra)
        return self._post(self.cx.evof(eng).mark(fn()), reads, writes)

    def dma(self, eng, ev, fn, reads=(), writes=(), extra=()):
        self._pre(eng, reads, writes, extra)
        return self._post(ev.mark(fn()), reads, writes)

    def all_done(self, engines):
        cx = self.cx
        for e in engines:
            for ev in (cx.pe, cx.act, cx.dve, cx.pool):
                cx.wait(e, (ev, ev.n))


NVT = 10


def rwkv_phase(cx, S, NRT, projR, rvec_dram, wup_dram, aup_dram, gup_dram, yT_dram, y_row0, C, gn_eps, after_round=None):
    nc = cx.nc
    RW = NRT * 128
    NCH = S // 64
    NP = S // 512
    V, ACT_, PE, POOL = nc.vector, nc.scalar, nc.tensor, nc.gpsimd
    with A(nc) as al:
        rvec = al.sb("rk_vec", [128, NRT * NVT + 4], F32)
        omka = al.sb("rk_omka", [128, NRT], F32)
        wup, aup = al.sb("rk_wup", [128, RW], BF16), al.sb("rk_aup", [128, RW], BF16)
        gup = al.sb("rk_gup", [128, 2, RW], BF16)
        lxw, lxa, lxg = al.sb("rk_lxw", [128, S], BF16), al.sb("rk_lxa", [128, S], BF16), al.sb("rk_lxg", [128, 2, S], BF16)
        seg = al.sb("rk_seg", [128, S], BF16)
        bones = al.sb("rk_bones", [128, 128], F32)
        idrep = al.sb("rk_idrep", [128, 8, 64], F32)
        m1, m2 = al.sb("rk_m1", [128, 4, 128], F32), al.sb("rk_m2", [128, 4, 128], F32)
        m3 = al.sb("rk_m3", [128, 4, 64], F32)
        Fb = [al.sb(f"rk_F{i}", [128, S], F32) for i in range(8)]
        VB, KT, BT = al.sb("rk_VB", [128, S], BF16), al.sb("rk_KT", [128, S], BF16), al.sb("rk_BT", [128, S], BF16)
        KH, NBH = al.sb("rk_KH", [128, S], BF16), al.sb("rk_NBH", [128, S], BF16)
        KR = al.sb("rk_KR", [128, NCH, 2, 64], BF16)
        Vtok, Khat, nBhat = al.sb("rk_Vt", [128, NCH, 64], BF16), al.sb("rk_Kh", [128, NCH, 64], BF16), al.sb("rk_nBh", [128, NCH, 64], BF16)
        NTB, AB2 = al.sb("rk_NTB", [128, NCH, 128], BF16), al.sb("rk_AB2", [128, NCH, 128], BF16)
        Nb = [al.sb("rk_Na", [128, NCH, 64], BF16), al.sb("rk_Nb", [128, NCH, 64], BF16)]
        NTb = [al.sb("rk_NTa", [128, NCH, 64], BF16), al.sb("rk_NTb", [128, NCH, 64], BF16)]
        TTb = [al.sb("rk_TTa", [128, NCH, 64], BF16), al.sb("rk_TTb", [128, NCH, 64], BF16)]
        PC, nPC = al.sb("rk_PC", [128, NCH], F32), al.sb("rk_nPC", [128, NCH], F32)
        M32, Mb = al.sb("rk_M32", [128, 64], F32), al.sb("rk_Mb", [128, 64], BF16)
        Xs, Us = al.sb("rk_Xs", [128, 64], BF16), al.sb("rk_Us", [128, 64], BF16)
        ysb = al.sb("rk_ysb", [128, S], BF16)
        ld = dma_ev(cx, "rkld")
        ld3 = [dma_ev(cx, "rkld3a"), dma_ev(cx, "rkld3b"), dma_ev(cx, "rkld3c")]
        ldw = dma_ev(cx, "rkldw")
        sty = dma_ev(cx, "rksty")
        ser = Ser(cx)
        t_vec = ld.mark(nc.sync.dma_start(out=rvec[:], in_=rvec_dram))
        ldw.mark(nc.gpsimd.dma_start(out=wup[:], in_=wup_dram))
        ldw.mark(nc.gpsimd.dma_start(out=aup[:], in_=aup_dram))
        t_w = ldw.mark(nc.gpsimd.dma_start(out=gup[:], in_=gup_dram.rearrange("(c p) n -> p c n", p=128)))
        ser.run(V, lambda: V.memset(seg[:], 1.0))
        ser.run(V, lambda: V.memset(seg[:].rearrange("p (c j) -> p c j", j=64)[:, :, 0:1], 0.0))
        ser.run(V, lambda: V.memset(bones[:], 0.0))
        ser.run(V, lambda: V.memset(bones[0:64, 0:64], 1.0))
        ser.run(V, lambda: V.memset(bones[64:128, 64:128], 1.0))
        ser.run(V, lambda: V.memset(idrep[:], 1.0))
        ser.run(V, lambda: V.memset(m1[:], -1.0))
        ser.run(V, lambda: V.memset(m2[:], 1.0))
        ser.run(V, lambda: V.memset(m3[:], -1.0))
        for hs in range(2):
            hp = slice(hs * 64, hs * 64 + 64)
            ser.run(POOL, lambda: POOL.affine_select(out=idrep[hp], in_=idrep[hp], pattern=[[0, 8], [-1, 64]], compare_op=ALU.is_equal,
                                                      fill=0.0, base=0, channel_multiplier=1))
            for mm_ in (m1, m2):
                ser.run(POOL, lambda: POOL.affine_select(out=mm_[hp, :, 0:64], in_=mm_[hp, :, 0:64], pattern=[[0, 4], [1, 64]],
                                                          compare_op=ALU.is_gt, fill=0.0, base=0, channel_multiplier=-1))
                ser.run(POOL, lambda: POOL.affine_select(out=mm_[hp, :, 64:128], in_=mm_[hp, :, 64:128], pattern=[[0, 4], [1, 64]],
                                                          compare_op=ALU.is_ge, fill=0.0, base=0, channel_multiplier=-1))
            ser.run(POOL, lambda: POOL.affine_select(out=m3[hp], in_=m3[hp], pattern=[[0, 4], [-1, 64]], compare_op=ALU.is_gt,
                                                      fill=0.0, base=0, channel_multiplier=1))
        GV = NRT * NVT
        ser.run(V, lambda: V.tensor_scalar(out=omka[:], in0=rvec[:, 0:GV].rearrange("p (t v) -> p t v", v=NVT)[:, :, 6], scalar1=-1.0, scalar2=1.0,
                                           op0=ALU.mult, op1=ALU.add), extra=[t_vec])

        def tshift(src_rows, mu_ap, Z, T0, T1):
            t = ld.mark(nc.sync.dma_start(out=T0[:], in_=projR[src_rows, :]))
            ser.run(V, lambda: V.tensor_tensor(out=T1[:, 1:S], in0=T0[:, 0:S - 1], in1=T0[:, 1:S], op=ALU.subtract), extra=[t])
            ser.run(V, lambda: V.tensor_scalar(out=T1[:, 0:1], in0=T0[:, 0:1], scalar1=-1.0, scalar2=None, op0=ALU.mult))
            ser.run(V, lambda: V.scalar_tensor_tensor(out=Z, in0=T1[:], scalar=mu_ap, in1=T0[:], op0=ALU.mult, op1=ALU.add))

        r0 = 3 * RW
        cx.wait(nc.sync, ser.last)
        tshift(slice(r0, r0 + 128), rvec[:, GV + 0:GV + 1], Fb[2][:], Fb[0], Fb[1])
        ser.run(ACT_, lambda: ACT_.activation(out=lxw[:], in_=Fb[2][:], func=AF.Tanh))
        cx.wait(nc.sync, ser.last)
        tshift(slice(r0 + 128, r0 + 256), rvec[:, GV + 1:GV + 2], Fb[2][:], Fb[0], Fb[1])
        ser.run(ACT_, lambda: ACT_.copy(out=lxa[:], in_=Fb[2][:]))
        for c2 in range(2):
            cx.wait(nc.sync, ser.last)
            tshift(slice(r0 + 256 + c2 * 128, r0 + 384 + c2 * 128), rvec[:, GV + 2 + c2:GV + 3 + c2], Fb[2][:], Fb[0], Fb[1])
            ser.run(ACT_, lambda: ACT_.activation(out=lxg[:, c2, :], in_=Fb[2][:], func=AF.Sigmoid))
        cx.wait(PE, t_w)
        for e_ in (PE, ACT_, V, nc.sync, POOL):
            for ev_ in (cx.pe, cx.act, cx.dve, cx.pool):
                cx.wait(e_, (ev_, ev_.n))
        for ti in range(NRT):
            vc = lambda i: rvec[:, ti * NVT + i:ti * NVT + i + 1]
            csl = slice(ti * 128, (ti + 1) * 128)
            F0, F1, F2, F3, F4, F5, F6, F7 = Fb
            dp = Dep(cx)
            NPc = NP

            def KK(name, p4=None):
                return [(name, p4)] if p4 is not None else [(name, i) for i in range(NPc)]

            def shift2(rows, mu_ap, Zn, T0n):
                Z, T0, T1 = Fmap[Zn], Fmap[T0n], Fmap["F2"]
                ldx = ld3[{"F0": 0, "F5": 1, "F6": 2}[T0n]]
                dp.dma(nc.sync, ldx, lambda: nc.sync.dma_start(out=T0[:], in_=projR[rows, :]), writes=KK(T0n))
                dp.run(V, lambda: V.tensor_tensor(out=T1[:, 1:S], in0=T0[:, 0:S - 1], in1=T0[:, 1:S], op=ALU.subtract), reads=KK(T0n), writes=KK("F2"))
                dp.run(V, lambda: V.tensor_scalar(out=T1[:, 0:1], in0=T0[:, 0:1], scalar1=-1.0, scalar2=None, op0=ALU.mult), reads=KK(T0n), writes=KK("F2"))
                dp.run(V, lambda: V.scalar_tensor_tensor(out=Z[:], in0=T1[:], scalar=mu_ap, in1=T0[:], op0=ALU.mult, op1=ALU.add),
                       reads=KK("F2") + KK(T0n), writes=KK(Zn))

            Fmap = {f"F{i}": Fb[i] for i in range(8)}
            shift2(slice(ti * 128, ti * 128 + 128), vc(0), "F1", "F0")
            shift2(slice(RW + ti * 128, RW + ti * 128 + 128), vc(1), "F3", "F5")
            shift2(slice(2 * RW + ti * 128, 2 * RW + ti * 128 + 128), vc(2), "F4", "F6")
            with A(nc) as pl:
                pAs = [pl.ps("rk_pA0", [128, 512]), pl.ps("rk_pA1", [128, 512])]
                pi = [0]

                def nextp():
                    pi[0] += 1
                    i = pi[0] % 2
                    return pAs[i], [("pA", i)]

                for p4 in range(NP):
                    psl = slice(p4 * 512, (p4 + 1) * 512)
                    pA, pk = nextp()
                    dp.run(PE, lambda: PE.matmul(pA[:], lhsT=wup[:, csl], rhs=lxw[:, psl], start=True, stop=True), writes=pk)
                    dp.run(ACT_, lambda: ACT_.activation(out=F5[:, psl], in_=pA[:], func=AF.Sigmoid, bias=vc(3)), reads=pk, writes=KK("F5", p4))
                    pA, pk = nextp()
                    dp.run(PE, lambda: PE.matmul(pA[:], lhsT=aup[:, csl], rhs=lxa[:, psl], start=True, stop=True), writes=pk)
                    dp.run(ACT_, lambda: ACT_.activation(out=F6[:, psl], in_=pA[:], func=AF.Sigmoid, bias=vc(4)), reads=pk, writes=KK("F6", p4))
                    pA, pk = nextp()
                    dp.run(PE, lambda: PE.matmul(pA[:], lhsT=gup[:, 0, csl], rhs=lxg[:, 0, psl], start=True, stop=False), writes=pk)
                    dp.run(PE, lambda: PE.matmul(pA[:], lhsT=gup[:, 1, csl], rhs=lxg[:, 1, psl], start=False, stop=True), writes=pk)
                    dp.run(ACT_, lambda: ACT_.copy(out=F7[:, psl], in_=pA[:]), reads=pk, writes=KK("F7", p4))
                dp.run(V, lambda: V.tensor_scalar(out=F5[:], in0=F5[:], scalar1=-float(np.exp(-0.5)), scalar2=None, op0=ALU.mult),
                       reads=KK("F5"), writes=KK("F5"))
                dp.run(V, lambda: V.tensor_scalar(out=F0[:], in0=F3[:], scalar1=vc(5), scalar2=None, op0=ALU.mult), reads=KK("F3"), writes=KK("F0"))
                dp.run(ACT_, lambda: ACT_.activation(out=F2[:], in_=F0[:], func=AF.Square), reads=KK("F0"), writes=KK("F2"))
                for p4 in range(NP):
                    psl = slice(p4 * 512, (p4 + 1) * 512)
                    pA, pk = nextp()
                    dp.run(PE, lambda: PE.matmul(pA[:], lhsT=bones[:], rhs=F2[:, psl], start=True, stop=True), reads=KK("F2", p4), writes=pk)
                    dp.run(V, lambda: V.tensor_scalar(out=F2[:, psl], in0=pA[:], scalar1=1e-24, scalar2=None, op0=ALU.max), reads=pk, writes=KK("F2", p4))
                dp.run(ACT_, lambda: ACT_.activation(out=F2[:], in_=F2[:], func=AF.Ln), reads=KK("F2"), writes=KK("F2"))
                dp.run(ACT_, lambda: ACT_.activation(out=F2[:], in_=F2[:], func=AF.Exp, scale=-0.5), reads=KK("F2"), writes=KK("F2"))
                dp.run(V, lambda: V.tensor_tensor(out=F0[:], in0=F0[:], in1=F2[:], op=ALU.mult), reads=KK("F0") + KK("F2"), writes=KK("F0"))
                dp.run(V, lambda: V.tensor_scalar(out=F2[:], in0=F6[:], scalar1=vc(6), scalar2=omka[:, ti:ti + 1], op0=ALU.mult, op1=ALU.add),
                       reads=KK("F6"), writes=KK("F2"))
                dp.run(V, lambda: V.tensor_tensor(out=F3[:], in0=F3[:], in1=F2[:], op=ALU.mult), reads=KK("F3") + KK("F2"), writes=KK("F3"))
                dp.run(V, lambda: V.scalar_tensor_tensor(out=F2[:], in0=F1[:], scalar=vc(7), in1=F3[:], op0=ALU.mult, op1=ALU.mult),
                       reads=KK("F1") + KK("F3"), writes=KK("F2"))
                for p4 in range(NP):
                    psl = slice(p4 * 512, (p4 + 1) * 512)
                    pA, pk = nextp()
                    dp.run(PE, lambda: PE.matmul(pA[:], lhsT=bones[:], rhs=F2[:, psl], start=True, stop=True), reads=KK("F2", p4), writes=pk)
                    dp.run(V, lambda: V.tensor_tensor(out=F2[:, psl], in0=pA[:], in1=F4[:, psl], op=ALU.mult), reads=pk + KK("F4", p4), writes=KK("F2", p4))
                dp.run(V, lambda: V.tensor_tensor(out=F6[:], in0=F0[:], in1=F6[:], op=ALU.mult), reads=KK("F0") + KK("F6"), writes=KK("F6"))
                dp.run(ACT_, lambda: ACT_.copy(out=VB[:], in_=F4[:]), reads=KK("F4"), writes=[("VB", 0)])
                dp.run(V, lambda: V.tensor_tensor_scan(out=F4[:], data0=seg[:], data1=F5[:], initial=0.0, op0=ALU.mult, op1=ALU.add),
                       reads=KK("F5"), writes=KK("F4"))
                clv = F4[:].rearrange("p (c j) -> p c j", j=64)
                dp.run(ACT_, lambda: ACT_.activation(out=PC[:], in_=clv[:, :, 63], func=AF.Exp), reads=KK("F4"), writes=[("PC", 0)])
                dp.run(V, lambda: V.tensor_scalar(out=nPC[:], in0=PC[:], scalar1=-1.0, scalar2=None, op0=ALU.mult), reads=[("PC", 0)], writes=[("nPC", 0)])
                dp.run(V, lambda: V.tensor_tensor(out=F5[:], in0=F4[:], in1=F5[:], op=ALU.subtract), reads=KK("F4") + KK("F5"), writes=KK("F5"))
                dp.run(ACT_, lambda: ACT_.activation(out=F5[:], in_=F5[:], func=AF.Exp), reads=KK("F5"), writes=KK("F5"))
                dp.run(V, lambda: V.tensor_tensor(out=KR[:, :, 0, :], in0=F0[:].rearrange("p (c j) -> p c j", j=64),
                                                  in1=F5[:].rearrange("p (c j) -> p c j", j=64), op=ALU.mult), reads=KK("F0") + KK("F5"), writes=[("KR", 0)])
                dp.run(ACT_, lambda: ACT_.activation(out=F5[:], in_=F4[:], func=AF.Exp), reads=KK("F4"), writes=KK("F5"))
                dp.run(V, lambda: V.tensor_tensor(out=KR[:, :, 1, :], in0=F1[:].rearrange("p (c j) -> p c j", j=64),
                                                  in1=F5[:].rearrange("p (c j) -> p c j", j=64), op=ALU.mult), reads=KK("F1") + KK("F5"), writes=[("KR", 1)])
                dp.run(ACT_, lambda: ACT_.activation(out=F5[:], in_=F4[:], func=AF.Exp, scale=-1.0), reads=KK("F4"), writes=KK("F5"))
                dp.run(V, lambda: V.tensor_tensor(out=F3[:], in0=F3[:], in1=F5[:], op=ALU.mult), reads=KK("F3") + KK("F5"), writes=KK("F3"))
                dp.run(V, lambda: V.tensor_tensor(out=F6[:], in0=F6[:], in1=F5[:], op=ALU.mult), reads=KK("F6") + KK("F5"), writes=KK("F6"))
                dp.run(ACT_, lambda: ACT_.copy(out=KT[:], in_=F3[:]), reads=KK("F3"), writes=[("KT", 0)])
                dp.run(ACT_, lambda: ACT_.copy(out=BT[:], in_=F6[:]), reads=KK("F6"), writes=[("BT", 0)])
                dp.run(V, lambda: V.tensor_tensor(out=KH[:].rearrange("p (c j) -> p c j", j=64), in0=F3[:].rearrange("p (c j) -> p c j", j=64),
                                                  in1=PC[:].unsqueeze(2).to_broadcast([128, NCH, 64]), op=ALU.mult), reads=KK("F3") + [("PC", 0)], writes=[("KH", 0)])
                dp.run(V, lambda: V.tensor_tensor(out=NBH[:].rearrange("p (c j) -> p c j", j=64), in0=F6[:].rearrange("p (c j) -> p c j", j=64),
                                                  in1=nPC[:].unsqueeze(2).to_broadcast([128, NCH, 64]), op=ALU.mult), reads=KK("F6") + [("nPC", 0)], writes=[("NBH", 0)])
                dp.all_done([PE, ACT_, V])
            ser = Ser(cx, (cx.dve, cx.dve.n))
            prep_tok = ser.last
            YT = F1
            with A(nc) as pl:
                pT = [pl.ps("rk_pT0", [128, 8, 64]), pl.ps("rk_pT1", [128, 8, 64])]
                pT_free = [None, None]
                cx.wait(PE, prep_tok)
                u = 0
                for (src, dst) in ((VB, Vtok), (KH, Khat), (NBH, nBhat)):
                    for g8 in range(NCH // 8):
                        s = u % 2
                        u += 1
                        cx.wait(PE, pT_free[s])
                        for c8 in range(8):
                            c = g8 * 8 + c8
                            for hs in range(2):
                                hp = slice(hs * 64, hs * 64 + 64)
                                ins = PE.matmul(pT[s][hp, c8, :], lhsT=src[hp, c * 64:(c + 1) * 64], rhs=C["ident_b"][hp, hs * 64:hs * 64 + 64],
                                                start=True, stop=True)
                        p_tok = cx.pe.mark(ins)
                        cx.wait(ACT_, p_tok)
                        pT_free[s] = cx.act.mark(ACT_.copy(out=dst[:, g8 * 8:(g8 + 1) * 8, :], in_=pT[s][:]))
                tm_tok = pT_free[(u - 1) % 2]
                tm_tok2 = pT_free[u % 2]
            with A(nc) as pl:
                p1 = [pl.ps("rk_p1a", [128, 4, 128]), pl.ps("rk_p1b", [128, 4, 128])]
                p2 = [pl.ps("rk_p2a", [128, 4, 128]), pl.ps("rk_p2b", [128, 4, 128])]
                p3 = [pl.ps("rk_p3a", [128, 4, 64]), pl.ps("rk_p3b", [128, 4, 64])]
                pfree = [None, None]
                psum_fence(cx)
                cx.wait(V, tm_tok)
                cx.wait(V, tm_tok2)
                for g4 in range(NCH // 4):
                    s = g4 % 2
                    cx.wait(PE, pfree[s])
                    for c4 in range(4):
                        c = g4 * 4 + c4
                        tsl = slice(c * 64, (c + 1) * 64)
                        for hs in range(2):
                            hp = slice(hs * 64, hs * 64 + 64)
                            PE.matmul(p1[s][hp, c4, :], lhsT=BT[hp, tsl], rhs=KR[hp, c, :, :], start=True, stop=True)
                            PE.matmul(p2[s][hp, c4, :], lhsT=KT[hp, tsl], rhs=KR[hp, c, :, :], start=True, stop=True)
                            ins = PE.matmul(p3[s][hp, c4, :], lhsT=KR[hp, c, 0, :], rhs=BT[hp, tsl], start=True, stop=True)
                    p_tok = cx.pe.mark(ins)
                    cx.wait(V, p_tok)
                    gsl = slice(g4 * 4, g4 * 4 + 4)
                    V.tensor_tensor(out=NTB[:, gsl, :], in0=p1[s][:], in1=m1[:], op=ALU.mult)
                    V.tensor_tensor(out=AB2[:, gsl, :], in0=p2[s][:], in1=m2[:], op=ALU.mult)
                    pfree[s] = cx.dve.mark(V.tensor_tensor(out=Nb[0][:, gsl, :], in0=p3[s][:], in1=m3[:], op=ALU.mult))
                ab_tok = pfree[(NCH // 4 - 1) % 2]
            NG8 = NCH // 8
            with A(nc) as pl:
                pN = [pl.ps("rk_pNa", [128, 8, 64]), pl.ps("rk_pNb", [128, 8, 64])]
                pNT = [pl.ps("rk_pNTa", [128, 8, 64]), pl.ps("rk_pNTb", [128, 8, 64])]
                pTT = [pl.ps("rk_pTa", [128, 8, 64]), pl.ps("rk_pTb", [128, 8, 64])]
                psum_fence(cx)
                cx.wait(V, ab_tok)
                for g8 in range(NG8):
                    gsl = slice(g8 * 8, g8 * 8 + 8)
                    V.tensor_copy(out=NTb[0][:, gsl, :], in_=NTB[:, gsl, 0:64])
                    t0_tok = cx.dve.mark(V.tensor_tensor(out=TTb[0][:, gsl, :], in0=NTB[:, gsl, 0:64], in1=idrep[:], op=ALU.add))
                cur = 0
                lvl_tok = t0_tok
                pN_free, pNT_free, pTT_free = [None, None], [None, None], [None, None]
                for k in range(5):
                    nxt = 1 - cur
                    cx.wait(PE, lvl_tok)
                    n_toks = []
                    for g8 in range(NG8):
                        s = g8 % 2
                        gsl = slice(g8 * 8, g8 * 8 + 8)
                        cx.wait(PE, pN_free[s])
                        cx.wait(PE, pNT_free[s])
                        for c8 in range(8):
                            c = g8 * 8 + c8
                            for hs in range(2):
                                hp = slice(hs * 64, hs * 64 + 64)
                                ins = PE.matmul(pN[s][hp, c8, :], lhsT=NTb[cur][hp, c, :], rhs=Nb[cur][hp, c, :], start=True, stop=True)
                                if k < 4:
                                    ins = PE.matmul(pNT[s][hp, c8, :], lhsT=Nb[cur][hp, c, :], rhs=NTb[cur][hp, c, :], start=True, stop=True)
                        p_tok = cx.pe.mark(ins)
                        cx.wait(ACT_, p_tok)
                        a_tok = cx.act.mark(ACT_.copy(out=Nb[nxt][:, gsl, :], in_=pN[s][:]))
                        pN_free[s] = a_tok
                        n_toks.append(a_tok)
                        if k < 4:
                            cx.wait(V, p_tok)
                            pNT_free[s] = cx.dve.mark(V.tensor_copy(out=NTb[nxt][:, gsl, :], in_=pNT[s][:]))
                    for g8 in range(NG8):
                        s = g8 % 2
                        gsl = slice(g8 * 8, g8 * 8 + 8)
                        cx.wait(PE, n_toks[g8])
                        cx.wait(PE, pTT_free[s])
                        for c8 in range(8):
                            c = g8 * 8 + c8
                            for hs in range(2):
                                hp = slice(hs * 64, hs * 64 + 64)
                                ins = PE.matmul(pTT[s][hp, c8, :], lhsT=Nb[nxt][hp, c, :], rhs=TTb[cur][hp, c, :], start=True, stop=True)
                        p_tok = cx.pe.mark(ins)
                        cx.wait(V, p_tok)
                        pTT_free[s] = cx.dve.mark(V.tensor_tensor(out=TTb[nxt][:, gsl, :], in0=pTT[s][:], in1=TTb[cur][:, gsl, :], op=ALU.add))
                    lvl_tok = pTT_free[(NG8 - 1) % 2]
                    cur = nxt
                TT = TTb[cur]
                inv_tok = (cx.dve, cx.dve.n)
            with A(nc) as pl:
                pX, pU, pY, pM = pl.ps("rk_pX", [128, 64]), pl.ps("rk_pU", [128, 64]), pl.ps("rk_pY", [128, 64]), pl.ps("rk_pM", [128, 64])
                cx.wait(V, inv_tok)
                V.memset(M32[:], 0.0)
                m_tok = cx.dve.mark(V.memset(Mb[:], 0.0))
                cx.wait(PE, inv_tok)
                cx.wait(PE, (cx.act, cx.act.n))
                y_tok = None
                xs_tok = None
                us_tok = None
                mb_tok = m_tok
                m32_tok = None
                for c in range(NCH):
                    for hs in range(2):
                        hp = slice(hs * 64, hs * 64 + 64)
                        PE.matmul(pX[hp, :], lhsT=AB2[hp, c, 0:64], rhs=Vtok[hp, c, :], start=True, stop=False)
                    cx.wait(PE, mb_tok)
                    for hs in range(2):
                        hp = slice(hs * 64, hs * 64 + 64)
                        ins = PE.matmul(pX[hp, :], lhsT=KR[hp, c, 0, :], rhs=Mb[hp, :], start=False, stop=True)
                    x_tok = cx.pe.mark(ins)
                    cx.wait(ACT_, x_tok)
                    xs_tok = cx.act.mark(ACT_.copy(out=Xs[:], in_=pX[:]))
                    cx.wait(PE, xs_tok)
                    for hs in range(2):
                        hp = slice(hs * 64, hs * 64 + 64)
                        ins = PE.matmul(pU[hp, :], lhsT=TT[hp, c, :], rhs=Xs[hp, :], start=True, stop=True)
                    u_tok = cx.pe.mark(ins)
                    cx.wait(V, u_tok)
                    us_tok = cx.dve.mark(V.tensor_copy(out=Us[:], in_=pU[:]))
                    cx.wait(PE, y_tok)
                    for hs in range(2):
                        hp = slice(hs * 64, hs * 64 + 64)
                        PE.matmul(pY[hp, :], lhsT=Mb[hp, :], rhs=KR[hp, c, 1, :], start=True, stop=False)
                        PE.matmul(pY[hp, :], lhsT=Vtok[hp, c, :], rhs=AB2[hp, c, 64:128], start=False, stop=False)
                    cx.wait(PE, us_tok)
                    for hs in range(2):
                        hp = slice(hs * 64, hs * 64 + 64)
                        ins = PE.matmul(pY[hp, :], lhsT=Us[hp, :], rhs=NTB[hp, c, 64:128], start=False, stop=True)
                    yp_tok = cx.pe.mark(ins)
                    cx.wait(PE, m32_tok)
                    for hs in range(2):
                        hp = slice(hs * 64, hs * 64 + 64)
                        PE.matmul(pM[hp, :], lhsT=Khat[hp, c, :], rhs=Vtok[hp, c, :], start=True, stop=False)
                        ins = PE.matmul(pM[hp, :], lhsT=nBhat[hp, c, :], rhs=Us[hp, :], start=False, stop=True)
                    mp_tok = cx.pe.mark(ins)
                    cx.wait(ACT_, yp_tok)
                    y_tok = cx.act.mark(ACT_.copy(out=YT[:, c * 64:(c + 1) * 64], in_=pY[:]))
                    cx.wait(V, mp_tok)
                    mb_tok = cx.dve.mark(V.scalar_tensor_tensor(out=Mb[:], in0=M32[:], scalar=PC[:, c:c + 1], in1=pM[:], op0=ALU.mult, op1=ALU.add))
                    m32_tok = cx.dve.mark(V.scalar_tensor_tensor(out=M32[:], in0=M32[:], scalar=PC[:, c:c + 1], in1=pM[:], op0=ALU.mult, op1=ALU.add))
                loop_tok = m32_tok
            ser = Ser(cx, loop_tok)
            with A(nc) as pl:
                pA = pl.ps("rk_pB", [128, 512])
                F0, F3 = Fb[0], Fb[3]
                for p4 in range(NP):
                    psl = slice(p4 * 512, (p4 + 1) * 512)
                    ser.run(PE, lambda: PE.matmul(pA[:], lhsT=bones[:], rhs=YT[:, psl], start=True, stop=True), extra=[y_tok])
                    ser.run(V, lambda: V.scalar_tensor_tensor(out=YT[:, psl], in0=pA[:], scalar=-1.0 / 64, in1=YT[:, psl], op0=ALU.mult, op1=ALU.add))
                ser.run(ACT_, lambda: ACT_.activation(out=F0[:], in_=YT[:], func=AF.Square))
                for p4 in range(NP):
                    psl = slice(p4 * 512, (p4 + 1) * 512)
                    ser.run(PE, lambda: PE.matmul(pA[:], lhsT=bones[:], rhs=F0[:, psl], start=True, stop=True))
                    ser.run(V, lambda: V.tensor_scalar(out=F0[:, psl], in0=pA[:], scalar1=1.0 / 64, scalar2=gn_eps, op0=ALU.mult, op1=ALU.add))
                ser.run(ACT_, lambda: ACT_.activation(out=F0[:], in_=F0[:], func=AF.Ln))
                ser.run(ACT_, lambda: ACT_.activation(out=F0[:], in_=F0[:], func=AF.Exp, scale=-0.5))
                ser.run(V, lambda: V.tensor_tensor(out=YT[:], in0=YT[:], in1=F0[:], op=ALU.mult))
                ser.run(V, lambda: V.tensor_scalar(out=YT[:], in0=YT[:], scalar1=vc(8), scalar2=vc(9), op0=ALU.mult, op1=ALU.add))
                ser.run(V, lambda: V.tensor_tensor(out=YT[:], in0=YT[:], in1=Fb[2][:], op=ALU.add))
                ser.run(V, lambda: V.tensor_tensor(out=ysb[:], in0=YT[:], in1=Fb[7][:], op=ALU.mult), extra=[(sty, sty.n)])
                cx.wait(nc.sync, ser.last)
                sty.mark(nc.sync.dma_start(out=yT_dram[y_row0 + ti * 128:y_row0 + (ti + 1) * 128, :], in_=ysb[:]))
            barrier(cx)
            if after_round is not None:
                after_round(ti)


def setup_consts(cx):
    nc = cx.nc
    C = {}
    C["ident_f"] = nc.alloc_sbuf_tensor("ident_f", [128, 128], F32)
    C["ones_f"] = nc.alloc_sbuf_tensor("ones_f", [128, 128], F32)
    C["ident_b"] = nc.alloc_sbuf_tensor("ident_b", [128, 128], BF16)
    C["ones_b"] = nc.alloc_sbuf_tensor("ones_b", [128, 128], BF16)
    t = cx.dve.mark(nc.vector.memset(C["ones_f"][:], 1.0))
    nc.vector.memset(C["ones_b"][:], 1.0)
    cx.wait(nc.gpsimd, t)
    t2 = cx.pool.mark(nc.gpsimd.affine_select(out=C["ident_f"][:], in_=C["ones_f"][:], pattern=[[-1, 128]], compare_op=ALU.is_equal,
                                              fill=0.0, base=0, channel_multiplier=1))
    cx.wait(nc.vector, t2)
    cx.dve.mark(nc.vector.tensor_copy(out=C["ident_b"][:], in_=C["ident_f"][:]))
    return C


CFG_FULL = dict(D=4096, F=11008, T=1024, NFH=8, NRT=8, HX=4, NM=256)
GROUPS = [[0, 1], [2, 3], [4, 5], [6, 7]]


def build_full(cfg):
    D, F, T, NFH, NRT, HX, NM = (cfg[k] for k in ("D", "F", "T", "NFH", "NRT", "HX", "NM"))
    S = 2 * T
    DC, FC = D // 128, F // 128
    FW, RW = NFH * 128, NRT * 128
    NTF = 3 * NFH
    NTR = 3 * NRT + 4 + 1
    NCOL = (NTF + NTR) * 128
    YC = FW + RW
    assert 2 * YC == D
    eps = 1e-6
    nc = bass.Bass("TRN2", target_bir_lowering=False)
    ein = lambda n, sh, dt=F32: nc.dram_tensor(n, sh, dt, kind="ExternalInput").ap()
    x = ein("x", [T, D])
    mem = ein("mem", [NM, D])
    wg1, wu1, wd1 = ein("wg1", [D, F]), ein("wu1", [D, F]), ein("wd1", [F, D])
    wg2, wu2, wd2 = ein("wg2", [D, F]), ein("wu2", [D, F]), ein("wd2", [F, D])
    w_in = ein("w_in_c", [D, NCOL])
    w_out = ein("w_out_p", [D, D])
    wq, wk, wv, wo = ein("wq", [D, D]), ein("wk", [D, D]), ein("wv", [D, D]), ein("wo", [D, D])
    gains = ein("gains", [128, 9, DC])
    rvec_d = ein("rvec", [128, NRT * NVT + 4])
    wup_d, aup_d, gup_d = ein("wup", [128, RW]), ein("aup", [128, RW]), ein("gup", [256, RW])
    fbias_d = ein("fbias", [128, 1])
    sel_d = ein("sel", [128, 2])
    out = nc.dram_tensor("out", [T, D], F32, kind="ExternalOutput").ap()
    xT = nc.dram_tensor("xT_s", [D, T], F32).ap()
    uT = nc.dram_tensor("uT_s", [F, T], BF16).ap()
    rawT = nc.dram_tensor("rawT_s", [D, T], F32).ap()
    qT = nc.dram_tensor("qT_s", [D, T], BF16).ap()
    HCH = max(1, (D * T * 2) // (1 << 20))
    HR = D // HCH
    hsnd = nc.dram_tensor("hsnd_s", [D, T], BF16)
    hrcv = nc.dram_tensor("hrcv_s", [HCH, 2 * HR, T], BF16)
    YCH = max(1, (YC * S * 2) // (1 << 20))
    YR = YC // YCH
    assert HR % 128 == 0 and YR % 128 == 0
    ysnd = nc.dram_tensor("ysnd_s", [YC, S], BF16)
    yrcv = nc.dram_tensor("yrcv_s", [YCH, 2 * YR, S], BF16)
    projF = nc.dram_tensor("projF_s", [NTF * 128, S], BF16).ap()
    projR = nc.dram_tensor("projR_s", [NTR * 128, S], F32).ap()
    kT_d = nc.dram_tensor("kT_s", [128, DC, NM], BF16).ap()
    vtok_d = nc.dram_tensor("vtok_s", [128, NM // 128, D], BF16).ap()

    cx = Cx(nc)
    C = setup_consts(cx)
    g_sb = nc.alloc_sbuf_tensor("g_sb", [128, 9, DC], F32)
    fb_sb = nc.alloc_sbuf_tensor("fb_sb", [128, 1], F32)
    sel_sb = nc.alloc_sbuf_tensor("sel_sb", [128, 2], F32)
    gl = dma_ev(cx, "gl")
    gl.mark(nc.sync.dma_start(out=g_sb[:], in_=gains))
    gl.mark(nc.sync.dma_start(out=fb_sb[:], in_=fbias_d))
    tok = gl.mark(nc.sync.dma_start(out=sel_sb[:], in_=sel_d))
    cx.wait(nc.vector, tok)
    for gi in (1, 7):
        cx.dve.mark(nc.vector.tensor_scalar(out=g_sb[:, gi, :], in0=g_sb[:, gi, :], scalar1=0.5, scalar2=None, op0=ALU.mult))
    barrier(cx)
    G = lambda i: g_sb[:, i, :]
    wsems = [dma_ev(cx, "w0"), dma_ev(cx, "w1")]
    wstate = {}
    cc = Ev(nc, "cc_ev", 1)
    ACTN = max(DC * T, FC * min(T, 512))

    with A(nc) as s1:
        actbuf = s1.sb("actbuf", [128, ACTN], BF16)
        act = actbuf[:, 0:DC * T].rearrange("p (k t) -> p k t", k=DC)
        norm_in_tokmajor(cx, x, T, DC, G(0), act, xT, C, eps)
        wbufs = [s1.sb("wb0", [128, 16384], BF16), s1.sb("wb1", [128, 16384], BF16)]
        ffn_up(cx, act, DC, T, wg1, wu1, FC, uT, wbufs, wsems, wstate)
        with A(nc) as sp:
            stA, stB = sp.ps("stA", [128, T]), sp.ps("stB", [128, T])
            ffn_down(cx, actbuf, FC, T, wd1, DC, uT, rawT, stA, C, wbufs, wsems, wstate)
            residual_pass(cx, rawT, stA, xT, T, DC, G(1), G(2), act, C, eps, stB)
        hs = dma_ev(cx, "hsnd")
        t = hs.mark(nc.sync.dma_start(out=hsnd.ap().rearrange("(k p) t -> p k t", p=128), in_=act))
        cx.wait(nc.gpsimd, t)
        barrier(cx)
        for c in range(HCH):
            t_cc = cc.mark(nc.gpsimd.collective_compute("AllGather", ALU.bypass, replica_groups=GROUPS,
                                                        ins=[hsnd.ap()[c * HR:(c + 1) * HR, :].opt()], outs=[hrcv.ap()[c].opt()]))
    wstate = {}
    with A(nc) as s0:
        wbufs = [s0.sb("wb0", [128, 16384], BF16), s0.sb("wb1", [128, 16384], BF16)]
        xattn_kv_phase(cx, DC, NM, mem, G(8), wk, wv, kT_d, vtok_d, C, eps, wbufs, wsems, wstate)
    wstate = {}
    for e_ in (nc.gpsimd, nc.sync, nc.tensor, nc.scalar, nc.vector):
        cx.wait(e_, t_cc)
    with A(nc) as s1b:
        actbuf = s1b.sb("actbuf", [128, ACTN], BF16)
        wbufs = [s1b.sb("wb0", [128, 16384], BF16), s1b.sb("wb1", [128, 16384], BF16)]
        halves = [[hrcv.ap()[c][th * HR:(th + 1) * HR, :] for c in range(HCH)] for th in range(2)]
        inproj_phase(cx, actbuf, None, S, DC, w_in, NTF, NTR, projF, projR, wbufs, wsems, wstate, halves=halves)
    wstate = {}
    ystate = {"next": 0, "tok": None}

    def issue_y(rows_done):
        while ystate["next"] < YCH and (ystate["next"] + 1) * YR <= rows_done:
            c = ystate["next"]
            ystate["tok"] = cc.mark(nc.gpsimd.collective_compute("AllGather", ALU.bypass, replica_groups=GROUPS,
                                                                 ins=[ysnd.ap()[c * YR:(c + 1) * YR, :].opt()], outs=[yrcv.ap()[c].opt()]))
            ystate["next"] += 1

    fox_phase(cx, S, NFH, projF, projR[(NTR - 1) * 128:NTR * 128, :], fb_sb[:], ysnd.ap(), C)
    issue_y(FW)
    rwkv_phase(cx, S, NRT, projR, rvec_d, wup_d, aup_d, gup_d, ysnd.ap(), FW, C, 64e-5, after_round=lambda ti: issue_y(FW + (ti + 1) * 128))
    assert ystate["next"] == YCH
    t = ystate["tok"]
    for e_ in (nc.gpsimd, nc.sync, nc.tensor, nc.scalar, nc.vector):
        cx.wait(e_, t)
    barrier(cx)
    with A(nc) as s2:
        actbuf = s2.sb("actbuf", [128, ACTN], BF16)
        act = actbuf[:, 0:DC * T].rearrange("p (k t) -> p k t", k=DC)
        wbufs = [s2.sb("wb0", [128, 16384], BF16), s2.sb("wb1", [128, 16384], BF16)]
        with A(nc) as sb:
            g0s = [sb.sb("bl_g00", [128, T], BF16), sb.sb("bl_g01", [128, T], BF16)]
            g1s = [sb.sb("bl_g10", [128, T], BF16), sb.sb("bl_g11", [128, T], BF16)]
            ldb = [dma_ev(cx, "bl0"), dma_ev(cx, "bl1")]
            free = [None, None]
            for k in range(DC):
                b = k % 2
                rank, loc = (k * 128) // YC, (k * 128) % YC
                cch, sub = loc // YR, loc % YR
                yk = yrcv.ap()[cch][rank * YR + sub:rank * YR + sub + 128, :]
                cx.wait(nc.sync, free[b])
                ldb[b].mark(nc.sync.dma_start(out=g0s[b][:], in_=yk[:, 0:T]))
                lt = ldb[b].mark(nc.sync.dma_start(out=g1s[b][:], in_=yk[:, T:2 * T]))
                cx.wait(nc.vector, lt)
                d = cx.dve.mark(nc.vector.tensor_scalar(out=g0s[b][:], in0=g0s[b][:], scalar1=sel_sb[:, 0:1], scalar2=None, op0=ALU.mult))
                cx.wait(nc.vector, d)
                free[b] = cx.dve.mark(nc.vector.scalar_tensor_tensor(out=act[:, k, :], in0=g1s[b][:], scalar=sel_sb[:, 1:2], in1=g0s[b][:],
                                                                     op0=ALU.mult, op1=ALU.add))
            barrier(cx)
        with A(nc) as sp:
            stA, stB = sp.ps("stA", [128, T]), sp.ps("stB", [128, T])
            proj_raw_stats(cx, actbuf, T, DC, w_out, rawT, stA, C, wbufs, wsems, wstate)
            residual_pass(cx, rawT, stA, xT, T, DC, G(3), G(4), act, C, eps, stB)
        xattn_phase(cx, actbuf, T, DC, HX, NM, kT_d, vtok_d, wq, qT, C, wbufs, wsems, wstate)
        with A(nc) as sp:
            stA, stB = sp.ps("stA", [128, T]), sp.ps("stB", [128, T])
            proj_raw_stats(cx, actbuf, T, DC, wo, rawT, stA, C, wbufs, wsems, wstate)
            residual_pass(cx, rawT, stA, xT, T, DC, G(5), G(6), act, C, eps, stB)
        ffn_up(cx, act, DC, T, wg2, wu2, FC, uT, wbufs, wsems, wstate)
        with A(nc) as sp:
            stA, stB = sp.ps("stA", [128, T]), sp.ps("stB", [128, T])
            ffn_down(cx, actbuf, FC, T, wd2, DC, uT, rawT, stA, C, wbufs, wsems, wstate)
            residual_pass(cx, rawT, stA, xT, T, DC, G(7), G(0), act, C, eps, stB, out_dram=out)
    barrier(cx)
    return nc


def host_layout(cfg, inp):
    D, F, T, NFH, NRT, HX, NM = (cfg[k] for k in ("D", "F", "T", "NFH", "NRT", "HX", "NM"))
    DC = D // 128
    FW, RW = NFH * 128, NRT * 128
    FOXW, RWW = 2 * FW, 2 * RW
    f32 = lambda a: np.ascontiguousarray(np.asarray(a, dtype=np.float32))
    tl = lambda v: np.ascontiguousarray(f32(v).reshape(-1, 128).T)
    w_in = f32(inp["w_in"][0])
    R0 = 3 * FOXW + 2 * NFH
    gl = [inp[k][0] for k in ("ffn1_pre_g", "ffn1_post_g", "mix_pre_g", "mix_post_g", "xattn_pre_g", "xattn_post_g",
                              "ffn2_pre_g", "ffn2_post_g", "mem_norm_g")]
    gains = np.ascontiguousarray(np.stack([tl(g) for g in gl], axis=1))
    w_out = f32(inp["w_out"][0])
    w_out_p = np.ascontiguousarray(np.concatenate([w_out[0:FW], w_out[FOXW:FOXW + RW], w_out[FW:FOXW], w_out[FOXW + RW:FOXW + RWW]], 0))
    shared = dict(
        wg1=f32(inp["ffn1_w_gate"][0]), wu1=f32(inp["ffn1_w_up"][0]), wd1=f32(inp["ffn1_w_down"][0]),
        wg2=f32(inp["ffn2_w_gate"][0]), wu2=f32(inp["ffn2_w_up"][0]), wd2=f32(inp["ffn2_w_down"][0]),
        w_out_p=w_out_p, wq=f32(inp["xattn_wq"][0]), wk=f32(inp["xattn_wk"][0]), wv=f32(inp["xattn_wv"][0]), wo=f32(inp["xattn_wo"][0]),
        gains=gains)
    mu = f32(inp["rwkv_mu"][0])
    per_hh = []
    for hh in range(2):
        cols = []
        for blk in range(3):
            cols.append(w_in[:, blk * FOXW + hh * FW: blk * FOXW + (hh + 1) * FW])
        for blk in range(3):
            cols.append(w_in[:, R0 + blk * RWW + hh * RW: R0 + blk * RWW + (hh + 1) * RW])
        pad = lambda a, n: np.concatenate([a, np.zeros((a.shape[0], n - a.shape[1]), np.float32)], 1)
        L0 = R0 + 3 * RWW
        cols.append(pad(w_in[:, L0:L0 + 96], 128))
        cols.append(pad(w_in[:, L0 + 96:L0 + 192], 128))
        cols.append(w_in[:, L0 + 192:L0 + 448])
        cols.append(pad(w_in[:, 3 * FOXW + hh * NFH: 3 * FOXW + (hh + 1) * NFH], 128))
        w_in_c = np.ascontiguousarray(np.concatenate(cols, 1))
        ch = slice(hh * RW, (hh + 1) * RW)
        rvec = np.zeros((128, NRT * NVT + 4), np.float32)
        vecs = [mu[0:RWW][ch], mu[RWW:2 * RWW][ch], mu[2 * RWW:3 * RWW][ch], inp["rwkv_w0"][0][ch], inp["rwkv_a0"][0][ch],
                inp["rwkv_k_k"][0][ch], inp["rwkv_k_a"][0][ch], inp["rwkv_r_k"][0][ch], inp["rwkv_ln_w"][0][ch], inp["rwkv_ln_b"][0][ch]]
        for i, v in enumerate(vecs):
            rvec[:, i:NRT * NVT:NVT] = tl(v)
        GV = NRT * NVT
        M0 = 3 * RWW
        rvec[0:96, GV] = mu[M0:M0 + 96]
        rvec[0:96, GV + 1] = mu[M0 + 96:M0 + 192]
        rvec[:, GV + 2] = mu[M0 + 192:M0 + 320]
        rvec[:, GV + 3] = mu[M0 + 320:M0 + 448]
        wup = np.zeros((128, RW), np.float32)
        wup[:96] = f32(inp["rwkv_w_up"][0])[:, ch]
        aup = np.zeros((128, RW), np.float32)
        aup[:96] = f32(inp["rwkv_a_up"][0])[:, ch]
        gup = np.ascontiguousarray(f32(inp["rwkv_g_up"][0])[:, ch])
        fb = np.zeros((128, 1), np.float32)
        fb[0:NFH, 0] = f32(inp["fox_f_bias"][0])[hh * NFH:(hh + 1) * NFH]
        sel = np.zeros((128, 2), np.float32)
        sel[:, hh] = 1.0
        per_hh.append(dict(w_in_c=w_in_c, rvec=rvec, wup=wup, aup=aup, gup=gup, fbias=fb, sel=sel))
    xs = f32(inp["x"])
    mems = f32(inp["mem"])
    in_maps = []
    for core in range(8):
        b, hh = core // 2, core % 2
        m = dict(shared)
        m.update(per_hh[hh])
        m["x"] = np.ascontiguousarray(xs[b, hh * T:(hh + 1) * T, :])
        m["mem"] = np.ascontiguousarray(mems[b])
        in_maps.append(m)
    return in_maps


def run_cfg(cfg, inp, trace=False):
    nc = build_full(cfg)
    in_maps = host_layout(cfg, inp)
    res = run_bass_kernel_spmd(nc, in_maps, core_ids=list(range(8)), **({"trace": True} if trace else {}))
    T, D = cfg["T"], cfg["D"]
    outp = np.zeros((4, 2 * T, D), np.float32)
    for core in range(8):
        b, hh = core // 2, core % 2
        outp[b, hh * T:(hh + 1) * T, :] = res.results[core]["out"]
    return outp, res


def kernel(**inputs):
    outp, _ = run_cfg(CFG_FULL, inputs)
    return outp
```

```python
import numpy as np
from concourse.bass_utils import run_bass_kernel_spmd
import concourse.bass as bass
import concourse.mybir as mybir

F32 = mybir.dt.float32
BF16 = mybir.dt.bfloat16
AF = mybir.ActivationFunctionType
ALU = mybir.AluOpType
_uid = [0]


def un(s):
    _uid[0] += 1
    return f"{s}_{_uid[0]}"


class A:
    def __init__(self, nc):
        from contextlib import ExitStack
        self.nc = nc
        self.st = ExitStack()

    def __enter__(self):
        self.st.__enter__()
        return self

    def __exit__(self, *a):
        return self.st.__exit__(*a)

    def sb(self, name, shape, dt):
        return self.st.enter_context(self.nc.sbuf_tensor(un(name), shape, dt))

    def ps(self, name, shape, dt=None):
        return self.st.enter_context(self.nc.psum_tensor(un(name), shape, dt if dt is not None else F32))


class Ev:
    def __init__(self, nc, name, step=1):
        self.sem = nc.alloc_semaphore(name)
        self.n = 0
        self.step = step

    def mark(self, ins):
        ins.then_inc(self.sem, self.step)
        self.n += self.step
        return (self, self.n)


class Cx:
    def __init__(self, nc):
        self.nc = nc
        self.pe = Ev(nc, "e_pe")
        self.act = Ev(nc, "e_act")
        self.dve = Ev(nc, "e_dve")
        self.pool = Ev(nc, "e_pool")
        self.waited = {}
        self._n = 0
        self.dma_evs = []

    def ev(self, name, step=16):
        self._n += 1
        return Ev(self.nc, f"{name}_{self._n}", step)

    def wait(self, eng, tok):
        if tok is None:
            return
        ev, v = tok
        if v <= 0:
            return
        key = (id(eng), id(ev))
        if self.waited.get(key, 0) >= v:
            return
        self.waited[key] = v
        eng.wait_ge(ev.sem, v)

    def evof(self, eng):
        nc = self.nc
        if eng is nc.tensor:
            return self.pe
        if eng is nc.scalar:
            return self.act
        if eng is nc.vector:
            return self.dve
        if eng is nc.gpsimd:
            return self.pool
        raise ValueError

    def op(self, eng, ins, deps=()):
        return self.evof(eng).mark(ins)


def gemm(cx, act, KC, T, Ws, n_tiles, ngt, wbufs, wsems, ps_sets, epilogue, pre_tok=None, wstate=None):
    nc = cx.nc
    nW = len(Ws)
    NG = ngt * 128
    n_groups = (n_tiles + ngt - 1) // ngt
    if wstate is None:
        wstate = {}
    wfree = wstate.setdefault("wfree", [None, None])
    gi0 = wstate.get("gi", 0)
    ps_free = [None, None]
    nth = (T + 511) // 512
    tile_i = 0
    last_tok = None
    for g in range(n_groups):
        b = (gi0 + g) % 2
        gt = min(ngt, n_tiles - g * ngt)
        ncols = gt * 128
        wv = wbufs[b][:, 0:nW * KC * NG].rearrange("p (w k n) -> p w k n", w=nW, k=KC)
        cx.wait(nc.gpsimd, wfree[b])
        for wi, W in enumerate(Ws):
            src = W.rearrange("(k p) n -> p k n", p=128)[:, :, g * NG:g * NG + ncols]
            wtok = wsems[b].mark(nc.gpsimd.dma_start(out=wv[:, wi, :, 0:ncols], in_=src))
        cx.wait(nc.tensor, wtok)
        if g == 0:
            cx.wait(nc.tensor, pre_tok)
        for jt in range(gt):
            j = g * ngt + jt
            s = tile_i % 2
            cx.wait(nc.tensor, ps_free[s])
            ins = None
            for wi in range(nW):
                for k in range(KC):
                    for th in range(nth):
                        t0 = th * 512
                        t1 = min(T, t0 + 512)
                        ins = nc.tensor.matmul(ps_sets[s][wi][:, t0:t1], lhsT=wv[:, wi, k, jt * 128:(jt + 1) * 128],
                                               rhs=act[:, k, t0:t1], start=(k == 0), stop=(k == KC - 1))
            pe_tok = cx.pe.mark(ins)
            last_tok = pe_tok
            ps_free[s] = epilogue(j, ps_sets[s], pe_tok)
            tile_i += 1
        wfree[b] = last_tok
    wstate["gi"] = gi0 + n_groups
    return last_tok


def psum_fence(cx):
    nc = cx.nc
    cx.wait(nc.tensor, (cx.act, cx.act.n))
    cx.wait(nc.tensor, (cx.dve, cx.dve.n))


def barrier(cx, engines=None):
    nc = cx.nc
    engs = engines or [nc.tensor, nc.scalar, nc.vector, nc.gpsimd, nc.sync]
    evs = [cx.pe, cx.act, cx.dve, cx.pool] + cx.dma_evs
    for e in engs:
        for ev in evs:
            cx.wait(e, (ev, ev.n))


def dma_ev(cx, name):
    ev = Ev(cx.nc, f"{name}_{len(cx.dma_evs)}", 16)
    cx.dma_evs.append(ev)
    return ev


def rstd_from_stats(cx, out_sb, st_ps, D, eps, pre_tok):
    nc = cx.nc
    cx.wait(nc.vector, pre_tok)
    t = cx.dve.mark(nc.vector.tensor_scalar(out=out_sb, in0=st_ps, scalar1=1.0 / D, scalar2=eps, op0=ALU.mult, op1=ALU.add))
    cx.wait(nc.vector, t)
    cx.wait(nc.scalar, t)
    t = cx.act.mark(nc.scalar.activation(out=out_sb, in_=out_sb, func=AF.Sqrt))
    cx.wait(nc.vector, t)
    t = cx.dve.mark(nc.vector.reciprocal(out=out_sb, in_=out_sb))
    cx.wait(nc.vector, t)
    return t


def norm_in_tokmajor(cx, x_dram, Tt, DC, g_sb, act_out, xT_dram, C, eps, pre_tok=None):
    nc = cx.nc
    D = DC * 128
    ntt = Tt // 128
    ncg = DC // 4
    with (nc.sbuf_tensor(un("n0_xt0"), [128, D], F32) as xt0, nc.sbuf_tensor(un("n0_xt1"), [128, D], F32) as xt1,
          nc.sbuf_tensor(un("n0_xTt"), [128, DC, 128], F32) as xTt,
          nc.sbuf_tensor(un("n0_sq0"), [128, 4, 128], F32) as sq0, nc.sbuf_tensor(un("n0_sq1"), [128, 4, 128], F32) as sq1,
          nc.sbuf_tensor(un("n0_rstd"), [128, 128], F32) as rstd,
          nc.psum_tensor(un("n0_pT0"), [128, 4, 128], F32) as pT0, nc.psum_tensor(un("n0_pT1"), [128, 4, 128], F32) as pT1,
          nc.psum_tensor(un("n0_st"), [128, 128], F32) as stp):
        xts = [xt0, xt1]
        sqs = [sq0, sq1]
        pTs = [pT0, pT1]
        ld = [dma_ev(cx, "n0ld0"), dma_ev(cx, "n0ld1")]
        st = dma_ev(cx, "n0st")
        xt_free = [pre_tok, pre_tok]
        pT_free = [None, None]
        sq_free = [None, None]
        xTt_free = [pre_tok, None]
        stp_free = None
        ld_tok = [None, None]

        def issue_load(tt):
            b = tt % 2
            cx.wait(nc.sync, xt_free[b])
            ld_tok[b] = ld[b].mark(nc.sync.dma_start(out=xts[b][:], in_=x_dram[tt * 128:(tt + 1) * 128, :]))

        issue_load(0)
        u = 0
        for tt in range(ntt):
            b = tt % 2
            if tt + 1 < ntt:
                issue_load(tt + 1)
            cx.wait(nc.tensor, ld_tok[b])
            pending = None
            a_tok = None
            for cg in range(ncg):
                s = u % 2
                u += 1
                cx.wait(nc.tensor, pT_free[s])
                for c4 in range(4):
                    c = cg * 4 + c4
                    ins = nc.tensor.transpose(pTs[s][:, c4, :], xts[b][:, c * 128:(c + 1) * 128], C["ident_f"][:])
                p_tok = cx.pe.mark(ins)
                if pending is not None:
                    pa_tok, ps_, pcg = pending
                    cx.wait(nc.tensor, pa_tok)
                    if pcg == 0:
                        cx.wait(nc.tensor, stp_free)
                    for c4 in range(4):
                        ins = nc.tensor.matmul(stp[:, :], lhsT=C["ones_f"][:], rhs=sqs[ps_][:, c4, :],
                                               start=(pcg == 0 and c4 == 0), stop=False)
                    sq_free[ps_] = cx.pe.mark(ins)
                cx.wait(nc.scalar, p_tok)
                if cg == 0:
                    cx.wait(nc.scalar, xTt_free[0])
                    cx.wait(nc.scalar, xTt_free[1])
                nc.scalar.copy(out=xTt[:, cg * 4:(cg + 1) * 4, :], in_=pTs[s][:])
                cx.wait(nc.scalar, sq_free[s])
                a_tok = cx.act.mark(nc.scalar.activation(out=sqs[s][:], in_=pTs[s][:], func=AF.Square))
                pT_free[s] = a_tok
                pending = (a_tok, s, cg)
            xt_free[b] = p_tok
            pa_tok, ps_, pcg = pending
            cx.wait(nc.tensor, pa_tok)
            if pcg == 0:
                cx.wait(nc.tensor, stp_free)
            for c4 in range(4):
                ins = nc.tensor.matmul(stp[:, :], lhsT=C["ones_f"][:], rhs=sqs[ps_][:, c4, :],
                                       start=(pcg == 0 and c4 == 0), stop=(c4 == 3))
            s_tok = cx.pe.mark(ins)
            sq_free[ps_] = s_tok
            r_tok = rstd_from_stats(cx, rstd[:], stp[:, :], D, eps, s_tok)
            stp_free = r_tok
            cx.wait(nc.vector, a_tok)
            for c in range(DC):
                ins = nc.vector.scalar_tensor_tensor(out=act_out[:, c, tt * 128:(tt + 1) * 128], in0=xTt[:, c, :],
                                                     scalar=g_sb[:, c:c + 1], in1=rstd[:], op0=ALU.mult, op1=ALU.mult)
            d_tok = cx.dve.mark(ins)
            xTt_free[0] = d_tok
            if xT_dram is not None:
                cx.wait(nc.sync, a_tok)
                xTt_free[1] = st.mark(nc.sync.dma_start(
                    out=xT_dram.rearrange("(c p) t -> p c t", p=128)[:, :, tt * 128:(tt + 1) * 128], in_=xTt[:]))
        barrier(cx)


def ffn_up(cx, act, KC, T, Wg, Wu, FC, uT_dram, wbufs, wsems, wstate, pre_tok=None):
    nc = cx.nc
    with (nc.sbuf_tensor(un("fu_sg0"), [128, T], F32) as sg0, nc.sbuf_tensor(un("fu_sg1"), [128, T], F32) as sg1,
          nc.sbuf_tensor(un("fu_u0"), [128, T], BF16) as u0, nc.sbuf_tensor(un("fu_u1"), [128, T], BF16) as u1,
          nc.psum_tensor(un("fu_pg0"), [128, T], F32) as pg0, nc.psum_tensor(un("fu_pu0"), [128, T], F32) as pu0,
          nc.psum_tensor(un("fu_pg1"), [128, T], F32) as pg1, nc.psum_tensor(un("fu_pu1"), [128, T], F32) as pu1):
        sgs = [sg0, sg1]
        us = [u0, u1]
        st = [dma_ev(cx, "fust0"), dma_ev(cx, "fust1")]
        sg_free = [None, None]
        u_free = [None, None]

        def epi(j, pss, pe_tok):
            s = j % 2
            cx.wait(nc.scalar, pe_tok)
            cx.wait(nc.scalar, sg_free[s])
            a_tok = cx.act.mark(nc.scalar.activation(out=sgs[s][:], in_=pss[0][:], func=AF.Silu))
            cx.wait(nc.vector, a_tok)
            cx.wait(nc.vector, u_free[s])
            d_tok = cx.dve.mark(nc.vector.tensor_tensor(out=us[s][:], in0=sgs[s][:], in1=pss[1][:], op=ALU.mult))
            sg_free[s] = d_tok
            cx.wait(nc.sync, d_tok)
            u_free[s] = st[s].mark(nc.sync.dma_start(out=uT_dram[j * 128:(j + 1) * 128, :], in_=us[s][:]))
            return d_tok

        ngt = max(1, min(FC, (16384 // (2 * KC)) // 128))
        gemm(cx, act, KC, T, [Wg, Wu], FC, ngt, wbufs, wsems, [[pg0[:], pu0[:]], [pg1[:], pu1[:]]], epi,
             pre_tok=pre_tok, wstate=wstate)
        barrier(cx)


def gemm_raw_stats(cx, act, KC, T, t_off, W, DC, rawT_dram, st_ps, C, wbufs, wsems, wstate, ngt, pre_tok=None, tagn=""):
    nc = cx.nc
    nth = (T + 511) // 512
    with (nc.sbuf_tensor(un("gr_raw0") + tagn, [128, T], F32) as r0, nc.sbuf_tensor(un("gr_raw1") + tagn, [128, T], F32) as r1,
          nc.sbuf_tensor(un("gr_sq0") + tagn, [128, T], F32) as q0, nc.sbuf_tensor(un("gr_sq1") + tagn, [128, T], F32) as q1,
          nc.psum_tensor(un("gr_p0") + tagn, [128, T], F32) as p0, nc.psum_tensor(un("gr_p1") + tagn, [128, T], F32) as p1):
        rs = [r0, r1]
        qs = [q0, q1]
        st = [dma_ev(cx, "grst0"), dma_ev(cx, "grst1")]
        r_free = [None, None]
        q_free = [None, None]
        pend = []

        def stats_mm(j, s, a_tok):
            cx.wait(nc.tensor, a_tok)
            for th in range(nth):
                t0, t1 = th * 512, min(T, th * 512 + 512)
                ins = nc.tensor.matmul(st_ps[:, t_off + t0:t_off + t1], lhsT=C["ones_f"][:], rhs=qs[s][:, t0:t1],
                                       start=(j == 0), stop=(j == DC - 1))
            q_free[s] = cx.pe.mark(ins)

        def epi(j, pss, pe_tok):
            s = j % 2
            if pend:
                stats_mm(*pend.pop())
            cx.wait(nc.scalar, pe_tok)
            cx.wait(nc.scalar, r_free[s])
            a0 = cx.act.mark(nc.scalar.copy(out=rs[s][:], in_=pss[0][:]))
            cx.wait(nc.scalar, q_free[s])
            a_tok = cx.act.mark(nc.scalar.activation(out=qs[s][:], in_=pss[0][:], func=AF.Square))
            cx.wait(nc.sync, a0)
            r_free[s] = st[s].mark(nc.sync.dma_start(out=rawT_dram[j * 128:(j + 1) * 128, t_off:t_off + T], in_=rs[s][:]))
            pend.append((j, s, a_tok))
            return a_tok

        gemm(cx, act, KC, T, [W], DC, ngt, wbufs, wsems, [[p0[:]], [p1[:]]], epi, pre_tok=pre_tok, wstate=wstate)
        stats_mm(*pend.pop())
        barrier(cx)


def gemm_raw_acc(cx, act, KC, T, W, DC, rawT_dram, st_ps, C, wbufs, wsems, wstate, ngt, pre_tok=None, second=False):
    nc = cx.nc
    nth = (T + 511) // 512
    with A(nc) as al:
        rs = [al.sb("ga_r0", [128, T], F32), al.sb("ga_r1", [128, T], F32)]
        ps = [al.ps("ga_p0", [128, T]), al.ps("ga_p1", [128, T])]
        st = [dma_ev(cx, "gast0"), dma_ev(cx, "gast1")]
        r_free = [None, None]
        if second:
            qs = [al.sb("ga_q0", [128, T], F32), al.sb("ga_q1", [128, T], F32)]
            pb = [al.sb("ga_pb0", [128, T], F32), al.sb("ga_pb1", [128, T], F32)]
            ldp = [dma_ev(cx, "gald0"), dma_ev(cx, "gald1")]
            q_free = [None, None]
            pb_free = [None, None]
            pend = []

            def stats_mm(j, s, a_tok):
                cx.wait(nc.tensor, a_tok)
                for th in range(nth):
                    t0, t1 = th * 512, min(T, th * 512 + 512)
                    ins = nc.tensor.matmul(st_ps[:, t0:t1], lhsT=C["ones_f"][:], rhs=qs[s][:, t0:t1], start=(j == 0), stop=(j == DC - 1))
                q_free[s] = cx.pe.mark(ins)

        def epi(j, pss, pe_tok):
            s = j % 2
            if not second:
                cx.wait(nc.scalar, pe_tok)
                cx.wait(nc.scalar, r_free[s])
                a0 = cx.act.mark(nc.scalar.copy(out=rs[s][:], in_=pss[0][:]))
                cx.wait(nc.sync, a0)
                r_free[s] = st[s].mark(nc.sync.dma_start(out=rawT_dram[j * 128:(j + 1) * 128, :], in_=rs[s][:]))
                return a0
            if pend:
                stats_mm(*pend.pop())
            cx.wait(nc.sync, pb_free[s])
            l_tok = ldp[s].mark(nc.sync.dma_start(out=pb[s][:], in_=rawT_dram[j * 128:(j + 1) * 128, :]))
            cx.wait(nc.vector, pe_tok)
            cx.wait(nc.vector, l_tok)
            cx.wait(nc.vector, r_free[s])
            d0 = cx.dve.mark(nc.vector.tensor_tensor(out=rs[s][:], in0=pss[0][:], in1=pb[s][:], op=ALU.add))
            pb_free[s] = d0
            cx.wait(nc.scalar, d0)
            cx.wait(nc.scalar, q_free[s])
            a_tok = cx.act.mark(nc.scalar.activation(out=qs[s][:], in_=rs[s][:], func=AF.Square))
            cx.wait(nc.sync, a_tok)
            r_free[s] = st[s].mark(nc.sync.dma_start(out=rawT_dram[j * 128:(j + 1) * 128, :], in_=rs[s][:]))
            pend.append((j, s, a_tok))
            return d0

        gemm(cx, act, KC, T, [W], DC, ngt, wbufs, wsems, [[ps[0][:]], [ps[1][:]]], epi, pre_tok=pre_tok, wstate=wstate)
        if second:
            stats_mm(*pend.pop())
        barrier(cx)


def ffn_down(cx, actbuf, FC, T, Wd, DC, uT_dram, rawT_dram, st_ps, C, wbufs, wsems, wstate):
    nc = cx.nc
    ld = dma_ev(cx, "fdld")
    KH = (FC + 1) // 2
    k0 = 0
    for ph in range(2):
        kc = min(KH, FC - k0)
        act = actbuf[:, 0:kc * T].rearrange("p (k t) -> p k t", k=kc)
        tok = ld.mark(nc.sync.dma_start(out=act, in_=uT_dram[k0 * 128:(k0 + kc) * 128, :].rearrange("(k p) t -> p k t", p=128)))
        ngt = max(1, min(DC, (16384 // kc) // 128))
        gemm_raw_acc(cx, act, kc, T, Wd[k0 * 128:(k0 + kc) * 128, :], DC, rawT_dram, st_ps, C, wbufs, wsems, wstate, ngt,
                     pre_tok=tok, second=(ph == 1))
        k0 += kc


def residual_pass(cx, rawT_dram, st_ps, xT_dram, T, DC, gpost_sb, gpre_sb, act_out, C, eps, st2_ps, out_dram=None):
    nc = cx.nc
    D = DC * 128
    nth = (T + 511) // 512
    final = out_dram is not None
    NBF = 4
    with A(nc) as al:
        rbs = [al.sb(f"rp_r{i}", [128, T], F32) for i in range(NBF)]
        xbs = [al.sb(f"rp_x{i}", [128, T], F32) for i in range(NBF)]
        qbs = [al.sb(f"rp_q{i}", [128, T], F32) for i in range(2)]
        rsa, rsb = al.sb("rp_rsa", [128, T], F32), al.sb("rp_rsb", [128, T], F32)
        pTs = [al.ps("rp_pT0", [128, 4, 128]), al.ps("rp_pT1", [128, 4, 128])]
        ld = [dma_ev(cx, f"rpld{i}") for i in range(NBF)]
        st = [dma_ev(cx, f"rpst{i}") for i in range(NBF)]
        ra_tok = rstd_from_stats(cx, rsa[:], st_ps[:, 0:T], D, eps, None)
        r_free = [None] * NBF
        x_free = [[None, None] for _ in range(NBF)]
        q_free = [None, None]
        pT_free = [None, None]
        xT_v = xT_dram.rearrange("(c p) t -> c p t", p=128)
        raw_v = rawT_dram.rearrange("(c p) t -> c p t", p=128)
        u = 0
        for c in range(DC):
            b = c % NBF
            qi = c % 2
            cx.wait(nc.sync, r_free[b])
            cx.wait(nc.sync, x_free[b][0])
            cx.wait(nc.sync, x_free[b][1])
            ld[b].mark(nc.sync.dma_start(out=rbs[b][:], in_=raw_v[c]))
            ltok = ld[b].mark(nc.sync.dma_start(out=xbs[b][:], in_=xT_v[c]))
            cx.wait(nc.vector, ltok)
            t1 = cx.dve.mark(nc.vector.scalar_tensor_tensor(out=rbs[b][:], in0=rbs[b][:], scalar=gpost_sb[:, c:c + 1], in1=rsa[:],
                                                            op0=ALU.mult, op1=ALU.mult))
            cx.wait(nc.vector, t1)
            d_tok = cx.dve.mark(nc.vector.tensor_tensor(out=xbs[b][:], in0=rbs[b][:], in1=xbs[b][:], op=ALU.add))
            r_free[b] = d_tok
            if not final:
                cx.wait(nc.scalar, d_tok)
                x_free[b][0] = st[b].mark(nc.scalar.dma_start(out=xT_v[c], in_=xbs[b][:]))
                cx.wait(nc.scalar, q_free[qi])
                a_tok = cx.act.mark(nc.scalar.activation(out=qbs[qi][:], in_=xbs[b][:], func=AF.Square))
                x_free[b][1] = a_tok
                cx.wait(nc.tensor, a_tok)
                for th in range(nth):
                    t0, t1_ = th * 512, min(T, th * 512 + 512)
                    ins = nc.tensor.matmul(st2_ps[:, t0:t1_], lhsT=C["ones_f"][:], rhs=qbs[qi][:, t0:t1_],
                                           start=(c == 0), stop=(c == DC - 1))
                q_free[qi] = cx.pe.mark(ins)
            else:
                cx.wait(nc.tensor, d_tok)
                ov = out_dram.rearrange("(tt p) d -> p tt d", p=128)
                for t4 in range(T // 512):
                    s = u % 2
                    u += 1
                    cx.wait(nc.tensor, pT_free[s])
                    for i4 in range(4):
                        tt = t4 * 4 + i4
                        ins = nc.tensor.transpose(pTs[s][:, i4, :], xbs[b][:, tt * 128:(tt + 1) * 128], C["ident_f"][:])
                    p_tok = cx.pe.mark(ins)
                    x_free[b][1] = p_tok
                    cx.wait(nc.scalar, p_tok)
                    qs = qbs[s][:, 0:512].rearrange("p (a d) -> p a d", a=4)
                    cx.wait(nc.scalar, q_free[s])
                    a_tok = cx.act.mark(nc.scalar.copy(out=qs, in_=pTs[s][:]))
                    pT_free[s] = a_tok
                    cx.wait(nc.scalar, a_tok)
                    q_free[s] = st[s].mark(nc.scalar.dma_start(out=ov[:, t4 * 4:(t4 + 1) * 4, c * 128:(c + 1) * 128], in_=qs))
        if final:
            barrier(cx)
            return
        barrier(cx)
        rb_tok = rstd_from_stats(cx, rsb[:], st2_ps[:, 0:T], D, eps, None)
        x_free = [None] * NBF
        for c in range(DC):
            b = c % NBF
            cx.wait(nc.sync, x_free[b])
            ltok = ld[b].mark(nc.sync.dma_start(out=xbs[b][:], in_=xT_v[c]))
            cx.wait(nc.vector, ltok)
            x_free[b] = cx.dve.mark(nc.vector.scalar_tensor_tensor(out=act_out[:, c, :], in0=xbs[b][:], scalar=gpre_sb[:, c:c + 1],
                                                                   in1=rsb[:], op0=ALU.mult, op1=ALU.mult))
        barrier(cx)


def xattn_kv_phase(cx, DC, NM, mem_dram, gmem_sb, wk, wv, kT_d, vtok_d, C, eps, wbufs, wsems, wstate):
    nc = cx.nc
    D = DC * 128
    NMC = NM // 128
    with A(nc) as al:
        mT, kT, vtok = al.sb("xa_mT", [128, DC, NM], BF16), al.sb("xa_kT", [128, DC, NM], BF16), al.sb("xa_vtok", [128, NMC, D], BF16)
        norm_in_tokmajor(cx, mem_dram, NM, DC, gmem_sb, mT[:], None, C, eps)
        with A(nc) as a2:
            vTs = [a2.sb("xa_vT0", [128, NM], BF16), a2.sb("xa_vT1", [128, NM], BF16)]
            pk0, pv0 = a2.ps("xa_pk0", [128, NM]), a2.ps("xa_pv0", [128, NM])
            pk1, pv1 = a2.ps("xa_pk1", [128, NM]), a2.ps("xa_pv1", [128, NM])
            pts = [a2.ps("xa_pt0", [128, NMC, 128], BF16), a2.ps("xa_pt1", [128, NMC, 128], BF16)]
            vT_free = [None, None]
            pt_free = [None, None]
            pend = []

            def do_tr(j, s, a_tok):
                cx.wait(nc.tensor, a_tok)
                cx.wait(nc.tensor, pt_free[s])
                for mc in range(NMC):
                    ins = nc.tensor.transpose(pts[s][:, mc, :], vTs[s][:, mc * 128:(mc + 1) * 128], C["ident_b"][:])
                p_tok = cx.pe.mark(ins)
                vT_free[s] = p_tok
                cx.wait(nc.vector, p_tok)
                pt_free[s] = cx.dve.mark(nc.vector.tensor_copy(out=vtok[:, :, j * 128:(j + 1) * 128], in_=pts[s][:]))

            def epi_kv(j, pss, pe_tok):
                s = j % 2
                if pend:
                    do_tr(*pend.pop())
                cx.wait(nc.scalar, pe_tok)
                nc.scalar.copy(out=kT[:, j, :], in_=pss[0][:])
                cx.wait(nc.scalar, vT_free[s])
                a_tok = cx.act.mark(nc.scalar.copy(out=vTs[s][:], in_=pss[1][:]))
                pend.append((j, s, a_tok))
                return a_tok

            ngt = max(1, min(DC, (16384 // (2 * DC)) // 128))
            gemm(cx, mT[:], DC, NM, [wk, wv], DC, ngt, wbufs, wsems, [[pk0[:], pv0[:]], [pk1[:], pv1[:]]], epi_kv, wstate=wstate)
            do_tr(*pend.pop())
            barrier(cx)
        st = dma_ev(cx, "xakvst")
        st.mark(nc.sync.dma_start(out=kT_d, in_=kT[:]))
        st.mark(nc.sync.dma_start(out=vtok_d, in_=vtok[:]))
        barrier(cx)


def xattn_phase(cx, actbuf, T, DC, H, NM, kT_d, vtok_d, wq, qT_dram, C, wbufs, wsems, wstate):
    nc = cx.nc
    D = DC * 128
    HD = D // H
    HDC = HD // 128
    NMC = NM // 128
    act = actbuf[:, 0:DC * T].rearrange("p (k t) -> p k t", k=DC)
    scale = float(HD) ** -0.5
    with A(nc) as al:
        kT, vtok = al.sb("xa_kT", [128, DC, NM], BF16), al.sb("xa_vtok", [128, NMC, D], BF16)
        ldkv = dma_ev(cx, "xakvld")
        ldkv.mark(nc.sync.dma_start(out=kT[:], in_=kT_d))
        ldkv.mark(nc.sync.dma_start(out=vtok[:], in_=vtok_d))
        with (nc.sbuf_tensor(un("xa_q0"), [128, T], BF16) as q0, nc.sbuf_tensor(un("xa_q1"), [128, T], BF16) as q1,
              nc.psum_tensor(un("xa_pq0"), [128, T], F32) as pq0, nc.psum_tensor(un("xa_pq1"), [128, T], F32) as pq1):
            qs = [q0, q1]
            st = [dma_ev(cx, "xaq0"), dma_ev(cx, "xaq1")]
            q_free = [None, None]

            def epi_q(j, pss, pe_tok):
                s = j % 2
                cx.wait(nc.scalar, pe_tok)
                cx.wait(nc.scalar, q_free[s])
                a_tok = cx.act.mark(nc.scalar.copy(out=qs[s][:], in_=pss[0][:]))
                cx.wait(nc.sync, a_tok)
                q_free[s] = st[s].mark(nc.sync.dma_start(out=qT_dram[j * 128:(j + 1) * 128, :], in_=qs[s][:]))
                return a_tok

            ngt = max(1, min(DC, (16384 // DC) // 128))
            gemm(cx, act, DC, T, [wq], DC, ngt, wbufs, wsems, [[pq0[:]], [pq1[:]]], epi_q, wstate=wstate)
            barrier(cx)
        NTH = T // 512
        with (nc.sbuf_tensor(un("xa_qh0"), [128, HDC, T], BF16) as qh0,
              nc.sbuf_tensor(un("xa_pT0"), [128, NMC, 512], BF16) as pT0, nc.sbuf_tensor(un("xa_pT1"), [128, NMC, 512], BF16) as pT1,
              nc.sbuf_tensor(un("xa_rd0"), [128, 512], F32) as rd0,
              nc.psum_tensor(un("xa_ps0"), [128, NMC, 512], F32) as ps0, nc.psum_tensor(un("xa_ps1"), [128, NMC, 512], F32) as ps1,
              nc.psum_tensor(un("xa_pd0"), [128, 512], F32) as pd0, nc.psum_tensor(un("xa_pd1"), [128, 512], F32) as pd1,
              nc.psum_tensor(un("xa_po0"), [128, 512], F32) as po0, nc.psum_tensor(un("xa_po1"), [128, 512], F32) as po1):
            qhs, pTs, rds, pss_, pds, pos = [qh0, qh0], [pT0, pT1], [rd0, rd0], [ps0, ps1], [pd0, pd1], [po0, po1]
            ld = [dma_ev(cx, "xaqh0"), dma_ev(cx, "xaqh1")]
            qh_free = [None, None]
            ps_free = [None, None]
            pT_free = [None, None]
            pd_free = [None, None]
            rd_free = [None, None]
            po_free = [None, None]
            it = 0
            oi = 0
            for h in range(H):
                hb = 0
                cx.wait(nc.sync, qh_free[hb])
                l_tok = ld[hb].mark(nc.sync.dma_start(out=qhs[hb][:], in_=qT_dram.rearrange("(c p) t -> p c t", p=128)[:, h * HDC:(h + 1) * HDC, :]))
                cx.wait(nc.tensor, l_tok)
                for th in range(NTH):
                    s = it % 2
                    it += 1
                    tsl = slice(th * 512, (th + 1) * 512)
                    cx.wait(nc.tensor, ps_free[s])
                    for mc in range(NMC):
                        for dc in range(HDC):
                            ins = nc.tensor.matmul(pss_[s][:, mc, :], lhsT=kT[:, h * HDC + dc, mc * 128:(mc + 1) * 128],
                                                   rhs=qhs[hb][:, dc, tsl], start=(dc == 0), stop=(dc == HDC - 1))
                    s_tok = cx.pe.mark(ins)
                    cx.wait(nc.scalar, s_tok)
                    cx.wait(nc.scalar, pT_free[s])
                    e_tok = cx.act.mark(nc.scalar.activation(out=pTs[s][:], in_=pss_[s][:], func=AF.Exp, scale=scale))
                    ps_free[s] = e_tok
                    cx.wait(nc.tensor, e_tok)
                    cx.wait(nc.tensor, pd_free[s])
                    for mc in range(NMC):
                        ins = nc.tensor.matmul(pds[s][:], lhsT=C["ones_b"][:], rhs=pTs[s][:, mc, :], start=(mc == 0), stop=(mc == NMC - 1))
                    d_tok = cx.pe.mark(ins)
                    cx.wait(nc.vector, d_tok)
                    cx.wait(nc.vector, rd_free[s])
                    r_tok = cx.dve.mark(nc.vector.reciprocal(out=rds[s][:], in_=pds[s][:]))
                    pd_free[s] = r_tok
                    cx.wait(nc.vector, r_tok)
                    for dc in range(HDC):
                        o = oi % 2
                        oi += 1
                        cx.wait(nc.tensor, po_free[o])
                        for mc in range(NMC):
                            ins = nc.tensor.matmul(pos[o][:], lhsT=vtok[:, mc, h * HD + dc * 128:h * HD + (dc + 1) * 128],
                                                   rhs=pTs[s][:, mc, :], start=(mc == 0), stop=(mc == NMC - 1))
                        o_tok = cx.pe.mark(ins)
                        cx.wait(nc.vector, o_tok)
                        po_free[o] = cx.dve.mark(nc.vector.tensor_tensor(out=act[:, h * HDC + dc, tsl], in0=pos[o][:], in1=rds[s][:], op=ALU.mult))
                    pT_free[s] = o_tok
                    rd_free[s] = po_free[o]
                qh_free[hb] = s_tok
            barrier(cx)


def proj_raw_stats(cx, actbuf, T, DC, wo, rawT_dram, st_ps, C, wbufs, wsems, wstate):
    act = actbuf[:, 0:DC * T].rearrange("p (k t) -> p k t", k=DC)
    ngt = max(1, min(DC, (16384 // DC) // 128))
    gemm_raw_stats(cx, act, DC, T, 0, wo, DC, rawT_dram, st_ps, C, wbufs, wsems, wstate, ngt)


def inproj_phase(cx, actbuf, hT_dram, S, DC, w_in, NTF, NTR, projF, projR, wbufs, wsems, wstate, halves=None):
    nc = cx.nc
    TT = min(S, 1024)
    ld = dma_ev(cx, "ipld")
    with (nc.sbuf_tensor(un("ip_b0"), [128, TT], BF16) as b0, nc.sbuf_tensor(un("ip_b1"), [128, TT], BF16) as b1,
          nc.sbuf_tensor(un("ip_f0"), [128, TT], F32) as f0, nc.sbuf_tensor(un("ip_f1"), [128, TT], F32) as f1,
          nc.psum_tensor(un("ip_p0"), [128, TT], F32) as p0, nc.psum_tensor(un("ip_p1"), [128, TT], F32) as p1):
        bs, fs = [b0, b1], [f0, f1]
        st = [dma_ev(cx, "ipst0"), dma_ev(cx, "ipst1")]
        free = [None, None]
        for th in range(S // TT):
            act = actbuf[:, 0:DC * TT].rearrange("p (k t) -> p k t", k=DC)
            if halves is not None:
                k0 = 0
                for piece in halves[th]:
                    nk = piece.shape[0] // 128
                    tok = ld.mark(nc.sync.dma_start(out=act[:, k0:k0 + nk, :], in_=piece.rearrange("(k p) t -> p k t", p=128)))
                    k0 += nk
            else:
                src = hT_dram.rearrange("(k p) t -> p k t", p=128)[:, :, th * TT:(th + 1) * TT]
                tok = ld.mark(nc.sync.dma_start(out=act, in_=src))

            def epi(j, pss, pe_tok, th=th):
                s = j % 2
                cx.wait(nc.scalar, pe_tok)
                cx.wait(nc.scalar, free[s])
                if j < NTF:
                    a_tok = cx.act.mark(nc.scalar.copy(out=bs[s][:], in_=pss[0][:]))
                    cx.wait(nc.sync, a_tok)
                    free[s] = st[s].mark(nc.sync.dma_start(out=projF[j * 128:(j + 1) * 128, th * TT:(th + 1) * TT], in_=bs[s][:]))
                else:
                    a_tok = cx.act.mark(nc.scalar.copy(out=fs[s][:], in_=pss[0][:]))
                    cx.wait(nc.sync, a_tok)
                    jj = j - NTF
                    free[s] = st[s].mark(nc.sync.dma_start(out=projR[jj * 128:(jj + 1) * 128, th * TT:(th + 1) * TT], in_=fs[s][:]))
                return a_tok

            ngt = max(1, min(NTF + NTR, (16384 // DC) // 128))
            gemm(cx, act, DC, TT, [w_in], NTF + NTR, ngt, wbufs, wsems, [[p0[:]], [p1[:]]], epi, pre_tok=tok, wstate=wstate)
            barrier(cx)


def fox_phase(cx, S, NFH, projF, fT_dram, fbias_sb, yT_dram, C):
    nc = cx.nc
    FW = NFH * 128
    NB = S // 128
    NP = S // 512
    scale = 128.0 ** -0.5
    NEG = -1.0e30
    with (nc.sbuf_tensor(un("fx_c"), [128, S], F32) as cT, nc.sbuf_tensor(un("fx_ones"), [128, S], F32) as onesS,
          nc.sbuf_tensor(un("fx_sel"), [128, 8, 128], F32) as sel, nc.sbuf_tensor(un("fx_ncs"), [128, NB, 8], F32) as ncs,
          nc.sbuf_tensor(un("fx_mask"), [128, 4, 512], F32) as mask, nc.sbuf_tensor(un("fx_CT"), [128, S], F32) as CT,
          nc.sbuf_tensor(un("fx_nb"), [128, 1], F32) as nbias):
        ld0 = dma_ev(cx, "fxld")
        t0 = ld0.mark(nc.sync.dma_start(out=cT[:], in_=fT_dram))
        d0 = cx.dve.mark(nc.vector.memset(onesS[:], 1.0))
        d0 = cx.dve.mark(nc.vector.tensor_scalar(out=nbias[:], in0=fbias_sb, scalar1=-1.0, scalar2=None, op0=ALU.mult))
        cx.wait(nc.gpsimd, d0)
        cx.pool.mark(nc.gpsimd.affine_select(out=sel[:], in_=onesS[:, 0:1024].rearrange("p (h m) -> p h m", h=8), pattern=[[-1, 8], [0, 128]],
                                             compare_op=ALU.is_equal, fill=0.0, base=0, channel_multiplier=1))
        pm = cx.pool.mark(nc.gpsimd.memset(mask[:], 0.0))
        cx.wait(nc.gpsimd, pm)
        p_tok = cx.pool.mark(nc.gpsimd.affine_select(out=mask[:], in_=mask[:], pattern=[[-128, 4], [1, 512]], compare_op=ALU.is_ge, fill=NEG,
                                                     base=0, channel_multiplier=-1))
        cx.wait(nc.scalar, t0)
        cx.wait(nc.scalar, d0)
        a_tok = cx.act.mark(nc.scalar.activation(out=cT[:], in_=cT[:], func=AF.Exp, scale=-1.0, bias=nbias[:]))
        cx.wait(nc.vector, a_tok)
        d_tok = cx.dve.mark(nc.vector.tensor_scalar(out=cT[:], in0=cT[:], scalar1=1.0, scalar2=None, op0=ALU.add))
        cx.wait(nc.scalar, d_tok)
        a_tok = cx.act.mark(nc.scalar.activation(out=cT[:], in_=cT[:], func=AF.Ln))
        cx.wait(nc.vector, a_tok)
        d_tok = cx.dve.mark(nc.vector.tensor_scalar(out=cT[:], in0=cT[:], scalar1=-1.0, scalar2=None, op0=ALU.mult))
        cx.wait(nc.vector, d_tok)
        d_tok = cx.dve.mark(nc.vector.tensor_tensor_scan(out=cT[:], data0=onesS[:], data1=cT[:], initial=0.0, op0=ALU.mult, op1=ALU.add))
        cx.wait(nc.tensor, d_tok)
        cx.wait(nc.tensor, p_tok)
        with nc.psum_tensor(un("fx_pt"), [128, NB, 8], F32) as ptc:
            for blk in range(NB):
                ins = nc.tensor.transpose(ptc[:, blk, :], cT[0:8, blk * 128:(blk + 1) * 128], C["ident_f"][0:8, 0:8])
            t_tok = cx.pe.mark(ins)
            cx.wait(nc.vector, t_tok)
            cx.dve.mark(nc.vector.tensor_scalar(out=ncs[:], in0=ptc[:], scalar1=-1.0, scalar2=None, op0=ALU.mult))
            barrier(cx)
        with A(nc) as al:
            qh, kh, vh = al.sb("fx_q", [128, S], BF16), al.sb("fx_k", [128, S], BF16), al.sb("fx_v", [128, S], BF16)
            vtok = al.sb("fx_vt", [128, NB, 128], BF16)
            NBF_ = 4
            DEFER = 2
            tms = [al.sb(f"fx_t{i}", [128, 512], F32) for i in range(NBF_)]
            pps = [al.sb(f"fx_p{i}", [128, 512], BF16) for i in range(NBF_)]
            rd = al.sb("fx_rd", [128, 512], F32)
            y0, y1 = al.sb("fx_y0", [128, 512], BF16), al.sb("fx_y1", [128, 512], BF16)
            pss_ = [al.ps(f"fx_ps{i}", [128, 512]) for i in range(NBF_)]
            pd, po, pc = al.ps("fx_pd", [128, 512]), al.ps("fx_po", [128, 512]), al.ps("fx_pc", [128, 512])
            pv = al.ps("fx_pv", [128, 4, 128], BF16)
            ys = [y0, y1]
            ldh = dma_ev(cx, "fxldh")
            sty = [dma_ev(cx, "fxsty0"), dma_ev(cx, "fxsty1")]
            it = 0
            yi = 0
            ps_free = [None] * NBF_
            tm_free = [None] * NBF_
            pp_free = [None] * NBF_
            y_free = [None, None]
            acc_free = None
            rd_free = None
            for h in range(NFH):
                barrier(cx)
                ldh.mark(nc.sync.dma_start(out=qh[:], in_=projF[h * 128:(h + 1) * 128, :]))
                ldh.mark(nc.sync.dma_start(out=kh[:], in_=projF[FW + h * 128:FW + (h + 1) * 128, :]))
                l_tok = ldh.mark(nc.sync.dma_start(out=vh[:], in_=projF[2 * FW + h * 128:2 * FW + (h + 1) * 128, :]))
                cx.wait(nc.tensor, l_tok)
                for p4 in range(NP):
                    ins = nc.tensor.matmul(pc[:], lhsT=sel[0:8, h, :], rhs=cT[0:8, p4 * 512:(p4 + 1) * 512], start=True, stop=True)
                    c_tok = cx.pe.mark(ins)
                    cx.wait(nc.scalar, c_tok)
                    a_tok = cx.act.mark(nc.scalar.copy(out=CT[:, p4 * 512:(p4 + 1) * 512], in_=pc[:]))
                    cx.wait(nc.tensor, a_tok)
                for b4 in range(NB // 4):
                    for i4 in range(4):
                        blk = b4 * 4 + i4
                        ins = nc.tensor.transpose(pv[:, i4, :], vh[:, blk * 128:(blk + 1) * 128], C["ident_b"][:])
                    v_tok = cx.pe.mark(ins)
                    cx.wait(nc.scalar, v_tok)
                    a_tok = cx.act.mark(nc.scalar.copy(out=vtok[:, b4 * 4:(b4 + 1) * 4, :], in_=pv[:]))
                    cx.wait(nc.tensor, a_tok)
                cx.wait(nc.vector, a_tok)
                for qp in range(NP):
                    qsl = slice(qp * 512, (qp + 1) * 512)
                    nkb = qp * 4 + 4
                    cx.wait(nc.tensor, acc_free)
                    pendq = []

                    def do_pv(last):
                        pe_, ps_, pkb = pendq.pop(0)
                        cx.wait(nc.tensor, pe_)
                        nc.tensor.matmul(pd[:], lhsT=C["ones_b"][:], rhs=pps[ps_][:], start=(pkb == 0), stop=last)
                        tk = cx.pe.mark(nc.tensor.matmul(po[:], lhsT=vtok[:, pkb, :], rhs=pps[ps_][:], start=(pkb == 0), stop=last))
                        pp_free[ps_] = tk
                        return tk

                    for kb in range(nkb):
                        s = it % NBF_
                        it += 1
                        cx.wait(nc.tensor, ps_free[s])
                        s_tok = cx.pe.mark(nc.tensor.matmul(pss_[s][:], lhsT=kh[:, kb * 128:(kb + 1) * 128], rhs=qh[:, qsl], start=True, stop=True))
                        if len(pendq) >= DEFER:
                            do_pv(False)
                        cx.wait(nc.vector, s_tok)
                        cx.wait(nc.vector, tm_free[s])
                        d_tok = cx.dve.mark(nc.vector.scalar_tensor_tensor(out=tms[s][:], in0=pss_[s][:], scalar=scale, in1=CT[:, qsl], op0=ALU.mult, op1=ALU.add))
                        ps_free[s] = d_tok
                        off = kb - qp * 4
                        if off >= 0:
                            cx.wait(nc.vector, d_tok)
                            d_tok = cx.dve.mark(nc.vector.tensor_tensor(out=tms[s][:], in0=tms[s][:], in1=mask[:, off, :], op=ALU.add))
                        cx.wait(nc.scalar, d_tok)
                        cx.wait(nc.scalar, pp_free[s])
                        e_tok = cx.act.mark(nc.scalar.activation(out=pps[s][:], in_=tms[s][:], func=AF.Exp, bias=ncs[:, kb, h:h + 1]))
                        tm_free[s] = e_tok
                        pendq.append((e_tok, s, kb))
                    while pendq:
                        o_tok = do_pv(len(pendq) == 1)
                    cx.wait(nc.vector, o_tok)
                    cx.wait(nc.vector, rd_free)
                    r_tok = cx.dve.mark(nc.vector.reciprocal(out=rd[:], in_=pd[:]))
                    cx.wait(nc.vector, r_tok)
                    ys_ = yi % 2
                    yi += 1
                    cx.wait(nc.vector, y_free[ys_])
                    y_tok = cx.dve.mark(nc.vector.tensor_tensor(out=ys[ys_][:], in0=po[:], in1=rd[:], op=ALU.mult))
                    acc_free = y_tok
                    rd_free = y_tok
                    cx.wait(nc.sync, y_tok)
                    y_free[ys_] = sty[ys_].mark(nc.sync.dma_start(out=yT_dram[h * 128:(h + 1) * 128, qsl], in_=ys[ys_][:]))
            barrier(cx)


class Ser:
    def __init__(self, cx, tok=None):
        self.cx = cx
        self.last = tok

    def run(self, eng, fn, extra=()):
        cx = self.cx
        cx.wait(eng, self.last)
        for t in extra:
            cx.wait(eng, t)
        self.last = cx.evof(eng).mark(fn())
        return self.last


class Dep:
    def __init__(self, cx):
        self.cx = cx
        self.w = {}
        self.r = {}

    def _pre(self, eng, reads, writes, extra):
        cx = self.cx
        for k in reads:
            cx.wait(eng, self.w.get(k))
        for k in writes:
            cx.wait(eng, self.w.get(k))
            for t in self.r.get(k, ()):
                cx.wait(eng, t)
        for t in extra:
            cx.wait(eng, t)

    def _post(self, tok, reads, writes):
        for k in reads:
            self.r.setdefault(k, []).append(tok)
        for k in writes:
            self.w[k] = tok
            self.r[k] = []
        return tok

    def run(self, eng, fn, reads=(), writes=(), extra=()):
        self._pre(eng, reads, writes, extra)
        return self._post(self.cx.evof(eng).mark(fn()), reads, writes)

    def dma(self, eng, ev, fn, reads=(), writes=(), extra=()):
        self._pre(eng, reads, writes, extra)
        return self._post(ev.mark(fn()), reads, writes)

    def all_done(self, engines):
        cx = self.cx
        for e in engines:
            for ev in (cx.pe, cx.act, cx.dve, cx.pool):
                cx.wait(e, (ev, ev.n))


NVT = 10


def rwkv_phase(cx, S, NRT, projR, rvec_dram, wup_dram, aup_dram, gup_dram, yT_dram, y_row0, C, gn_eps, after_round=None):
    nc = cx.nc
    RW = NRT * 128
    NCH = S // 64
    NP = S // 512
    V, ACT_, PE, POOL = nc.vector, nc.scalar, nc.tensor, nc.gpsimd
    with A(nc) as al:
        rvec = al.sb("rk_vec", [128, NRT * NVT + 4], F32)
        omka = al.sb("rk_omka", [128, NRT], F32)
        wup, aup = al.sb("rk_wup", [128, RW], BF16), al.sb("rk_aup", [128, RW], BF16)
        gup = al.sb("rk_gup", [128, 2, RW], BF16)
        lxw, lxa, lxg = al.sb("rk_lxw", [128, S], BF16), al.sb("rk_lxa", [128, S], BF16), al.sb("rk_lxg", [128, 2, S], BF16)
        seg = al.sb("rk_seg", [128, S], BF16)
        bones = al.sb("rk_bones", [128, 128], F32)
        idrep = al.sb("rk_idrep", [128, 8, 64], F32)
        m1, m2 = al.sb("rk_m1", [128, 4, 128], F32), al.sb("rk_m2", [128, 4, 128], F32)
        m3 = al.sb("rk_m3", [128, 4, 64], F32)
        Fb = [al.sb(f"rk_F{i}", [128, S], F32) for i in range(8)]
        VB, KT, BT = al.sb("rk_VB", [128, S], BF16), al.sb("rk_KT", [128, S], BF16), al.sb("rk_BT", [128, S], BF16)
        KH, NBH = al.sb("rk_KH", [128, S], BF16), al.sb("rk_NBH", [128, S], BF16)
        KR = al.sb("rk_KR", [128, NCH, 2, 64], BF16)
        Vtok, Khat, nBhat = al.sb("rk_Vt", [128, NCH, 64], BF16), al.sb("rk_Kh", [128, NCH, 64], BF16), al.sb("rk_nBh", [128, NCH, 64], BF16)
        NTB, AB2 = al.sb("rk_NTB", [128, NCH, 128], BF16), al.sb("rk_AB2", [128, NCH, 128], BF16)
        Nb = [al.sb("rk_Na", [128, NCH, 64], BF16), al.sb("rk_Nb", [128, NCH, 64], BF16)]
        NTb = [al.sb("rk_NTa", [128, NCH, 64], BF16), al.sb("rk_NTb", [128, NCH, 64], BF16)]
        TTb = [al.sb("rk_TTa", [128, NCH, 64], BF16), al.sb("rk_TTb", [128, NCH, 64], BF16)]
        PC, nPC = al.sb("rk_PC", [128, NCH], F32), al.sb("rk_nPC", [128, NCH], F32)
        M32, Mb = al.sb("rk_M32", [128, 64], F32), al.sb("rk_Mb", [128, 64], BF16)
        Xs, Us = al.sb("rk_Xs", [128, 64], BF16), al.sb("rk_Us", [128, 64], BF16)
        ysb = al.sb("rk_ysb", [128, S], BF16)
        ld = dma_ev(cx, "rkld")
        ld3 = [dma_ev(cx, "rkld3a"), dma_ev(cx, "rkld3b"), dma_ev(cx, "rkld3c")]
        ldw = dma_ev(cx, "rkldw")
        sty = dma_ev(cx, "rksty")
        ser = Ser(cx)
        t_vec = ld.mark(nc.sync.dma_start(out=rvec[:], in_=rvec_dram))
        ldw.mark(nc.gpsimd.dma_start(out=wup[:], in_=wup_dram))
        ldw.mark(nc.gpsimd.dma_start(out=aup[:], in_=aup_dram))
        t_w = ldw.mark(nc.gpsimd.dma_start(out=gup[:], in_=gup_dram.rearrange("(c p) n -> p c n", p=128)))
        ser.run(V, lambda: V.memset(seg[:], 1.0))
        ser.run(V, lambda: V.memset(seg[:].rearrange("p (c j) -> p c j", j=64)[:, :, 0:1], 0.0))
        ser.run(V, lambda: V.memset(bones[:], 0.0))
        ser.run(V, lambda: V.memset(bones[0:64, 0:64], 1.0))
        ser.run(V, lambda: V.memset(bones[64:128, 64:128], 1.0))
        ser.run(V, lambda: V.memset(idrep[:], 1.0))
        ser.run(V, lambda: V.memset(m1[:], -1.0))
        ser.run(V, lambda: V.memset(m2[:], 1.0))
        ser.run(V, lambda: V.memset(m3[:], -1.0))
        for hs in range(2):
            hp = slice(hs * 64, hs * 64 + 64)
            ser.run(POOL, lambda: POOL.affine_select(out=idrep[hp], in_=idrep[hp], pattern=[[0, 8], [-1, 64]], compare_op=ALU.is_equal,
                                                      fill=0.0, base=0, channel_multiplier=1))
            for mm_ in (m1, m2):
                ser.run(POOL, lambda: POOL.affine_select(out=mm_[hp, :, 0:64], in_=mm_[hp, :, 0:64], pattern=[[0, 4], [1, 64]],
                                                          compare_op=ALU.is_gt, fill=0.0, base=0, channel_multiplier=-1))
                ser.run(POOL, lambda: POOL.affine_select(out=mm_[hp, :, 64:128], in_=mm_[hp, :, 64:128], pattern=[[0, 4], [1, 64]],
                                                          compare_op=ALU.is_ge, fill=0.0, base=0, channel_multiplier=-1))
            ser.run(POOL, lambda: POOL.affine_select(out=m3[hp], in_=m3[hp], pattern=[[0, 4], [-1, 64]], compare_op=ALU.is_gt,
                                                      fill=0.0, base=0, channel_multiplier=1))
        GV = NRT * NVT
        ser.run(V, lambda: V.tensor_scalar(out=omka[:], in0=rvec[:, 0:GV].rearrange("p (t v) -> p t v", v=NVT)[:, :, 6], scalar1=-1.0, scalar2=1.0,
                                           op0=ALU.mult, op1=ALU.add), extra=[t_vec])

        def tshift(src_rows, mu_ap, Z, T0, T1):
            t = ld.mark(nc.sync.dma_start(out=T0[:], in_=projR[src_rows, :]))
            ser.run(V, lambda: V.tensor_tensor(out=T1[:, 1:S], in0=T0[:, 0:S - 1], in1=T0[:, 1:S], op=ALU.subtract), extra=[t])
            ser.run(V, lambda: V.tensor_scalar(out=T1[:, 0:1], in0=T0[:, 0:1], scalar1=-1.0, scalar2=None, op0=ALU.mult))
            ser.run(V, lambda: V.scalar_tensor_tensor(out=Z, in0=T1[:], scalar=mu_ap, in1=T0[:], op0=ALU.mult, op1=ALU.add))

        r0 = 3 * RW
        cx.wait(nc.sync, ser.last)
        tshift(slice(r0, r0 + 128), rvec[:, GV + 0:GV + 1], Fb[2][:], Fb[0], Fb[1])
        ser.run(ACT_, lambda: ACT_.activation(out=lxw[:], in_=Fb[2][:], func=AF.Tanh))
        cx.wait(nc.sync, ser.last)
        tshift(slice(r0 + 128, r0 + 256), rvec[:, GV + 1:GV + 2], Fb[2][:], Fb[0], Fb[1])
        ser.run(ACT_, lambda: ACT_.copy(out=lxa[:], in_=Fb[2][:]))
        for c2 in range(2):
            cx.wait(nc.sync, ser.last)
            tshift(slice(r0 + 256 + c2 * 128, r0 + 384 + c2 * 128), rvec[:, GV + 2 + c2:GV + 3 + c2], Fb[2][:], Fb[0], Fb[1])
            ser.run(ACT_, lambda: ACT_.activation(out=lxg[:, c2, :], in_=Fb[2][:], func=AF.Sigmoid))
        cx.wait(PE, t_w)
        for e_ in (PE, ACT_, V, nc.sync, POOL):
            for ev_ in (cx.pe, cx.act, cx.dve, cx.pool):
                cx.wait(e_, (ev_, ev_.n))
        for ti in range(NRT):
            vc = lambda i: rvec[:, ti * NVT + i:ti * NVT + i + 1]
            csl = slice(ti * 128, (ti + 1) * 128)
            F0, F1, F2, F3, F4, F5, F6, F7 = Fb
            dp = Dep(cx)
            NPc = NP

            def KK(name, p4=None):
                return [(name, p4)] if p4 is not None else [(name, i) for i in range(NPc)]

            def shift2(rows, mu_ap, Zn, T0n):
                Z, T0, T1 = Fmap[Zn], Fmap[T0n], Fmap["F2"]
                ldx = ld3[{"F0": 0, "F5": 1, "F6": 2}[T0n]]
                dp.dma(nc.sync, ldx, lambda: nc.sync.dma_start(out=T0[:], in_=projR[rows, :]), writes=KK(T0n))
                dp.run(V, lambda: V.tensor_tensor(out=T1[:, 1:S], in0=T0[:, 0:S - 1], in1=T0[:, 1:S], op=ALU.subtract), reads=KK(T0n), writes=KK("F2"))
                dp.run(V, lambda: V.tensor_scalar(out=T1[:, 0:1], in0=T0[:, 0:1], scalar1=-1.0, scalar2=None, op0=ALU.mult), reads=KK(T0n), writes=KK("F2"))
                dp.run(V, lambda: V.scalar_tensor_tensor(out=Z[:], in0=T1[:], scalar=mu_ap, in1=T0[:], op0=ALU.mult, op1=ALU.add),
                       reads=KK("F2") + KK(T0n), writes=KK(Zn))

            Fmap = {f"F{i}": Fb[i] for i in range(8)}
            shift2(slice(ti * 128, ti * 128 + 128), vc(0), "F1", "F0")
            shift2(slice(RW + ti * 128, RW + ti * 128 + 128), vc(1), "F3", "F5")
            shift2(slice(2 * RW + ti * 128, 2 * RW + ti * 128 + 128), vc(2), "F4", "F6")
            with A(nc) as pl:
                pAs = [pl.ps("rk_pA0", [128, 512]), pl.ps("rk_pA1", [128, 512])]
                pi = [0]

                def nextp():
                    pi[0] += 1
                    i = pi[0] % 2
                    return pAs[i], [("pA", i)]

                for p4 in range(NP):
                    psl = slice(p4 * 512, (p4 + 1) * 512)
                    pA, pk = nextp()
                    dp.run(PE, lambda: PE.matmul(pA[:], lhsT=wup[:, csl], rhs=lxw[:, psl], start=True, stop=True), writes=pk)
                    dp.run(ACT_, lambda: ACT_.activation(out=F5[:, psl], in_=pA[:], func=AF.Sigmoid, bias=vc(3)), reads=pk, writes=KK("F5", p4))
                    pA, pk = nextp()
                    dp.run(PE, lambda: PE.matmul(pA[:], lhsT=aup[:, csl], rhs=lxa[:, psl], start=True, stop=True), writes=pk)
                    dp.run(ACT_, lambda: ACT_.activation(out=F6[:, psl], in_=pA[:], func=AF.Sigmoid, bias=vc(4)), reads=pk, writes=KK("F6", p4))
                    pA, pk = nextp()
                    dp.run(PE, lambda: PE.matmul(pA[:], lhsT=gup[:, 0, csl], rhs=lxg[:, 0, psl], start=True, stop=False), writes=pk)
                    dp.run(PE, lambda: PE.matmul(pA[:], lhsT=gup[:, 1, csl], rhs=lxg[:, 1, psl], start=False, stop=True), writes=pk)
                    dp.run(ACT_, lambda: ACT_.copy(out=F7[:, psl], in_=pA[:]), reads=pk, writes=KK("F7", p4))
                dp.run(V, lambda: V.tensor_scalar(out=F5[:], in0=F5[:], scalar1=-float(np.exp(-0.5)), scalar2=None, op0=ALU.mult),
                       reads=KK("F5"), writes=KK("F5"))
                dp.run(V, lambda: V.tensor_scalar(out=F0[:], in0=F3[:], scalar1=vc(5), scalar2=None, op0=ALU.mult), reads=KK("F3"), writes=KK("F0"))
                dp.run(ACT_, lambda: ACT_.activation(out=F2[:], in_=F0[:], func=AF.Square), reads=KK("F0"), writes=KK("F2"))
                for p4 in range(NP):
                    psl = slice(p4 * 512, (p4 + 1) * 512)
                    pA, pk = nextp()
                    dp.run(PE, lambda: PE.matmul(pA[:], lhsT=bones[:], rhs=F2[:, psl], start=True, stop=True), reads=KK("F2", p4), writes=pk)
                    dp.run(V, lambda: V.tensor_scalar(out=F2[:, psl], in0=pA[:], scalar1=1e-24, scalar2=None, op0=ALU.max), reads=pk, writes=KK("F2", p4))
                dp.run(ACT_, lambda: ACT_.activation(out=F2[:], in_=F2[:], func=AF.Ln), reads=KK("F2"), writes=KK("F2"))
                dp.run(ACT_, lambda: ACT_.activation(out=F2[:], in_=F2[:], func=AF.Exp, scale=-0.5), reads=KK("F2"), writes=KK("F2"))
                dp.run(V, lambda: V.tensor_tensor(out=F0[:], in0=F0[:], in1=F2[:], op=ALU.mult), reads=KK("F0") + KK("F2"), writes=KK("F0"))
                dp.run(V, lambda: V.tensor_scalar(out=F2[:], in0=F6[:], scalar1=vc(6), scalar2=omka[:, ti:ti + 1], op0=ALU.mult, op1=ALU.add),
                       reads=KK("F6"), writes=KK("F2"))
                dp.run(V, lambda: V.tensor_tensor(out=F3[:], in0=F3[:], in1=F2[:], op=ALU.mult), reads=KK("F3") + KK("F2"), writes=KK("F3"))
                dp.run(V, lambda: V.scalar_tensor_tensor(out=F2[:], in0=F1[:], scalar=vc(7), in1=F3[:], op0=ALU.mult, op1=ALU.mult),
                       reads=KK("F1") + KK("F3"), writes=KK("F2"))
                for p4 in range(NP):
                    psl = slice(p4 * 512, (p4 + 1) * 512)
                    pA, pk = nextp()
                    dp.run(PE, lambda: PE.matmul(pA[:], lhsT=bones[:], rhs=F2[:, psl], start=True, stop=True), reads=KK("F2", p4), writes=pk)
                    dp.run(V, lambda: V.tensor_tensor(out=F2[:, psl], in0=pA[:], in1=F4[:, psl], op=ALU.mult), reads=pk + KK("F4", p4), writes=KK("F2", p4))
                dp.run(V, lambda: V.tensor_tensor(out=F6[:], in0=F0[:], in1=F6[:], op=ALU.mult), reads=KK("F0") + KK("F6"), writes=KK("F6"))
                dp.run(ACT_, lambda: ACT_.copy(out=VB[:], in_=F4[:]), reads=KK("F4"), writes=[("VB", 0)])
                dp.run(V, lambda: V.tensor_tensor_scan(out=F4[:], data0=seg[:], data1=F5[:], initial=0.0, op0=ALU.mult, op1=ALU.add),
                       reads=KK("F5"), writes=KK("F4"))
                clv = F4[:].rearrange("p (c j) -> p c j", j=64)
                dp.run(ACT_, lambda: ACT_.activation(out=PC[:], in_=clv[:, :, 63], func=AF.Exp), reads=KK("F4"), writes=[("PC", 0)])
                dp.run(V, lambda: V.tensor_scalar(out=nPC[:], in0=PC[:], scalar1=-1.0, scalar2=None, op0=ALU.mult), reads=[("PC", 0)], writes=[("nPC", 0)])
                dp.run(V, lambda: V.tensor_tensor(out=F5[:], in0=F4[:], in1=F5[:], op=ALU.subtract), reads=KK("F4") + KK("F5"), writes=KK("F5"))
                dp.run(ACT_, lambda: ACT_.activation(out=F5[:], in_=F5[:], func=AF.Exp), reads=KK("F5"), writes=KK("F5"))
                dp.run(V, lambda: V.tensor_tensor(out=KR[:, :, 0, :], in0=F0[:].rearrange("p (c j) -> p c j", j=64),
                                                  in1=F5[:].rearrange("p (c j) -> p c j", j=64), op=ALU.mult), reads=KK("F0") + KK("F5"), writes=[("KR", 0)])
                dp.run(ACT_, lambda: ACT_.activation(out=F5[:], in_=F4[:], func=AF.Exp), reads=KK("F4"), writes=KK("F5"))
                dp.run(V, lambda: V.tensor_tensor(out=KR[:, :, 1, :], in0=F1[:].rearrange("p (c j) -> p c j", j=64),
                                                  in1=F5[:].rearrange("p (c j) -> p c j", j=64), op=ALU.mult), reads=KK("F1") + KK("F5"), writes=[("KR", 1)])
                dp.run(ACT_, lambda: ACT_.activation(out=F5[:], in_=F4[:], func=AF.Exp, scale=-1.0), reads=KK("F4"), writes=KK("F5"))
                dp.run(V, lambda: V.tensor_tensor(out=F3[:], in0=F3[:], in1=F5[:], op=ALU.mult), reads=KK("F3") + KK("F5"), writes=KK("F3"))
                dp.run(V, lambda: V.tensor_tensor(out=F6[:], in0=F6[:], in1=F5[:], op=ALU.mult), reads=KK("F6") + KK("F5"), writes=KK("F6"))
                dp.run(ACT_, lambda: ACT_.copy(out=KT[:], in_=F3[:]), reads=KK("F3"), writes=[("KT", 0)])
                dp.run(ACT_, lambda: ACT_.copy(out=BT[:], in_=F6[:]), reads=KK("F6"), writes=[("BT", 0)])
                dp.run(V, lambda: V.tensor_tensor(out=KH[:].rearrange("p (c j) -> p c j", j=64), in0=F3[:].rearrange("p (c j) -> p c j", j=64),
                                                  in1=PC[:].unsqueeze(2).to_broadcast([128, NCH, 64]), op=ALU.mult), reads=KK("F3") + [("PC", 0)], writes=[("KH", 0)])
                dp.run(V, lambda: V.tensor_tensor(out=NBH[:].rearrange("p (c j) -> p c j", j=64), in0=F6[:].rearrange("p (c j) -> p c j", j=64),
                                                  in1=nPC[:].unsqueeze(2).to_broadcast([128, NCH, 64]), op=ALU.mult), reads=KK("F6") + [("nPC", 0)], writes=[("NBH", 0)])
                dp.all_done([PE, ACT_, V])
            ser = Ser(cx, (cx.dve, cx.dve.n))
            prep_tok = ser.last
            YT = F1
            with A(nc) as pl:
                pT = [pl.ps("rk_pT0", [128, 8, 64]), pl.ps("rk_pT1", [128, 8, 64])]
                pT_free = [None, None]
                cx.wait(PE, prep_tok)
                u = 0
                for (src, dst) in ((VB, Vtok), (KH, Khat), (NBH, nBhat)):
                    for g8 in range(NCH // 8):
                        s = u % 2
                        u += 1
                        cx.wait(PE, pT_free[s])
                        for c8 in range(8):
                            c = g8 * 8 + c8
                            for hs in range(2):
                                hp = slice(hs * 64, hs * 64 + 64)
                                ins = PE.matmul(pT[s][hp, c8, :], lhsT=src[hp, c * 64:(c + 1) * 64], rhs=C["ident_b"][hp, hs * 64:hs * 64 + 64],
                                                start=True, stop=True)
                        p_tok = cx.pe.mark(ins)
                        cx.wait(ACT_, p_tok)
                        pT_free[s] = cx.act.mark(ACT_.copy(out=dst[:, g8 * 8:(g8 + 1) * 8, :], in_=pT[s][:]))
                tm_tok = pT_free[(u - 1) % 2]
                tm_tok2 = pT_free[u % 2]
            with A(nc) as pl:
                p1 = [pl.ps("rk_p1a", [128, 4, 128]), pl.ps("rk_p1b", [128, 4, 128])]
                p2 = [pl.ps("rk_p2a", [128, 4, 128]), pl.ps("rk_p2b", [128, 4, 128])]
                p3 = [pl.ps("rk_p3a", [128, 4, 64]), pl.ps("rk_p3b", [128, 4, 64])]
                pfree = [None, None]
                psum_fence(cx)
                cx.wait(V, tm_tok)
                cx.wait(V, tm_tok2)
                for g4 in range(NCH // 4):
                    s = g4 % 2
                    cx.wait(PE, pfree[s])
                    for c4 in range(4):
                        c = g4 * 4 + c4
                        tsl = slice(c * 64, (c + 1) * 64)
                        for hs in range(2):
                            hp = slice(hs * 64, hs * 64 + 64)
                            PE.matmul(p1[s][hp, c4, :], lhsT=BT[hp, tsl], rhs=KR[hp, c, :, :], start=True, stop=True)
                            PE.matmul(p2[s][hp, c4, :], lhsT=KT[hp, tsl], rhs=KR[hp, c, :, :], start=True, stop=True)
                            ins = PE.matmul(p3[s][hp, c4, :], lhsT=KR[hp, c, 0, :], rhs=BT[hp, tsl], start=True, stop=True)
                    p_tok = cx.pe.mark(ins)
                    cx.wait(V, p_tok)
                    gsl = slice(g4 * 4, g4 * 4 + 4)
                    V.tensor_tensor(out=NTB[:, gsl, :], in0=p1[s][:], in1=m1[:], op=ALU.mult)
                    V.tensor_tensor(out=AB2[:, gsl, :], in0=p2[s][:], in1=m2[:], op=ALU.mult)
                    pfree[s] = cx.dve.mark(V.tensor_tensor(out=Nb[0][:, gsl, :], in0=p3[s][:], in1=m3[:], op=ALU.mult))
                ab_tok = pfree[(NCH // 4 - 1) % 2]
            NG8 = NCH // 8
            with A(nc) as pl:
                pN = [pl.ps("rk_pNa", [128, 8, 64]), pl.ps("rk_pNb", [128, 8, 64])]
                pNT = [pl.ps("rk_pNTa", [128, 8, 64]), pl.ps("rk_pNTb", [128, 8, 64])]
                pTT = [pl.ps("rk_pTa", [128, 8, 64]), pl.ps("rk_pTb", [128, 8, 64])]
                psum_fence(cx)
                cx.wait(V, ab_tok)
                for g8 in range(NG8):
                    gsl = slice(g8 * 8, g8 * 8 + 8)
                    V.tensor_copy(out=NTb[0][:, gsl, :], in_=NTB[:, gsl, 0:64])
                    t0_tok = cx.dve.mark(V.tensor_tensor(out=TTb[0][:, gsl, :], in0=NTB[:, gsl, 0:64], in1=idrep[:], op=ALU.add))
                cur = 0
                lvl_tok = t0_tok
                pN_free, pNT_free, pTT_free = [None, None], [None, None], [None, None]
                for k in range(5):
                    nxt = 1 - cur
                    cx.wait(PE, lvl_tok)
                    n_toks = []
                    for g8 in range(NG8):
                        s = g8 % 2
                        gsl = slice(g8 * 8, g8 * 8 + 8)
                        cx.wait(PE, pN_free[s])
                        cx.wait(PE, pNT_free[s])
                        for c8 in range(8):
                            c = g8 * 8 + c8
                            for hs in range(2):
                                hp = slice(hs * 64, hs * 64 + 64)
                                ins = PE.matmul(pN[s][hp, c8, :], lhsT=NTb[cur][hp, c, :], rhs=Nb[cur][hp, c, :], start=True, stop=True)
                                if k < 4:
                                    ins = PE.matmul(pNT[s][hp, c8, :], lhsT=Nb[cur][hp, c, :], rhs=NTb[cur][hp, c, :], start=True, stop=True)
                        p_tok = cx.pe.mark(ins)
                        cx.wait(ACT_, p_tok)
                        a_tok = cx.act.mark(ACT_.copy(out=Nb[nxt][:, gsl, :], in_=pN[s][:]))
                        pN_free[s] = a_tok
                        n_toks.append(a_tok)
                        if k < 4:
                            cx.wait(V, p_tok)
                            pNT_free[s] = cx.dve.mark(V.tensor_copy(out=NTb[nxt][:, gsl, :], in_=pNT[s][:]))
                    for g8 in range(NG8):
                        s = g8 % 2
                        gsl = slice(g8 * 8, g8 * 8 + 8)
                        cx.wait(PE, n_toks[g8])
                        cx.wait(PE, pTT_free[s])
                        for c8 in range(8):
                            c = g8 * 8 + c8
                            for hs in range(2):
                                hp = slice(hs * 64, hs * 64 + 64)
                                ins = PE.matmul(pTT[s][hp, c8, :], lhsT=Nb[nxt][hp, c, :], rhs=TTb[cur][hp, c, :], start=True, stop=True)
                        p_tok = cx.pe.mark(ins)
                        cx.wait(V, p_tok)
                        pTT_free[s] = cx.dve.mark(V.tensor_tensor(out=TTb[nxt][:, gsl, :], in0=pTT[s][:], in1=TTb[cur][:, gsl, :], op=ALU.add))
                    lvl_tok = pTT_free[(NG8 - 1) % 2]
                    cur = nxt
                TT = TTb[cur]
                inv_tok = (cx.dve, cx.dve.n)
            with A(nc) as pl:
                pX, pU, pY, pM = pl.ps("rk_pX", [128, 64]), pl.ps("rk_pU", [128, 64]), pl.ps("rk_pY", [128, 64]), pl.ps("rk_pM", [128, 64])
                cx.wait(V, inv_tok)
                V.memset(M32[:], 0.0)
                m_tok = cx.dve.mark(V.memset(Mb[:], 0.0))
                cx.wait(PE, inv_tok)
                cx.wait(PE, (cx.act, cx.act.n))
                y_tok = None
                xs_tok = None
                us_tok = None
                mb_tok = m_tok
                m32_tok = None
                for c in range(NCH):
                    for hs in range(2):
                        hp = slice(hs * 64, hs * 64 + 64)
                        PE.matmul(pX[hp, :], lhsT=AB2[hp, c, 0:64], rhs=Vtok[hp, c, :], start=True, stop=False)
                    cx.wait(PE, mb_tok)
                    for hs in range(2):
                        hp = slice(hs * 64, hs * 64 + 64)
                        ins = PE.matmul(pX[hp, :], lhsT=KR[hp, c, 0, :], rhs=Mb[hp, :], start=False, stop=True)
                    x_tok = cx.pe.mark(ins)
                    cx.wait(ACT_, x_tok)
                    xs_tok = cx.act.mark(ACT_.copy(out=Xs[:], in_=pX[:]))
                    cx.wait(PE, xs_tok)
                    for hs in range(2):
                        hp = slice(hs * 64, hs * 64 + 64)
                        ins = PE.matmul(pU[hp, :], lhsT=TT[hp, c, :], rhs=Xs[hp, :], start=True, stop=True)
                    u_tok = cx.pe.mark(ins)
                    cx.wait(V, u_tok)
                    us_tok = cx.dve.mark(V.tensor_copy(out=Us[:], in_=pU[:]))
                    cx.wait(PE, y_tok)
                    for hs in range(2):
                        hp = slice(hs * 64, hs * 64 + 64)
                        PE.matmul(pY[hp, :], lhsT=Mb[hp, :], rhs=KR[hp, c, 1, :], start=True, stop=False)
                        PE.matmul(pY[hp, :], lhsT=Vtok[hp, c, :], rhs=AB2[hp, c, 64:128], start=False, stop=False)
                    cx.wait(PE, us_tok)
                    for hs in range(2):
                        hp = slice(hs * 64, hs * 64 + 64)
                        ins = PE.matmul(pY[hp, :], lhsT=Us[hp, :], rhs=NTB[hp, c, 64:128], start=False, stop=True)
                    yp_tok = cx.pe.mark(ins)
                    cx.wait(PE, m32_tok)
                    for hs in range(2):
                        hp = slice(hs * 64, hs * 64 + 64)
                        PE.matmul(pM[hp, :], lhsT=Khat[hp, c, :], rhs=Vtok[hp, c, :], start=True, stop=False)
                        ins = PE.matmul(pM[hp, :], lhsT=nBhat[hp, c, :], rhs=Us[hp, :], start=False, stop=True)
                    mp_tok = cx.pe.mark(ins)
                    cx.wait(ACT_, yp_tok)
                    y_tok = cx.act.mark(ACT_.copy(out=YT[:, c * 64:(c + 1) * 64], in_=pY[:]))
                    cx.wait(V, mp_tok)
                    mb_tok = cx.dve.mark(V.scalar_tensor_tensor(out=Mb[:], in0=M32[:], scalar=PC[:, c:c + 1], in1=pM[:], op0=ALU.mult, op1=ALU.add))
                    m32_tok = cx.dve.mark(V.scalar_tensor_tensor(out=M32[:], in0=M32[:], scalar=PC[:, c:c + 1], in1=pM[:], op0=ALU.mult, op1=ALU.add))
                loop_tok = m32_tok
            ser = Ser(cx, loop_tok)
            with A(nc) as pl:
                pA = pl.ps("rk_pB", [128, 512])
                F0, F3 = Fb[0], Fb[3]
                for p4 in range(NP):
                    psl = slice(p4 * 512, (p4 + 1) * 512)
                    ser.run(PE, lambda: PE.matmul(pA[:], lhsT=bones[:], rhs=YT[:, psl], start=True, stop=True), extra=[y_tok])
                    ser.run(V, lambda: V.scalar_tensor_tensor(out=YT[:, psl], in0=pA[:], scalar=-1.0 / 64, in1=YT[:, psl], op0=ALU.mult, op1=ALU.add))
                ser.run(ACT_, lambda: ACT_.activation(out=F0[:], in_=YT[:], func=AF.Square))
                for p4 in range(NP):
                    psl = slice(p4 * 512, (p4 + 1) * 512)
                    ser.run(PE, lambda: PE.matmul(pA[:], lhsT=bones[:], rhs=F0[:, psl], start=True, stop=True))
                    ser.run(V, lambda: V.tensor_scalar(out=F0[:, psl], in0=pA[:], scalar1=1.0 / 64, scalar2=gn_eps, op0=ALU.mult, op1=ALU.add))
                ser.run(ACT_, lambda: ACT_.activation(out=F0[:], in_=F0[:], func=AF.Ln))
                ser.run(ACT_, lambda: ACT_.activation(out=F0[:], in_=F0[:], func=AF.Exp, scale=-0.5))
                ser.run(V, lambda: V.tensor_tensor(out=YT[:], in0=YT[:], in1=F0[:], op=ALU.mult))
                ser.run(V, lambda: V.tensor_scalar(out=YT[:], in0=YT[:], scalar1=vc(8), scalar2=vc(9), op0=ALU.mult, op1=ALU.add))
                ser.run(V, lambda: V.tensor_tensor(out=YT[:], in0=YT[:], in1=Fb[2][:], op=ALU.add))
                ser.run(V, lambda: V.tensor_tensor(out=ysb[:], in0=YT[:], in1=Fb[7][:], op=ALU.mult), extra=[(sty, sty.n)])
                cx.wait(nc.sync, ser.last)
                sty.mark(nc.sync.dma_start(out=yT_dram[y_row0 + ti * 128:y_row0 + (ti + 1) * 128, :], in_=ysb[:]))
            barrier(cx)
            if after_round is not None:
                after_round(ti)


def setup_consts(cx):
    nc = cx.nc
    C = {}
    C["ident_f"] = nc.alloc_sbuf_tensor("ident_f", [128, 128], F32)
    C["ones_f"] = nc.alloc_sbuf_tensor("ones_f", [128, 128], F32)
    C["ident_b"] = nc.alloc_sbuf_tensor("ident_b", [128, 128], BF16)
    C["ones_b"] = nc.alloc_sbuf_tensor("ones_b", [128, 128], BF16)
    t = cx.dve.mark(nc.vector.memset(C["ones_f"][:], 1.0))
    nc.vector.memset(C["ones_b"][:], 1.0)
    cx.wait(nc.gpsimd, t)
    t2 = cx.pool.mark(nc.gpsimd.affine_select(out=C["ident_f"][:], in_=C["ones_f"][:], pattern=[[-1, 128]], compare_op=ALU.is_equal,
                                              fill=0.0, base=0, channel_multiplier=1))
    cx.wait(nc.vector, t2)
    cx.dve.mark(nc.vector.tensor_copy(out=C["ident_b"][:], in_=C["ident_f"][:]))
    return C


CFG_FULL = dict(D=4096, F=11008, T=1024, NFH=8, NRT=8, HX=4, NM=256)
GROUPS = [[0, 1], [2, 3], [4, 5], [6, 7]]


def build_full(cfg):
    D, F, T, NFH, NRT, HX, NM = (cfg[k] for k in ("D", "F", "T", "NFH", "NRT", "HX", "NM"))
    S = 2 * T
    DC, FC = D // 128, F // 128
    FW, RW = NFH * 128, NRT * 128
    NTF = 3 * NFH
    NTR = 3 * NRT + 4 + 1
    NCOL = (NTF + NTR) * 128
    YC = FW + RW
    assert 2 * YC == D
    eps = 1e-6
    nc = bass.Bass("TRN2", target_bir_lowering=False)
    ein = lambda n, sh, dt=F32: nc.dram_tensor(n, sh, dt, kind="ExternalInput").ap()
    x = ein("x", [T, D])
    mem = ein("mem", [NM, D])
    wg1, wu1, wd1 = ein("wg1", [D, F]), ein("wu1", [D, F]), ein("wd1", [F, D])
    wg2, wu2, wd2 = ein("wg2", [D, F]), ein("wu2", [D, F]), ein("wd2", [F, D])
    w_in = ein("w_in_c", [D, NCOL])
    w_out = ein("w_out_p", [D, D])
    wq, wk, wv, wo = ein("wq", [D, D]), ein("wk", [D, D]), ein("wv", [D, D]), ein("wo", [D, D])
    gains = ein("gains", [128, 9, DC])
    rvec_d = ein("rvec", [128, NRT * NVT + 4])
    wup_d, aup_d, gup_d = ein("wup", [128, RW]), ein("aup", [128, RW]), ein("gup", [256, RW])
    fbias_d = ein("fbias", [128, 1])
    sel_d = ein("sel", [128, 2])
    out = nc.dram_tensor("out", [T, D], F32, kind="ExternalOutput").ap()
    xT = nc.dram_tensor("xT_s", [D, T], F32).ap()
    uT = nc.dram_tensor("uT_s", [F, T], BF16).ap()
    rawT = nc.dram_tensor("rawT_s", [D, T], F32).ap()
    qT = nc.dram_tensor("qT_s", [D, T], BF16).ap()
    HCH = max(1, (D * T * 2) // (1 << 20))
    HR = D // HCH
    hsnd = nc.dram_tensor("hsnd_s", [D, T], BF16)
    hrcv = nc.dram_tensor("hrcv_s", [HCH, 2 * HR, T], BF16)
    YCH = max(1, (YC * S * 2) // (1 << 20))
    YR = YC // YCH
    assert HR % 128 == 0 and YR % 128 == 0
    ysnd = nc.dram_tensor("ysnd_s", [YC, S], BF16)
    yrcv = nc.dram_tensor("yrcv_s", [YCH, 2 * YR, S], BF16)
    projF = nc.dram_tensor("projF_s", [NTF * 128, S], BF16).ap()
    projR = nc.dram_tensor("projR_s", [NTR * 128, S], F32).ap()
    kT_d = nc.dram_tensor("kT_s", [128, DC, NM], BF16).ap()
    vtok_d = nc.dram_tensor("vtok_s", [128, NM // 128, D], BF16).ap()

    cx = Cx(nc)
    C = setup_consts(cx)
    g_sb = nc.alloc_sbuf_tensor("g_sb", [128, 9, DC], F32)
    fb_sb = nc.alloc_sbuf_tensor("fb_sb", [128, 1], F32)
    sel_sb = nc.alloc_sbuf_tensor("sel_sb", [128, 2], F32)
    gl = dma_ev(cx, "gl")
    gl.mark(nc.sync.dma_start(out=g_sb[:], in_=gains))
    gl.mark(nc.sync.dma_start(out=fb_sb[:], in_=fbias_d))
    tok = gl.mark(nc.sync.dma_start(out=sel_sb[:], in_=sel_d))
    cx.wait(nc.vector, tok)
    for gi in (1, 7):
        cx.dve.mark(nc.vector.tensor_scalar(out=g_sb[:, gi, :], in0=g_sb[:, gi, :], scalar1=0.5, scalar2=None, op0=ALU.mult))
    barrier(cx)
    G = lambda i: g_sb[:, i, :]
    wsems = [dma_ev(cx, "w0"), dma_ev(cx, "w1")]
    wstate = {}
    cc = Ev(nc, "cc_ev", 1)
    ACTN = max(DC * T, FC * min(T, 512))

    with A(nc) as s1:
        actbuf = s1.sb("actbuf", [128, ACTN], BF16)
        act = actbuf[:, 0:DC * T].rearrange("p (k t) -> p k t", k=DC)
        norm_in_tokmajor(cx, x, T, DC, G(0), act, xT, C, eps)
        wbufs = [s1.sb("wb0", [128, 16384], BF16), s1.sb("wb1", [128, 16384], BF16)]
        ffn_up(cx, act, DC, T, wg1, wu1, FC, uT, wbufs, wsems, wstate)
        with A(nc) as sp:
            stA, stB = sp.ps("stA", [128, T]), sp.ps("stB", [128, T])
            ffn_down(cx, actbuf, FC, T, wd1, DC, uT, rawT, stA, C, wbufs, wsems, wstate)
            residual_pass(cx, rawT, stA, xT, T, DC, G(1), G(2), act, C, eps, stB)
        hs = dma_ev(cx, "hsnd")
        t = hs.mark(nc.sync.dma_start(out=hsnd.ap().rearrange("(k p) t -> p k t", p=128), in_=act))
        cx.wait(nc.gpsimd, t)
        barrier(cx)
        for c in range(HCH):
            t_cc = cc.mark(nc.gpsimd.collective_compute("AllGather", ALU.bypass, replica_groups=GROUPS,
                                                        ins=[hsnd.ap()[c * HR:(c + 1) * HR, :].opt()], outs=[hrcv.ap()[c].opt()]))
    wstate = {}
    with A(nc) as s0:
        wbufs = [s0.sb("wb0", [128, 16384], BF16), s0.sb("wb1", [128, 16384], BF16)]
        xattn_kv_phase(cx, DC, NM, mem, G(8), wk, wv, kT_d, vtok_d, C, eps, wbufs, wsems, wstate)
    wstate = {}
    for e_ in (nc.gpsimd, nc.sync, nc.tensor, nc.scalar, nc.vector):
        cx.wait(e_, t_cc)
    with A(nc) as s1b:
        actbuf = s1b.sb("actbuf", [128, ACTN], BF16)
        wbufs = [s1b.sb("wb0", [128, 16384], BF16), s1b.sb("wb1", [128, 16384], BF16)]
        halves = [[hrcv.ap()[c][th * HR:(th + 1) * HR, :] for c in range(HCH)] for th in range(2)]
        inproj_phase(cx, actbuf, None, S, DC, w_in, NTF, NTR, projF, projR, wbufs, wsems, wstate, halves=halves)
    wstate = {}
    ystate = {"next": 0, "tok": None}

    def issue_y(rows_done):
        while ystate["next"] < YCH and (ystate["next"] + 1) * YR <= rows_done:
            c = ystate["next"]
            ystate["tok"] = cc.mark(nc.gpsimd.collective_compute("AllGather", ALU.bypass, replica_groups=GROUPS,
                                                                 ins=[ysnd.ap()[c * YR:(c + 1) * YR, :].opt()], outs=[yrcv.ap()[c].opt()]))
            ystate["next"] += 1

    fox_phase(cx, S, NFH, projF, projR[(NTR - 1) * 128:NTR * 128, :], fb_sb[:], ysnd.ap(), C)
    issue_y(FW)
    rwkv_phase(cx, S, NRT, projR, rvec_d, wup_d, aup_d, gup_d, ysnd.ap(), FW, C, 64e-5, after_round=lambda ti: issue_y(FW + (ti + 1) * 128))
    assert ystate["next"] == YCH
    t = ystate["tok"]
    for e_ in (nc.gpsimd, nc.sync, nc.tensor, nc.scalar, nc.vector):
        cx.wait(e_, t)
    barrier(cx)
    with A(nc) as s2:
        actbuf = s2.sb("actbuf", [128, ACTN], BF16)
        act = actbuf[:, 0:DC * T].rearrange("p (k t) -> p k t", k=DC)
        wbufs = [s2.sb("wb0", [128, 16384], BF16), s2.sb("wb1", [128, 16384], BF16)]
        with A(nc) as sb:
            g0s = [sb.sb("bl_g00", [128, T], BF16), sb.sb("bl_g01", [128, T], BF16)]
            g1s = [sb.sb("bl_g10", [128, T], BF16), sb.sb("bl_g11", [128, T], BF16)]
            ldb = [dma_ev(cx, "bl0"), dma_ev(cx, "bl1")]
            free = [None, None]
            for k in range(DC):
                b = k % 2
                rank, loc = (k * 128) // YC, (k * 128) % YC
                cch, sub = loc // YR, loc % YR
                yk = yrcv.ap()[cch][rank * YR + sub:rank * YR + sub + 128, :]
                cx.wait(nc.sync, free[b])
                ldb[b].mark(nc.sync.dma_start(out=g0s[b][:], in_=yk[:, 0:T]))
                lt = ldb[b].mark(nc.sync.dma_start(out=g1s[b][:], in_=yk[:, T:2 * T]))
                cx.wait(nc.vector, lt)
                d = cx.dve.mark(nc.vector.tensor_scalar(out=g0s[b][:], in0=g0s[b][:], scalar1=sel_sb[:, 0:1], scalar2=None, op0=ALU.mult))
                cx.wait(nc.vector, d)
                free[b] = cx.dve.mark(nc.vector.scalar_tensor_tensor(out=act[:, k, :], in0=g1s[b][:], scalar=sel_sb[:, 1:2], in1=g0s[b][:],
                                                                     op0=ALU.mult, op1=ALU.add))
            barrier(cx)
        with A(nc) as sp:
            stA, stB = sp.ps("stA", [128, T]), sp.ps("stB", [128, T])
            proj_raw_stats(cx, actbuf, T, DC, w_out, rawT, stA, C, wbufs, wsems, wstate)
            residual_pass(cx, rawT, stA, xT, T, DC, G(3), G(4), act, C, eps, stB)
        xattn_phase(cx, actbuf, T, DC, HX, NM, kT_d, vtok_d, wq, qT, C, wbufs, wsems, wstate)
        with A(nc) as sp:
            stA, stB = sp.ps("stA", [128, T]), sp.ps("stB", [128, T])
            proj_raw_stats(cx, actbuf, T, DC, wo, rawT, stA, C, wbufs, wsems, wstate)
            residual_pass(cx, rawT, stA, xT, T, DC, G(5), G(6), act, C, eps, stB)
        ffn_up(cx, act, DC, T, wg2, wu2, FC, uT, wbufs, wsems, wstate)
        with A(nc) as sp:
            stA, stB = sp.ps("stA", [128, T]), sp.ps("stB", [128, T])
            ffn_down(cx, actbuf, FC, T, wd2, DC, uT, rawT, stA, C, wbufs, wsems, wstate)
            residual_pass(cx, rawT, stA, xT, T, DC, G(7), G(0), act, C, eps, stB, out_dram=out)
    barrier(cx)
    return nc


def host_layout(cfg, inp):
    D, F, T, NFH, NRT, HX, NM = (cfg[k] for k in ("D", "F", "T", "NFH", "NRT", "HX", "NM"))
    DC = D // 128
    FW, RW = NFH * 128, NRT * 128
    FOXW, RWW = 2 * FW, 2 * RW
    f32 = lambda a: np.ascontiguousarray(np.asarray(a, dtype=np.float32))
    tl = lambda v: np.ascontiguousarray(f32(v).reshape(-1, 128).T)
    w_in = f32(inp["w_in"][0])
    R0 = 3 * FOXW + 2 * NFH
    gl = [inp[k][0] for k in ("ffn1_pre_g", "ffn1_post_g", "mix_pre_g", "mix_post_g", "xattn_pre_g", "xattn_post_g",
                              "ffn2_pre_g", "ffn2_post_g", "mem_norm_g")]
    gains = np.ascontiguousarray(np.stack([tl(g) for g in gl], axis=1))
    w_out = f32(inp["w_out"][0])
    w_out_p = np.ascontiguousarray(np.concatenate([w_out[0:FW], w_out[FOXW:FOXW + RW], w_out[FW:FOXW], w_out[FOXW + RW:FOXW + RWW]], 0))
    shared = dict(
        wg1=f32(inp["ffn1_w_gate"][0]), wu1=f32(inp["ffn1_w_up"][0]), wd1=f32(inp["ffn1_w_down"][0]),
        wg2=f32(inp["ffn2_w_gate"][0]), wu2=f32(inp["ffn2_w_up"][0]), wd2=f32(inp["ffn2_w_down"][0]),
        w_out_p=w_out_p, wq=f32(inp["xattn_wq"][0]), wk=f32(inp["xattn_wk"][0]), wv=f32(inp["xattn_wv"][0]), wo=f32(inp["xattn_wo"][0]),
        gains=gains)
    mu = f32(inp["rwkv_mu"][0])
    per_hh = []
    for hh in range(2):
        cols = []
        for blk in range(3):
            cols.append(w_in[:, blk * FOXW + hh * FW: blk * FOXW + (hh + 1) * FW])
        for blk in range(3):
            cols.append(w_in[:, R0 + blk * RWW + hh * RW: R0 + blk * RWW + (hh + 1) * RW])
        pad = lambda a, n: np.concatenate([a, np.zeros((a.shape[0], n - a.shape[1]), np.float32)], 1)
        L0 = R0 + 3 * RWW
        cols.append(pad(w_in[:, L0:L0 + 96], 128))
        cols.append(pad(w_in[:, L0 + 96:L0 + 192], 128))
        cols.append(w_in[:, L0 + 192:L0 + 448])
        cols.append(pad(w_in[:, 3 * FOXW + hh * NFH: 3 * FOXW + (hh + 1) * NFH], 128))
        w_in_c = np.ascontiguousarray(np.concatenate(cols, 1))
        ch = slice(hh * RW, (hh + 1) * RW)
        rvec = np.zeros((128, NRT * NVT + 4), np.float32)
        vecs = [mu[0:RWW][ch], mu[RWW:2 * RWW][ch], mu[2 * RWW:3 * RWW][ch], inp["rwkv_w0"][0][ch], inp["rwkv_a0"][0][ch],
                inp["rwkv_k_k"][0][ch], inp["rwkv_k_a"][0][ch], inp["rwkv_r_k"][0][ch], inp["rwkv_ln_w"][0][ch], inp["rwkv_ln_b"][0][ch]]
        for i, v in enumerate(vecs):
            rvec[:, i:NRT * NVT:NVT] = tl(v)
        GV = NRT * NVT
        M0 = 3 * RWW
        rvec[0:96, GV] = mu[M0:M0 + 96]
        rvec[0:96, GV + 1] = mu[M0 + 96:M0 + 192]
        rvec[:, GV + 2] = mu[M0 + 192:M0 + 320]
        rvec[:, GV + 3] = mu[M0 + 320:M0 + 448]
        wup = np.zeros((128, RW), np.float32)
        wup[:96] = f32(inp["rwkv_w_up"][0])[:, ch]
        aup = np.zeros((128, RW), np.float32)
        aup[:96] = f32(inp["rwkv_a_up"][0])[:, ch]
        gup = np.ascontiguousarray(f32(inp["rwkv_g_up"][0])[:, ch])
        fb = np.zeros((128, 1), np.float32)
        fb[0:NFH, 0] = f32(inp["fox_f_bias"][0])[hh * NFH:(hh + 1) * NFH]
        sel = np.zeros((128, 2), np.float32)
        sel[:, hh] = 1.0
        per_hh.append(dict(w_in_c=w_in_c, rvec=rvec, wup=wup, aup=aup, gup=gup, fbias=fb, sel=sel))
    xs = f32(inp["x"])
    mems = f32(inp["mem"])
    in_maps = []
    for core in range(8):
        b, hh = core // 2, core % 2
        m = dict(shared)
        m.update(per_hh[hh])
        m["x"] = np.ascontiguousarray(xs[b, hh * T:(hh + 1) * T, :])
        m["mem"] = np.ascontiguousarray(mems[b])
        in_maps.append(m)
    return in_maps


def run_cfg(cfg, inp, trace=False):
    nc = build_full(cfg)
    in_maps = host_layout(cfg, inp)
    res = run_bass_kernel_spmd(nc, in_maps, core_ids=list(range(8)), **({"trace": True} if trace else {}))
    T, D = cfg["T"], cfg["D"]
    outp = np.zeros((4, 2 * T, D), np.float32)
    for core in range(8):
        b, hh = core // 2, core % 2
        outp[b, hh * T:(hh + 1) * T, :] = res.results[core]["out"]
    return outp, res


def kernel(**inputs):
    outp, _ = run_cfg(CFG_FULL, inputs)
    return outp
```

```python
import numpy as np
from concourse.bass_utils import run_bass_kernel_spmd
import concourse.bass as bass
import concourse.mybir as mybir

F32 = mybir.dt.float32
BF16 = mybir.dt.bfloat16
AF = mybir.ActivationFunctionType
ALU = mybir.AluOpType
_uid = [0]


def un(s):
    _uid[0] += 1
    return f"{s}_{_uid[0]}"


class A:
    def __init__(self, nc):
        from contextlib import ExitStack
        self.nc = nc
        self.st = ExitStack()

    def __enter__(self):
        self.st.__enter__()
        return self

    def __exit__(self, *a):
        return self.st.__exit__(*a)

    def sb(self, name, shape, dt):
        return self.st.enter_context(self.nc.sbuf_tensor(un(name), shape, dt))

    def ps(self, name, shape, dt=None):
        return self.st.enter_context(self.nc.psum_tensor(un(name), shape, dt if dt is not None else F32))


class Ev:
    def __init__(self, nc, name, step=1):
        self.sem = nc.alloc_semaphore(name)
        self.n = 0
        self.step = step

    def mark(self, ins):
        ins.then_inc(self.sem, self.step)
        self.n += self.step
        return (self, self.n)


class Cx:
    def __init__(self, nc):
        self.nc = nc
        self.pe = Ev(nc, "e_pe")
        self.act = Ev(nc, "e_act")
        self.dve = Ev(nc, "e_dve")
        self.pool = Ev(nc, "e_pool")
        self.waited = {}
        self._n = 0
        self.dma_evs = []

    def ev(self, name, step=16):
        self._n += 1
        return Ev(self.nc, f"{name}_{self._n}", step)

    def wait(self, eng, tok):
        if tok is None:
            return
        ev, v = tok
        if v <= 0:
            return
        key = (id(eng), id(ev))
        if self.waited.get(key, 0) >= v:
            return
        self.waited[key] = v
        eng.wait_ge(ev.sem, v)

    def evof(self, eng):
        nc = self.nc
        if eng is nc.tensor:
            return self.pe
        if eng is nc.scalar:
            return self.act
        if eng is nc.vector:
            return self.dve
        if eng is nc.gpsimd:
            return self.pool
        raise ValueError

    def op(self, eng, ins, deps=()):
        return self.evof(eng).mark(ins)


def _issue_wgroup(cx, KC, Ws, n_tiles, ngt, wbufs, wsems, wstate, g):
    nc = cx.nc
    nW = len(Ws)
    NG = ngt * 128
    wfree = wstate.setdefault("wfree", [None, None])
    b = (wstate.get("gi", 0) + g) % 2
    gt = min(ngt, n_tiles - g * ngt)
    ncols = gt * 128
    wv = wbufs[b][:, 0:nW * KC * NG].rearrange("p (w k n) -> p w k n", w=nW, k=KC)
    cx.wait(nc.gpsimd, wfree[b])
    for wi, W in enumerate(Ws):
        src = W.rearrange("(k p) n -> p k n", p=128)[:, :, g * NG:g * NG + ncols]
        wtok = wsems[b].mark(nc.gpsimd.dma_start(out=wv[:, wi, :, 0:ncols], in_=src))
    return wtok


def gemm_prefetch(cx, KC, Ws, n_tiles, ngt, wbufs, wsems, wstate):
    wstate["pre"] = (tuple(id(W.tensor) for W in Ws) + (KC, n_tiles, ngt, tuple(W.offset for W in Ws)),
                     _issue_wgroup(cx, KC, Ws, n_tiles, ngt, wbufs, wsems, wstate, 0))


def gemm(cx, act, KC, T, Ws, n_tiles, ngt, wbufs, wsems, ps_sets, epilogue, pre_tok=None, wstate=None, k_toks=None):
    nc = cx.nc
    nW = len(Ws)
    NG = ngt * 128
    n_groups = (n_tiles + ngt - 1) // ngt
    if wstate is None:
        wstate = {}
    wfree = wstate.setdefault("wfree", [None, None])
    gi0 = wstate.get("gi", 0)
    ps_free = [None, None]
    nth = (T + 511) // 512
    tile_i = 0
    last_tok = None
    for g in range(n_groups):
        b = (gi0 + g) % 2
        gt = min(ngt, n_tiles - g * ngt)
        ncols = gt * 128
        wv = wbufs[b][:, 0:nW * KC * NG].rearrange("p (w k n) -> p w k n", w=nW, k=KC)
        pre = wstate.pop("pre", None) if g == 0 else None
        key = tuple(id(W.tensor) for W in Ws) + (KC, n_tiles, ngt, tuple(W.offset for W in Ws))
        if pre is not None and pre[0] == key:
            wtok = pre[1]
        else:
            assert pre is None, "stale weight prefetch"
            wtok = _issue_wgroup(cx, KC, Ws, n_tiles, ngt, wbufs, wsems, wstate, g)
        cx.wait(nc.tensor, wtok)
        if g == 0:
            cx.wait(nc.tensor, pre_tok)
        for jt in range(gt):
            j = g * ngt + jt
            s = tile_i % 2
            cx.wait(nc.tensor, ps_free[s])
            ins = None
            for wi in range(nW):
                for k in range(KC):
                    if k_toks is not None and tile_i == 0 and wi == 0:
                        cx.wait(nc.tensor, k_toks.get(k))
                    for th in range(nth):
                        t0 = th * 512
                        t1 = min(T, t0 + 512)
                        ins = nc.tensor.matmul(ps_sets[s][wi][:, t0:t1], lhsT=wv[:, wi, k, jt * 128:(jt + 1) * 128],
                                               rhs=act[:, k, t0:t1], start=(k == 0), stop=(k == KC - 1))
            pe_tok = cx.pe.mark(ins)
            last_tok = pe_tok
            ps_free[s] = epilogue(j, ps_sets[s], pe_tok)
            tile_i += 1
        wfree[b] = last_tok
    wstate["gi"] = gi0 + n_groups
    return last_tok


def psum_fence(cx):
    nc = cx.nc
    cx.wait(nc.tensor, (cx.act, cx.act.n))
    cx.wait(nc.tensor, (cx.dve, cx.dve.n))


def barrier(cx, engines=None):
    nc = cx.nc
    engs = engines or [nc.tensor, nc.scalar, nc.vector, nc.gpsimd, nc.sync]
    evs = [cx.pe, cx.act, cx.dve, cx.pool] + cx.dma_evs
    for e in engs:
        for ev in evs:
            cx.wait(e, (ev, ev.n))


def dma_ev(cx, name):
    cache = cx.__dict__.setdefault("ev_cache", {})
    if name in cache:
        return cache[name]
    ev = Ev(cx.nc, f"{name}_{len(cx.dma_evs)}", 16)
    cx.dma_evs.append(ev)
    cache[name] = ev
    return ev


def rstd_from_stats(cx, out_sb, st_ps, D, eps, pre_tok):
    nc = cx.nc
    cx.wait(nc.vector, pre_tok)
    t = cx.dve.mark(nc.vector.tensor_scalar(out=out_sb, in0=st_ps, scalar1=1.0 / D, scalar2=eps, op0=ALU.mult, op1=ALU.add))
    cx.wait(nc.vector, t)
    cx.wait(nc.scalar, t)
    t = cx.act.mark(nc.scalar.activation(out=out_sb, in_=out_sb, func=AF.Sqrt))
    cx.wait(nc.vector, t)
    t = cx.dve.mark(nc.vector.reciprocal(out=out_sb, in_=out_sb))
    cx.wait(nc.vector, t)
    return t


def norm_in_tokmajor(cx, x_dram, Tt, DC, g_sb, act_out, xT_dram, C, eps, pre_tok=None):
    nc = cx.nc
    D = DC * 128
    ntt = Tt // 128
    ncg = DC // 4
    with (nc.sbuf_tensor(un("n0_xt0"), [128, D], F32) as xt0, nc.sbuf_tensor(un("n0_xt1"), [128, D], F32) as xt1,
          nc.sbuf_tensor(un("n0_xTt"), [128, DC, 128], F32) as xTt,
          nc.sbuf_tensor(un("n0_sq0"), [128, 4, 128], F32) as sq0, nc.sbuf_tensor(un("n0_sq1"), [128, 4, 128], F32) as sq1,
          nc.sbuf_tensor(un("n0_rstd"), [128, 128], F32) as rstd,
          nc.psum_tensor(un("n0_pT0"), [128, 4, 128], F32) as pT0, nc.psum_tensor(un("n0_pT1"), [128, 4, 128], F32) as pT1,
          nc.psum_tensor(un("n0_st"), [128, 128], F32) as stp):
        xts = [xt0, xt1]
        sqs = [sq0, sq1]
        pTs = [pT0, pT1]
        ld = [dma_ev(cx, "n0ld0"), dma_ev(cx, "n0ld1")]
        st = dma_ev(cx, "n0st")
        xt_free = [pre_tok, pre_tok]
        pT_free = [None, None]
        sq_free = [None, None]
        xTt_free = [pre_tok, None]
        stp_free = None
        ld_tok = [None, None]

        def issue_load(tt):
            b = tt % 2
            cx.wait(nc.sync, xt_free[b])
            ld_tok[b] = ld[b].mark(nc.sync.dma_start(out=xts[b][:], in_=x_dram[tt * 128:(tt + 1) * 128, :]))

        issue_load(0)
        u = 0
        for tt in range(ntt):
            b = tt % 2
            if tt + 1 < ntt:
                issue_load(tt + 1)
            cx.wait(nc.tensor, ld_tok[b])
            pending = None
            a_tok = None
            for cg in range(ncg):
                s = u % 2
                u += 1
                cx.wait(nc.tensor, pT_free[s])
                for c4 in range(4):
                    c = cg * 4 + c4
                    ins = nc.tensor.transpose(pTs[s][:, c4, :], xts[b][:, c * 128:(c + 1) * 128], C["ident_f"][:])
                p_tok = cx.pe.mark(ins)
                if pending is not None:
                    pa_tok, ps_, pcg = pending
                    cx.wait(nc.tensor, pa_tok)
                    if pcg == 0:
                        cx.wait(nc.tensor, stp_free)
                    for c4 in range(4):
                        ins = nc.tensor.matmul(stp[:, :], lhsT=C["ones_f"][:], rhs=sqs[ps_][:, c4, :],
                                               start=(pcg == 0 and c4 == 0), stop=False)
                    sq_free[ps_] = cx.pe.mark(ins)
                cx.wait(nc.scalar, p_tok)
                if cg == 0:
                    cx.wait(nc.scalar, xTt_free[0])
                    cx.wait(nc.scalar, xTt_free[1])
                nc.scalar.copy(out=xTt[:, cg * 4:(cg + 1) * 4, :], in_=pTs[s][:])
                cx.wait(nc.scalar, sq_free[s])
                a_tok = cx.act.mark(nc.scalar.activation(out=sqs[s][:], in_=pTs[s][:], func=AF.Square))
                pT_free[s] = a_tok
                pending = (a_tok, s, cg)
            xt_free[b] = p_tok
            pa_tok, ps_, pcg = pending
            cx.wait(nc.tensor, pa_tok)
            if pcg == 0:
                cx.wait(nc.tensor, stp_free)
            for c4 in range(4):
                ins = nc.tensor.matmul(stp[:, :], lhsT=C["ones_f"][:], rhs=sqs[ps_][:, c4, :],
                                       start=(pcg == 0 and c4 == 0), stop=(c4 == 3))
            s_tok = cx.pe.mark(ins)
            sq_free[ps_] = s_tok
            r_tok = rstd_from_stats(cx, rstd[:], stp[:, :], D, eps, s_tok)
            stp_free = r_tok
            cx.wait(nc.vector, a_tok)
            for c in range(DC):
                ins = nc.vector.scalar_tensor_tensor(out=act_out[:, c, tt * 128:(tt + 1) * 128], in0=xTt[:, c, :],
                                                     scalar=g_sb[:, c:c + 1], in1=rstd[:], op0=ALU.mult, op1=ALU.mult)
            d_tok = cx.dve.mark(ins)
            xTt_free[0] = d_tok
            if xT_dram is not None:
                cx.wait(nc.sync, a_tok)
                xTt_free[1] = st.mark(nc.sync.dma_start(
                    out=xT_dram.rearrange("(c p) t -> p c t", p=128)[:, :, tt * 128:(tt + 1) * 128], in_=xTt[:]))
        barrier(cx)


def ffn_up(cx, act, KC, T, Wg, Wu, FC, uT_dram, wbufs, wsems, wstate, pre_tok=None, Wd_next=None):
    nc = cx.nc
    with (nc.sbuf_tensor(un("fu_sg0"), [128, T], F32) as sg0, nc.sbuf_tensor(un("fu_sg1"), [128, T], F32) as sg1,
          nc.sbuf_tensor(un("fu_u0"), [128, T], BF16) as u0, nc.sbuf_tensor(un("fu_u1"), [128, T], BF16) as u1,
          nc.psum_tensor(un("fu_pg0"), [128, T], F32) as pg0, nc.psum_tensor(un("fu_pu0"), [128, T], F32) as pu0,
          nc.psum_tensor(un("fu_pg1"), [128, T], F32) as pg1, nc.psum_tensor(un("fu_pu1"), [128, T], F32) as pu1):
        sgs = [sg0, sg1]
        us = [u0, u1]
        st = [dma_ev(cx, "fust0"), dma_ev(cx, "fust1")]
        sg_free = [None, None]
        u_free = [None, None]

        def epi(j, pss, pe_tok):
            s = j % 2
            cx.wait(nc.scalar, pe_tok)
            cx.wait(nc.scalar, sg_free[s])
            a_tok = cx.act.mark(nc.scalar.activation(out=sgs[s][:], in_=pss[0][:], func=AF.Silu))
            cx.wait(nc.vector, a_tok)
            cx.wait(nc.vector, u_free[s])
            d_tok = cx.dve.mark(nc.vector.tensor_tensor(out=us[s][:], in0=sgs[s][:], in1=pss[1][:], op=ALU.mult))
            sg_free[s] = d_tok
            cx.wait(nc.sync, d_tok)
            u_free[s] = st[s].mark(nc.sync.dma_start(out=uT_dram[j * 128:(j + 1) * 128, :], in_=us[s][:]))
            return d_tok

        ngt = max(1, min(FC, (16384 // (2 * KC)) // 128))
        gemm(cx, act, KC, T, [Wg, Wu], FC, ngt, wbufs, wsems, [[pg0[:], pu0[:]], [pg1[:], pu1[:]]], epi,
             pre_tok=pre_tok, wstate=wstate)
        if Wd_next is not None:
            kc_ = (FC + 1) // 2
            gemm_prefetch(cx, kc_, [Wd_next[0:kc_ * 128, :]], KC, max(1, min(KC, (16384 // kc_) // 128)), wbufs, wsems, wstate)
        barrier(cx)


def gemm_raw_stats(cx, act, KC, T, t_off, W, DC, rawT_dram, st_ps, C, wbufs, wsems, wstate, ngt, pre_tok=None, tagn=""):
    nc = cx.nc
    nth = (T + 511) // 512
    with (nc.sbuf_tensor(un("gr_raw0") + tagn, [128, T], F32) as r0, nc.sbuf_tensor(un("gr_raw1") + tagn, [128, T], F32) as r1,
          nc.sbuf_tensor(un("gr_sq0") + tagn, [128, T], F32) as q0, nc.sbuf_tensor(un("gr_sq1") + tagn, [128, T], F32) as q1,
          nc.psum_tensor(un("gr_p0") + tagn, [128, T], F32) as p0, nc.psum_tensor(un("gr_p1") + tagn, [128, T], F32) as p1):
        rs = [r0, r1]
        qs = [q0, q1]
        st = [dma_ev(cx, "grst0"), dma_ev(cx, "grst1")]
        r_free = [None, None]
        q_free = [None, None]
        pend = []

        def stats_mm(j, s, a_tok):
            cx.wait(nc.tensor, a_tok)
            for th in range(nth):
                t0, t1 = th * 512, min(T, th * 512 + 512)
                ins = nc.tensor.matmul(st_ps[:, t_off + t0:t_off + t1], lhsT=C["ones_f"][:], rhs=qs[s][:, t0:t1],
                                       start=(j == 0), stop=(j == DC - 1))
            q_free[s] = cx.pe.mark(ins)

        def epi(j, pss, pe_tok):
            s = j % 2
            if pend:
                stats_mm(*pend.pop())
            cx.wait(nc.scalar, pe_tok)
            cx.wait(nc.scalar, r_free[s])
            a0 = cx.act.mark(nc.scalar.copy(out=rs[s][:], in_=pss[0][:]))
            cx.wait(nc.scalar, q_free[s])
            a_tok = cx.act.mark(nc.scalar.activation(out=qs[s][:], in_=pss[0][:], func=AF.Square))
            cx.wait(nc.sync, a0)
            r_free[s] = st[s].mark(nc.sync.dma_start(out=rawT_dram[j * 128:(j + 1) * 128, t_off:t_off + T], in_=rs[s][:]))
            pend.append((j, s, a_tok))
            return a_tok

        gemm(cx, act, KC, T, [W], DC, ngt, wbufs, wsems, [[p0[:]], [p1[:]]], epi, pre_tok=pre_tok, wstate=wstate)
        stats_mm(*pend.pop())
        barrier(cx)


def gemm_raw_acc(cx, act, KC, T, W, DC, rawT_dram, st_ps, C, wbufs, wsems, wstate, ngt, pre_tok=None, second=False, k_toks=None, next_w=None):
    nc = cx.nc
    nth = (T + 511) // 512
    with A(nc) as al:
        rs = [al.sb("ga_r0", [128, T], F32), al.sb("ga_r1", [128, T], F32)]
        ps = [al.ps("ga_p0", [128, T]), al.ps("ga_p1", [128, T])]
        st = [dma_ev(cx, "gast0"), dma_ev(cx, "gast1")]
        r_free = [None, None]
        if second:
            qs = [al.sb("ga_q0", [128, T], F32), al.sb("ga_q1", [128, T], F32)]
            pb = [al.sb("ga_pb0", [128, T], F32), al.sb("ga_pb1", [128, T], F32)]
            ldp = [dma_ev(cx, "gald0"), dma_ev(cx, "gald1")]
            q_free = [None, None]
            pb_free = [None, None]
            pend = []

            def stats_mm(j, s, a_tok):
                cx.wait(nc.tensor, a_tok)
                for th in range(nth):
                    t0, t1 = th * 512, min(T, th * 512 + 512)
                    ins = nc.tensor.matmul(st_ps[:, t0:t1], lhsT=C["ones_f"][:], rhs=qs[s][:, t0:t1], start=(j == 0), stop=(j == DC - 1))
                q_free[s] = cx.pe.mark(ins)

        def epi(j, pss, pe_tok):
            s = j % 2
            if not second:
                cx.wait(nc.scalar, pe_tok)
                cx.wait(nc.scalar, r_free[s])
                a0 = cx.act.mark(nc.scalar.copy(out=rs[s][:], in_=pss[0][:]))
                cx.wait(nc.sync, a0)
                r_free[s] = st[s].mark(nc.sync.dma_start(out=rawT_dram[j * 128:(j + 1) * 128, :], in_=rs[s][:]))
                return a0
            if pend:
                stats_mm(*pend.pop())
            cx.wait(nc.sync, pb_free[s])
            l_tok = ldp[s].mark(nc.sync.dma_start(out=pb[s][:], in_=rawT_dram[j * 128:(j + 1) * 128, :]))
            cx.wait(nc.vector, pe_tok)
            cx.wait(nc.vector, l_tok)
            cx.wait(nc.vector, r_free[s])
            d0 = cx.dve.mark(nc.vector.tensor_tensor(out=rs[s][:], in0=pss[0][:], in1=pb[s][:], op=ALU.add))
            pb_free[s] = d0
            cx.wait(nc.scalar, d0)
            cx.wait(nc.scalar, q_free[s])
            a_tok = cx.act.mark(nc.scalar.activation(out=qs[s][:], in_=rs[s][:], func=AF.Square))
            cx.wait(nc.sync, a_tok)
            r_free[s] = st[s].mark(nc.sync.dma_start(out=rawT_dram[j * 128:(j + 1) * 128, :], in_=rs[s][:]))
            pend.append((j, s, a_tok))
            return d0

        gemm(cx, act, KC, T, [W], DC, ngt, wbufs, wsems, [[ps[0][:]], [ps[1][:]]], epi, pre_tok=pre_tok, wstate=wstate, k_toks=k_toks)
        if next_w is not None:
            nkc, nW_ = next_w
            gemm_prefetch(cx, nkc, [nW_], DC, max(1, min(DC, (16384 // nkc) // 128)), wbufs, wsems, wstate)
        if second:
            stats_mm(*pend.pop())
        barrier(cx)


def ffn_down(cx, actbuf, FC, T, Wd, DC, uT_dram, rawT_dram, st_ps, C, wbufs, wsems, wstate):
    nc = cx.nc
    NPIECE = 4
    lds = [dma_ev(cx, f"fdld{i}") for i in range(NPIECE)]
    KH = (FC + 1) // 2
    k0 = 0
    for ph in range(2):
        kc = min(KH, FC - k0)
        act = actbuf[:, 0:kc * T].rearrange("p (k t) -> p k t", k=kc)
        ngt = max(1, min(DC, (16384 // kc) // 128))
        Wh = Wd[k0 * 128:(k0 + kc) * 128, :]
        if "pre" not in wstate:
            gemm_prefetch(cx, kc, [Wh], DC, ngt, wbufs, wsems, wstate)
        k_toks = {}
        pk = (kc + NPIECE - 1) // NPIECE
        for i in range(NPIECE):
            a, b_ = i * pk, min(kc, (i + 1) * pk)
            if a >= b_:
                continue
            k_toks[a] = lds[i].mark(nc.sync.dma_start(
                out=act[:, a:b_, :], in_=uT_dram[(k0 + a) * 128:(k0 + b_) * 128, :].rearrange("(k p) t -> p k t", p=128)))
        gemm_raw_acc(cx, act, kc, T, Wh, DC, rawT_dram, st_ps, C, wbufs, wsems, wstate, ngt,
                     second=(ph == 1), k_toks=k_toks,
                     next_w=None if ph == 1 else (min(KH, FC - k0 - kc), Wd[(k0 + kc) * 128:(k0 + kc + min(KH, FC - k0 - kc)) * 128, :]))
        k0 += kc


def residual_pass(cx, rawT_dram, st_ps, xT_dram, T, DC, gpost_sb, gpre_sb, act_out, C, eps, st2_ps, out_dram=None):
    nc = cx.nc
    D = DC * 128
    nth = (T + 511) // 512
    final = out_dram is not None
    NBF = 4
    with A(nc) as al:
        rbs = [al.sb(f"rp_r{i}", [128, T], F32) for i in range(NBF)]
        xbs = [al.sb(f"rp_x{i}", [128, T], F32) for i in range(NBF)]
        qbs = [al.sb(f"rp_q{i}", [128, T], F32) for i in range(2)]
        rsa, rsb = al.sb("rp_rsa", [128, T], F32), al.sb("rp_rsb", [128, T], F32)
        pTs = [al.ps("rp_pT0", [128, 4, 128]), al.ps("rp_pT1", [128, 4, 128])]
        ld = [dma_ev(cx, f"rpld{i}") for i in range(NBF)]
        st = [dma_ev(cx, f"rpst{i}") for i in range(NBF)]
        ra_tok = rstd_from_stats(cx, rsa[:], st_ps[:, 0:T], D, eps, None)
        r_free = [None] * NBF
        x_free = [[None, None] for _ in range(NBF)]
        q_free = [None, None]
        pT_free = [None, None]
        xT_v = xT_dram.rearrange("(c p) t -> c p t", p=128)
        raw_v = rawT_dram.rearrange("(c p) t -> c p t", p=128)
        u = 0
        for c in range(DC):
            b = c % NBF
            qi = c % 2
            cx.wait(nc.sync, r_free[b])
            cx.wait(nc.sync, x_free[b][0])
            cx.wait(nc.sync, x_free[b][1])
            ld[b].mark(nc.sync.dma_start(out=rbs[b][:], in_=raw_v[c]))
            ltok = ld[b].mark(nc.sync.dma_start(out=xbs[b][:], in_=xT_v[c]))
            cx.wait(nc.vector, ltok)
            t1 = cx.dve.mark(nc.vector.scalar_tensor_tensor(out=rbs[b][:], in0=rbs[b][:], scalar=gpost_sb[:, c:c + 1], in1=rsa[:],
                                                            op0=ALU.mult, op1=ALU.mult))
            cx.wait(nc.vector, t1)
            d_tok = cx.dve.mark(nc.vector.tensor_tensor(out=xbs[b][:], in0=rbs[b][:], in1=xbs[b][:], op=ALU.add))
            r_free[b] = d_tok
            if not final:
                cx.wait(nc.scalar, d_tok)
                x_free[b][0] = st[b].mark(nc.scalar.dma_start(out=xT_v[c], in_=xbs[b][:]))
                cx.wait(nc.scalar, q_free[qi])
                a_tok = cx.act.mark(nc.scalar.activation(out=qbs[qi][:], in_=xbs[b][:], func=AF.Square))
                x_free[b][1] = a_tok
                cx.wait(nc.tensor, a_tok)
                for th in range(nth):
                    t0, t1_ = th * 512, min(T, th * 512 + 512)
                    ins = nc.tensor.matmul(st2_ps[:, t0:t1_], lhsT=C["ones_f"][:], rhs=qbs[qi][:, t0:t1_],
                                           start=(c == 0), stop=(c == DC - 1))
                q_free[qi] = cx.pe.mark(ins)
            else:
                cx.wait(nc.tensor, d_tok)
                ov = out_dram.rearrange("(tt p) d -> p tt d", p=128)
                for t4 in range(T // 512):
                    s = u % 2
                    u += 1
                    cx.wait(nc.tensor, pT_free[s])
                    for i4 in range(4):
                        tt = t4 * 4 + i4
                        ins = nc.tensor.transpose(pTs[s][:, i4, :], xbs[b][:, tt * 128:(tt + 1) * 128], C["ident_f"][:])
                    p_tok = cx.pe.mark(ins)
                    x_free[b][1] = p_tok
                    cx.wait(nc.scalar, p_tok)
                    qs = qbs[s][:, 0:512].rearrange("p (a d) -> p a d", a=4)
                    cx.wait(nc.scalar, q_free[s])
                    a_tok = cx.act.mark(nc.scalar.copy(out=qs, in_=pTs[s][:]))
                    pT_free[s] = a_tok
                    cx.wait(nc.scalar, a_tok)
                    q_free[s] = st[s].mark(nc.scalar.dma_start(out=ov[:, t4 * 4:(t4 + 1) * 4, c * 128:(c + 1) * 128], in_=qs))
        if final:
            barrier(cx)
            return
        barrier(cx)
        rb_tok = rstd_from_stats(cx, rsb[:], st2_ps[:, 0:T], D, eps, None)
        x_free = [None] * NBF
        for c in range(DC):
            b = c % NBF
            cx.wait(nc.sync, x_free[b])
            ltok = ld[b].mark(nc.sync.dma_start(out=xbs[b][:], in_=xT_v[c]))
            cx.wait(nc.vector, ltok)
            x_free[b] = cx.dve.mark(nc.vector.scalar_tensor_tensor(out=act_out[:, c, :], in0=xbs[b][:], scalar=gpre_sb[:, c:c + 1],
                                                                   in1=rsb[:], op0=ALU.mult, op1=ALU.mult))
        barrier(cx)


def xattn_kv_phase(cx, DC, NM, mem_dram, gmem_sb, wk, wv, kT_d, vtok_d, C, eps, wbufs, wsems, wstate):
    nc = cx.nc
    D = DC * 128
    NMC = NM // 128
    with A(nc) as al:
        mT, kT, vtok = al.sb("xa_mT", [128, DC, NM], BF16), al.sb("xa_kT", [128, DC, NM], BF16), al.sb("xa_vtok", [128, NMC, D], BF16)
        gemm_prefetch(cx, DC, [wk, wv], DC, max(1, min(DC, (16384 // (2 * DC)) // 128)), wbufs, wsems, wstate)
        norm_in_tokmajor(cx, mem_dram, NM, DC, gmem_sb, mT[:], None, C, eps)
        with A(nc) as a2:
            vTs = [a2.sb("xa_vT0", [128, NM], BF16), a2.sb("xa_vT1", [128, NM], BF16)]
            pk0, pv0 = a2.ps("xa_pk0", [128, NM]), a2.ps("xa_pv0", [128, NM])
            pk1, pv1 = a2.ps("xa_pk1", [128, NM]), a2.ps("xa_pv1", [128, NM])
            pts = [a2.ps("xa_pt0", [128, NMC, 128], BF16), a2.ps("xa_pt1", [128, NMC, 128], BF16)]
            vT_free = [None, None]
            pt_free = [None, None]
            pend = []

            def do_tr(j, s, a_tok):
                cx.wait(nc.tensor, a_tok)
                cx.wait(nc.tensor, pt_free[s])
                for mc in range(NMC):
                    ins = nc.tensor.transpose(pts[s][:, mc, :], vTs[s][:, mc * 128:(mc + 1) * 128], C["ident_b"][:])
                p_tok = cx.pe.mark(ins)
                vT_free[s] = p_tok
                cx.wait(nc.vector, p_tok)
                pt_free[s] = cx.dve.mark(nc.vector.tensor_copy(out=vtok[:, :, j * 128:(j + 1) * 128], in_=pts[s][:]))

            def epi_kv(j, pss, pe_tok):
                s = j % 2
                if pend:
                    do_tr(*pend.pop())
                cx.wait(nc.scalar, pe_tok)
                nc.scalar.copy(out=kT[:, j, :], in_=pss[0][:])
                cx.wait(nc.scalar, vT_free[s])
                a_tok = cx.act.mark(nc.scalar.copy(out=vTs[s][:], in_=pss[1][:]))
                pend.append((j, s, a_tok))
                return a_tok

            ngt = max(1, min(DC, (16384 // (2 * DC)) // 128))
            gemm(cx, mT[:], DC, NM, [wk, wv], DC, ngt, wbufs, wsems, [[pk0[:], pv0[:]], [pk1[:], pv1[:]]], epi_kv, wstate=wstate)
            do_tr(*pend.pop())
            barrier(cx)
        st = dma_ev(cx, "xakvst")
        st.mark(nc.sync.dma_start(out=kT_d, in_=kT[:]))
        st.mark(nc.sync.dma_start(out=vtok_d, in_=vtok[:]))
        barrier(cx)


def xattn_phase(cx, actbuf, T, DC, H, NM, kT_d, vtok_d, wq, qT_dram, C, wbufs, wsems, wstate, wo_next=None):
    nc = cx.nc
    D = DC * 128
    HD = D // H
    HDC = HD // 128
    NMC = NM // 128
    act = actbuf[:, 0:DC * T].rearrange("p (k t) -> p k t", k=DC)
    scale = float(HD) ** -0.5
    with A(nc) as al:
        kT, vtok = al.sb("xa_kT", [128, DC, NM], BF16), al.sb("xa_vtok", [128, NMC, D], BF16)
        ldkv = dma_ev(cx, "xakvld")
        ldkv.mark(nc.sync.dma_start(out=kT[:], in_=kT_d))
        ldkv.mark(nc.sync.dma_start(out=vtok[:], in_=vtok_d))
        with (nc.sbuf_tensor(un("xa_q0"), [128, T], BF16) as q0, nc.sbuf_tensor(un("xa_q1"), [128, T], BF16) as q1,
              nc.psum_tensor(un("xa_pq0"), [128, T], F32) as pq0, nc.psum_tensor(un("xa_pq1"), [128, T], F32) as pq1):
            qs = [q0, q1]
            st = [dma_ev(cx, "xaq0"), dma_ev(cx, "xaq1")]
            q_free = [None, None]

            def epi_q(j, pss, pe_tok):
                s = j % 2
                cx.wait(nc.scalar, pe_tok)
                cx.wait(nc.scalar, q_free[s])
                a_tok = cx.act.mark(nc.scalar.copy(out=qs[s][:], in_=pss[0][:]))
                cx.wait(nc.sync, a_tok)
                q_free[s] = st[s].mark(nc.sync.dma_start(out=qT_dram[j * 128:(j + 1) * 128, :], in_=qs[s][:]))
                return a_tok

            ngt = max(1, min(DC, (16384 // DC) // 128))
            gemm(cx, act, DC, T, [wq], DC, ngt, wbufs, wsems, [[pq0[:]], [pq1[:]]], epi_q, wstate=wstate)
            if wo_next is not None:
                gemm_prefetch(cx, DC, [wo_next], DC, ngt, wbufs, wsems, wstate)
            barrier(cx)
        NTH = T // 512
        with (nc.sbuf_tensor(un("xa_qh0"), [128, HDC, T], BF16) as qh0,
              nc.sbuf_tensor(un("xa_pT0"), [128, NMC, 512], BF16) as pT0, nc.sbuf_tensor(un("xa_pT1"), [128, NMC, 512], BF16) as pT1,
              nc.sbuf_tensor(un("xa_rd0"), [128, 512], F32) as rd0,
              nc.psum_tensor(un("xa_ps0"), [128, NMC, 512], F32) as ps0, nc.psum_tensor(un("xa_ps1"), [128, NMC, 512], F32) as ps1,
              nc.psum_tensor(un("xa_pd0"), [128, 512], F32) as pd0, nc.psum_tensor(un("xa_pd1"), [128, 512], F32) as pd1,
              nc.psum_tensor(un("xa_po0"), [128, 512], F32) as po0, nc.psum_tensor(un("xa_po1"), [128, 512], F32) as po1):
            qhs, pTs, rds, pss_, pds, pos = [qh0, qh0], [pT0, pT1], [rd0, rd0], [ps0, ps1], [pd0, pd1], [po0, po1]
            ld = [dma_ev(cx, "xaqh0"), dma_ev(cx, "xaqh1")]
            qh_free = [None, None]
            ps_free = [None, None]
            pT_free = [None, None]
            pd_free = [None, None]
            rd_free = [None, None]
            po_free = [None, None]
            it = 0
            oi = 0
            for h in range(H):
                hb = 0
                cx.wait(nc.sync, qh_free[hb])
                l_tok = ld[hb].mark(nc.sync.dma_start(out=qhs[hb][:], in_=qT_dram.rearrange("(c p) t -> p c t", p=128)[:, h * HDC:(h + 1) * HDC, :]))
                cx.wait(nc.tensor, l_tok)
                for th in range(NTH):
                    s = it % 2
                    it += 1
                    tsl = slice(th * 512, (th + 1) * 512)
                    cx.wait(nc.tensor, ps_free[s])
                    for mc in range(NMC):
                        for dc in range(HDC):
                            ins = nc.tensor.matmul(pss_[s][:, mc, :], lhsT=kT[:, h * HDC + dc, mc * 128:(mc + 1) * 128],
                                                   rhs=qhs[hb][:, dc, tsl], start=(dc == 0), stop=(dc == HDC - 1))
                    s_tok = cx.pe.mark(ins)
                    cx.wait(nc.scalar, s_tok)
                    cx.wait(nc.scalar, pT_free[s])
                    e_tok = cx.act.mark(nc.scalar.activation(out=pTs[s][:], in_=pss_[s][:], func=AF.Exp, scale=scale))
                    ps_free[s] = e_tok
                    cx.wait(nc.tensor, e_tok)
                    cx.wait(nc.tensor, pd_free[s])
                    for mc in range(NMC):
                        ins = nc.tensor.matmul(pds[s][:], lhsT=C["ones_b"][:], rhs=pTs[s][:, mc, :], start=(mc == 0), stop=(mc == NMC - 1))
                    d_tok = cx.pe.mark(ins)
                    cx.wait(nc.vector, d_tok)
                    cx.wait(nc.vector, rd_free[s])
                    r_tok = cx.dve.mark(nc.vector.reciprocal(out=rds[s][:], in_=pds[s][:]))
                    pd_free[s] = r_tok
                    cx.wait(nc.vector, r_tok)
                    for dc in range(HDC):
                        o = oi % 2
                        oi += 1
                        cx.wait(nc.tensor, po_free[o])
                        for mc in range(NMC):
                            ins = nc.tensor.matmul(pos[o][:], lhsT=vtok[:, mc, h * HD + dc * 128:h * HD + (dc + 1) * 128],
                                                   rhs=pTs[s][:, mc, :], start=(mc == 0), stop=(mc == NMC - 1))
                        o_tok = cx.pe.mark(ins)
                        cx.wait(nc.vector, o_tok)
                        po_free[o] = cx.dve.mark(nc.vector.tensor_tensor(out=act[:, h * HDC + dc, tsl], in0=pos[o][:], in1=rds[s][:], op=ALU.mult))
                    pT_free[s] = o_tok
                    rd_free[s] = po_free[o]
                qh_free[hb] = s_tok
            barrier(cx)


def proj_raw_stats(cx, actbuf, T, DC, wo, rawT_dram, st_ps, C, wbufs, wsems, wstate):
    act = actbuf[:, 0:DC * T].rearrange("p (k t) -> p k t", k=DC)
    ngt = max(1, min(DC, (16384 // DC) // 128))
    gemm_raw_stats(cx, act, DC, T, 0, wo, DC, rawT_dram, st_ps, C, wbufs, wsems, wstate, ngt)


def inproj_phase(cx, actbuf, hT_dram, S, DC, w_in, NTF, NTR, projF, projR, wbufs, wsems, wstate, halves=None):
    nc = cx.nc
    TT = min(S, 1024)
    ld = dma_ev(cx, "ipld")
    ldp = []
    with (nc.sbuf_tensor(un("ip_b0"), [128, TT], BF16) as b0, nc.sbuf_tensor(un("ip_b1"), [128, TT], BF16) as b1,
          nc.sbuf_tensor(un("ip_f0"), [128, TT], F32) as f0, nc.sbuf_tensor(un("ip_f1"), [128, TT], F32) as f1,
          nc.psum_tensor(un("ip_p0"), [128, TT], F32) as p0, nc.psum_tensor(un("ip_p1"), [128, TT], F32) as p1):
        bs, fs = [b0, b1], [f0, f1]
        st = [dma_ev(cx, "ipst0"), dma_ev(cx, "ipst1")]
        free = [None, None]
        for th in range(S // TT):
            act = actbuf[:, 0:DC * TT].rearrange("p (k t) -> p k t", k=DC)
            k_toks = None
            if halves is not None:
                k0 = 0
                k_toks = {}
                while len(ldp) < len(halves[th]):
                    ldp.append(dma_ev(cx, f"ipldp{len(ldp)}"))
                for pi_, piece in enumerate(halves[th]):
                    nk = piece.shape[0] // 128
                    tok = ldp[pi_].mark(nc.sync.dma_start(out=act[:, k0:k0 + nk, :], in_=piece.rearrange("(k p) t -> p k t", p=128)))
                    k_toks[k0] = tok
                    k0 += nk
                tok = None
            else:
                src = hT_dram.rearrange("(k p) t -> p k t", p=128)[:, :, th * TT:(th + 1) * TT]
                tok = ld.mark(nc.sync.dma_start(out=act, in_=src))

            def epi(j, pss, pe_tok, th=th):
                s = j % 2
                cx.wait(nc.scalar, pe_tok)
                cx.wait(nc.scalar, free[s])
                if j < NTF:
                    a_tok = cx.act.mark(nc.scalar.copy(out=bs[s][:], in_=pss[0][:]))
                    cx.wait(nc.sync, a_tok)
                    free[s] = st[s].mark(nc.sync.dma_start(out=projF[j * 128:(j + 1) * 128, th * TT:(th + 1) * TT], in_=bs[s][:]))
                else:
                    a_tok = cx.act.mark(nc.scalar.copy(out=fs[s][:], in_=pss[0][:]))
                    cx.wait(nc.sync, a_tok)
                    jj = j - NTF
                    free[s] = st[s].mark(nc.sync.dma_start(out=projR[jj * 128:(jj + 1) * 128, th * TT:(th + 1) * TT], in_=fs[s][:]))
                return a_tok

            ngt = max(1, min(NTF + NTR, (16384 // DC) // 128))
            gemm(cx, act, DC, TT, [w_in], NTF + NTR, ngt, wbufs, wsems, [[p0[:]], [p1[:]]], epi, pre_tok=tok, wstate=wstate, k_toks=k_toks)
            if th + 1 < S // TT:
                gemm_prefetch(cx, DC, [w_in], NTF + NTR, ngt, wbufs, wsems, wstate)
            barrier(cx)


def fox_phase(cx, S, NFH, projF, fT_dram, fbias_sb, yT_dram, C):
    nc = cx.nc
    FW = NFH * 128
    NB = S // 128
    NP = S // 512
    scale = 128.0 ** -0.5
    NEG = -1.0e30
    with (nc.sbuf_tensor(un("fx_c"), [128, S], F32) as cT, nc.sbuf_tensor(un("fx_ones"), [128, S], F32) as onesS,
          nc.sbuf_tensor(un("fx_sel"), [128, 8, 128], F32) as sel, nc.sbuf_tensor(un("fx_ncs"), [128, NB, 8], F32) as ncs,
          nc.sbuf_tensor(un("fx_mask"), [128, 4, 512], F32) as mask, nc.sbuf_tensor(un("fx_CT"), [128, S], F32) as CT,
          nc.sbuf_tensor(un("fx_nb"), [128, 1], F32) as nbias):
        ld0 = dma_ev(cx, "fxld")
        t0 = ld0.mark(nc.sync.dma_start(out=cT[:], in_=fT_dram))
        d0 = cx.dve.mark(nc.vector.memset(onesS[:], 1.0))
        d0 = cx.dve.mark(nc.vector.tensor_scalar(out=nbias[:], in0=fbias_sb, scalar1=-1.0, scalar2=None, op0=ALU.mult))
        cx.wait(nc.gpsimd, d0)
        cx.pool.mark(nc.gpsimd.affine_select(out=sel[:], in_=onesS[:, 0:1024].rearrange("p (h m) -> p h m", h=8), pattern=[[-1, 8], [0, 128]],
                                             compare_op=ALU.is_equal, fill=0.0, base=0, channel_multiplier=1))
        pm = cx.pool.mark(nc.gpsimd.memset(mask[:], 0.0))
        cx.wait(nc.gpsimd, pm)
        p_tok = cx.pool.mark(nc.gpsimd.affine_select(out=mask[:], in_=mask[:], pattern=[[-128, 4], [1, 512]], compare_op=ALU.is_ge, fill=NEG,
                                                     base=0, channel_multiplier=-1))
        cx.wait(nc.scalar, t0)
        cx.wait(nc.scalar, d0)
        a_tok = cx.act.mark(nc.scalar.activation(out=cT[:], in_=cT[:], func=AF.Exp, scale=-1.0, bias=nbias[:]))
        cx.wait(nc.vector, a_tok)
        d_tok = cx.dve.mark(nc.vector.tensor_scalar(out=cT[:], in0=cT[:], scalar1=1.0, scalar2=None, op0=ALU.add))
        cx.wait(nc.scalar, d_tok)
        a_tok = cx.act.mark(nc.scalar.activation(out=cT[:], in_=cT[:], func=AF.Ln))
        cx.wait(nc.vector, a_tok)
        d_tok = cx.dve.mark(nc.vector.tensor_scalar(out=cT[:], in0=cT[:], scalar1=-1.0, scalar2=None, op0=ALU.mult))
        cx.wait(nc.vector, d_tok)
        d_tok = cx.dve.mark(nc.vector.tensor_tensor_scan(out=cT[:], data0=onesS[:], data1=cT[:], initial=0.0, op0=ALU.mult, op1=ALU.add))
        cx.wait(nc.tensor, d_tok)
        cx.wait(nc.tensor, p_tok)
        with nc.psum_tensor(un("fx_pt"), [128, NB, 8], F32) as ptc:
            for blk in range(NB):
                ins = nc.tensor.transpose(ptc[:, blk, :], cT[0:8, blk * 128:(blk + 1) * 128], C["ident_f"][0:8, 0:8])
            t_tok = cx.pe.mark(ins)
            cx.wait(nc.vector, t_tok)
            cx.dve.mark(nc.vector.tensor_scalar(out=ncs[:], in0=ptc[:], scalar1=-1.0, scalar2=None, op0=ALU.mult))
            barrier(cx)
        with A(nc) as al:
            qh, kh, vh = al.sb("fx_q", [128, S], BF16), al.sb("fx_k", [128, S], BF16), al.sb("fx_v", [128, S], BF16)
            vtok = al.sb("fx_vt", [128, NB, 128], BF16)
            NBF_ = 4
            DEFER = 2
            tms = [al.sb(f"fx_t{i}", [128, 512], F32) for i in range(NBF_)]
            pps = [al.sb(f"fx_p{i}", [128, 512], BF16) for i in range(NBF_)]
            rd = al.sb("fx_rd", [128, 512], F32)
            y0, y1 = al.sb("fx_y0", [128, 512], BF16), al.sb("fx_y1", [128, 512], BF16)
            pss_ = [al.ps(f"fx_ps{i}", [128, 512]) for i in range(NBF_)]
            pd, po, pc = al.ps("fx_pd", [128, 512]), al.ps("fx_po", [128, 512]), al.ps("fx_pc", [128, 512])
            pv = al.ps("fx_pv", [128, 4, 128], BF16)
            ys = [y0, y1]
            ldh = dma_ev(cx, "fxldh")
            sty = [dma_ev(cx, "fxsty0"), dma_ev(cx, "fxsty1")]
            it = 0
            yi = 0
            ps_free = [None] * NBF_
            tm_free = [None] * NBF_
            pp_free = [None] * NBF_
            y_free = [None, None]
            acc_free = None
            rd_free = None
            for h in range(NFH):
                barrier(cx)
                ldh.mark(nc.sync.dma_start(out=qh[:], in_=projF[h * 128:(h + 1) * 128, :]))
                ldh.mark(nc.sync.dma_start(out=kh[:], in_=projF[FW + h * 128:FW + (h + 1) * 128, :]))
                l_tok = ldh.mark(nc.sync.dma_start(out=vh[:], in_=projF[2 * FW + h * 128:2 * FW + (h + 1) * 128, :]))
                cx.wait(nc.tensor, l_tok)
                for p4 in range(NP):
                    ins = nc.tensor.matmul(pc[:], lhsT=sel[0:8, h, :], rhs=cT[0:8, p4 * 512:(p4 + 1) * 512], start=True, stop=True)
                    c_tok = cx.pe.mark(ins)
                    cx.wait(nc.scalar, c_tok)
                    a_tok = cx.act.mark(nc.scalar.copy(out=CT[:, p4 * 512:(p4 + 1) * 512], in_=pc[:]))
                    cx.wait(nc.tensor, a_tok)
                for b4 in range(NB // 4):
                    for i4 in range(4):
                        blk = b4 * 4 + i4
                        ins = nc.tensor.transpose(pv[:, i4, :], vh[:, blk * 128:(blk + 1) * 128], C["ident_b"][:])
                    v_tok = cx.pe.mark(ins)
                    cx.wait(nc.scalar, v_tok)
                    a_tok = cx.act.mark(nc.scalar.copy(out=vtok[:, b4 * 4:(b4 + 1) * 4, :], in_=pv[:]))
                    cx.wait(nc.tensor, a_tok)
                cx.wait(nc.vector, a_tok)
                for qp in range(NP):
                    qsl = slice(qp * 512, (qp + 1) * 512)
                    nkb = qp * 4 + 4
                    cx.wait(nc.tensor, acc_free)
                    pendq = []

                    def do_pv(last):
                        pe_, ps_, pkb = pendq.pop(0)
                        cx.wait(nc.tensor, pe_)
                        nc.tensor.matmul(pd[:], lhsT=C["ones_b"][:], rhs=pps[ps_][:], start=(pkb == 0), stop=last)
                        tk = cx.pe.mark(nc.tensor.matmul(po[:], lhsT=vtok[:, pkb, :], rhs=pps[ps_][:], start=(pkb == 0), stop=last))
                        pp_free[ps_] = tk
                        return tk

                    for kb in range(nkb):
                        s = it % NBF_
                        it += 1
                        cx.wait(nc.tensor, ps_free[s])
                        s_tok = cx.pe.mark(nc.tensor.matmul(pss_[s][:], lhsT=kh[:, kb * 128:(kb + 1) * 128], rhs=qh[:, qsl], start=True, stop=True))
                        if len(pendq) >= DEFER:
                            do_pv(False)
                        cx.wait(nc.vector, s_tok)
                        cx.wait(nc.vector, tm_free[s])
                        d_tok = cx.dve.mark(nc.vector.scalar_tensor_tensor(out=tms[s][:], in0=pss_[s][:], scalar=scale, in1=CT[:, qsl], op0=ALU.mult, op1=ALU.add))
                        ps_free[s] = d_tok
                        off = kb - qp * 4
                        if off >= 0:
                            cx.wait(nc.vector, d_tok)
                            d_tok = cx.dve.mark(nc.vector.tensor_tensor(out=tms[s][:], in0=tms[s][:], in1=mask[:, off, :], op=ALU.add))
                        cx.wait(nc.scalar, d_tok)
                        cx.wait(nc.scalar, pp_free[s])
                        e_tok = cx.act.mark(nc.scalar.activation(out=pps[s][:], in_=tms[s][:], func=AF.Exp, bias=ncs[:, kb, h:h + 1]))
                        tm_free[s] = e_tok
                        pendq.append((e_tok, s, kb))
                    while pendq:
                        o_tok = do_pv(len(pendq) == 1)
                    cx.wait(nc.vector, o_tok)
                    cx.wait(nc.vector, rd_free)
                    r_tok = cx.dve.mark(nc.vector.reciprocal(out=rd[:], in_=pd[:]))
                    cx.wait(nc.vector, r_tok)
                    ys_ = yi % 2
                    yi += 1
                    cx.wait(nc.vector, y_free[ys_])
                    y_tok = cx.dve.mark(nc.vector.tensor_tensor(out=ys[ys_][:], in0=po[:], in1=rd[:], op=ALU.mult))
                    acc_free = y_tok
                    rd_free = y_tok
                    cx.wait(nc.sync, y_tok)
                    y_free[ys_] = sty[ys_].mark(nc.sync.dma_start(out=yT_dram[h * 128:(h + 1) * 128, qsl], in_=ys[ys_][:]))
            barrier(cx)


class Ser:
    def __init__(self, cx, tok=None):
        self.cx = cx
        self.last = tok

    def run(self, eng, fn, extra=()):
        cx = self.cx
        cx.wait(eng, self.last)
        for t in extra:
            cx.wait(eng, t)
        self.last = cx.evof(eng).mark(fn())
        return self.last


class Dep:
    def __init__(self, cx):
        self.cx = cx
        self.w = {}
        self.r = {}

    def _pre(self, eng, reads, writes, extra):
        cx = self.cx
        for k in reads:
            cx.wait(eng, self.w.get(k))
        for k in writes:
            cx.wait(eng, self.w.get(k))
            for t in self.r.get(k, ()):
                cx.wait(eng, t)
        for t in extra:
            cx.wait(eng, t)

    def _post(self, tok, reads, writes):
        for k in reads:
            self.r.setdefault(k, []).append(tok)
        for k in writes:
            self.w[k] = tok
            self.r[k] = []
        return tok

    def run(self, eng, fn, reads=(), writes=(), extra=()):
        self._pre(eng, reads, writes, extra)
        return self._post(self.cx.evof(eng).mark(fn()), reads, writes)

    def dma(self, eng, ev, fn, reads=(), writes=(), extra=()):
        self._pre(eng, reads, writes, extra)
        return self._post(ev.mark(fn()), reads, writes)

    def all_done(self, engines):
        cx = self.cx
        for e in engines:
            for ev in (cx.pe, cx.act, cx.dve, cx.pool):
                cx.wait(e, (ev, ev.n))


NVT = 10


def rwkv_phase(cx, S, NRT, projR, rvec_dram, wup_dram, aup_dram, gup_dram, yT_dram, y_row0, C, gn_eps, after_round=None):
    nc = cx.nc
    RW = NRT * 128
    NCH = S // 64
    NP = S // 512
    V, ACT_, PE, POOL = nc.vector, nc.scalar, nc.tensor, nc.gpsimd
    with A(nc) as al:
        rvec = al.sb("rk_vec", [128, NRT * NVT + 4], F32)
        omka = al.sb("rk_omka", [128, NRT], F32)
        wup, aup = al.sb("rk_wup", [128, RW], BF16), al.sb("rk_aup", [128, RW], BF16)
        gup = al.sb("rk_gup", [128, 2, RW], BF16)
        lxw, lxa, lxg = al.sb("rk_lxw", [128, S], BF16), al.sb("rk_lxa", [128, S], BF16), al.sb("rk_lxg", [128, 2, S], BF16)
        seg = al.sb("rk_seg", [128, S], BF16)
        bones = al.sb("rk_bones", [128, 128], F32)
        idrep = al.sb("rk_idrep", [128, 8, 64], F32)
        m1, m2 = al.sb("rk_m1", [128, 4, 128], F32), al.sb("rk_m2", [128, 4, 128], F32)
        m3 = al.sb("rk_m3", [128, 4, 64], F32)
        Fb = [al.sb(f"rk_F{i}", [128, S], F32) for i in range(8)]
        VB, KT, BT = al.sb("rk_VB", [128, S], BF16), al.sb("rk_KT", [128, S], BF16), al.sb("rk_BT", [128, S], BF16)
        KH, NBH = al.sb("rk_KH", [128, S], BF16), al.sb("rk_NBH", [128, S], BF16)
        KR = al.sb("rk_KR", [128, NCH, 2, 64], BF16)
        Vtok, Khat, nBhat = al.sb("rk_Vt", [128, NCH, 64], BF16), al.sb("rk_Kh", [128, NCH, 64], BF16), al.sb("rk_nBh", [128, NCH, 64], BF16)
        NTB, AB2 = al.sb("rk_NTB", [128, NCH, 128], BF16), al.sb("rk_AB2", [128, NCH, 128], BF16)
        Nb = [al.sb("rk_Na", [128, NCH, 64], BF16), al.sb("rk_Nb", [128, NCH, 64], BF16)]
        NTb = [al.sb("rk_NTa", [128, NCH, 64], BF16), al.sb("rk_NTb", [128, NCH, 64], BF16)]
        TTb = [al.sb("rk_TTa", [128, NCH, 64], BF16), al.sb("rk_TTb", [128, NCH, 64], BF16)]
        PC, nPC = al.sb("rk_PC", [128, NCH], F32), al.sb("rk_nPC", [128, NCH], F32)
        M32, Mb = al.sb("rk_M32", [128, 64], F32), al.sb("rk_Mb", [128, 64], BF16)
        Xs, Us = al.sb("rk_Xs", [128, 64], BF16), al.sb("rk_Us", [128, 64], BF16)
        ysb = al.sb("rk_ysb", [128, S], BF16)
        ld = dma_ev(cx, "rkld")
        ld3 = [dma_ev(cx, "rkld3a"), dma_ev(cx, "rkld3b"), dma_ev(cx, "rkld3c")]
        ldw = dma_ev(cx, "rkldw")
        sty = dma_ev(cx, "rksty")
        ser = Ser(cx)
        t_vec = ld.mark(nc.sync.dma_start(out=rvec[:], in_=rvec_dram))
        ldw.mark(nc.gpsimd.dma_start(out=wup[:], in_=wup_dram))
        ldw.mark(nc.gpsimd.dma_start(out=aup[:], in_=aup_dram))
        t_w = ldw.mark(nc.gpsimd.dma_start(out=gup[:], in_=gup_dram.rearrange("(c p) n -> p c n", p=128)))
        ser.run(V, lambda: V.memset(seg[:], 1.0))
        ser.run(V, lambda: V.memset(seg[:].rearrange("p (c j) -> p c j", j=64)[:, :, 0:1], 0.0))
        ser.run(V, lambda: V.memset(bones[:], 0.0))
        ser.run(V, lambda: V.memset(bones[0:64, 0:64], 1.0))
        ser.run(V, lambda: V.memset(bones[64:128, 64:128], 1.0))
        ser.run(V, lambda: V.memset(idrep[:], 1.0))
        ser.run(V, lambda: V.memset(m1[:], -1.0))
        ser.run(V, lambda: V.memset(m2[:], 1.0))
        ser.run(V, lambda: V.memset(m3[:], -1.0))
        for hs in range(2):
            hp = slice(hs * 64, hs * 64 + 64)
            ser.run(POOL, lambda: POOL.affine_select(out=idrep[hp], in_=idrep[hp], pattern=[[0, 8], [-1, 64]], compare_op=ALU.is_equal,
                                                      fill=0.0, base=0, channel_multiplier=1))
            for mm_ in (m1, m2):
                ser.run(POOL, lambda: POOL.affine_select(out=mm_[hp, :, 0:64], in_=mm_[hp, :, 0:64], pattern=[[0, 4], [1, 64]],
                                                          compare_op=ALU.is_gt, fill=0.0, base=0, channel_multiplier=-1))
                ser.run(POOL, lambda: POOL.affine_select(out=mm_[hp, :, 64:128], in_=mm_[hp, :, 64:128], pattern=[[0, 4], [1, 64]],
                                                          compare_op=ALU.is_ge, fill=0.0, base=0, channel_multiplier=-1))
            ser.run(POOL, lambda: POOL.affine_select(out=m3[hp], in_=m3[hp], pattern=[[0, 4], [-1, 64]], compare_op=ALU.is_gt,
                                                      fill=0.0, base=0, channel_multiplier=1))
        GV = NRT * NVT
        ser.run(V, lambda: V.tensor_scalar(out=omka[:], in0=rvec[:, 0:GV].rearrange("p (t v) -> p t v", v=NVT)[:, :, 6], scalar1=-1.0, scalar2=1.0,
                                           op0=ALU.mult, op1=ALU.add), extra=[t_vec])

        def tshift(src_rows, mu_ap, Z, T0, T1):
            t = ld.mark(nc.sync.dma_start(out=T0[:], in_=projR[src_rows, :]))
            ser.run(V, lambda: V.tensor_tensor(out=T1[:, 1:S], in0=T0[:, 0:S - 1], in1=T0[:, 1:S], op=ALU.subtract), extra=[t])
            ser.run(V, lambda: V.tensor_scalar(out=T1[:, 0:1], in0=T0[:, 0:1], scalar1=-1.0, scalar2=None, op0=ALU.mult))
            ser.run(V, lambda: V.scalar_tensor_tensor(out=Z, in0=T1[:], scalar=mu_ap, in1=T0[:], op0=ALU.mult, op1=ALU.add))

        r0 = 3 * RW
        cx.wait(nc.sync, ser.last)
        tshift(slice(r0, r0 + 128), rvec[:, GV + 0:GV + 1], Fb[2][:], Fb[0], Fb[1])
        ser.run(ACT_, lambda: ACT_.activation(out=lxw[:], in_=Fb[2][:], func=AF.Tanh))
        cx.wait(nc.sync, ser.last)
        tshift(slice(r0 + 128, r0 + 256), rvec[:, GV + 1:GV + 2], Fb[2][:], Fb[0], Fb[1])
        ser.run(ACT_, lambda: ACT_.copy(out=lxa[:], in_=Fb[2][:]))
        for c2 in range(2):
            cx.wait(nc.sync, ser.last)
            tshift(slice(r0 + 256 + c2 * 128, r0 + 384 + c2 * 128), rvec[:, GV + 2 + c2:GV + 3 + c2], Fb[2][:], Fb[0], Fb[1])
            ser.run(ACT_, lambda: ACT_.activation(out=lxg[:, c2, :], in_=Fb[2][:], func=AF.Sigmoid))
        cx.wait(PE, t_w)
        for e_ in (PE, ACT_, V, nc.sync, POOL):
            for ev_ in (cx.pe, cx.act, cx.dve, cx.pool):
                cx.wait(e_, (ev_, ev_.n))
        for ti in range(NRT):
            vc = lambda i: rvec[:, ti * NVT + i:ti * NVT + i + 1]
            csl = slice(ti * 128, (ti + 1) * 128)
            F0, F1, F2, F3, F4, F5, F6, F7 = Fb
            dp = Dep(cx)
            NPc = NP

            def KK(name, p4=None):
                return [(name, p4)] if p4 is not None else [(name, i) for i in range(NPc)]

            def shift2(rows, mu_ap, Zn, T0n):
                Z, T0, T1 = Fmap[Zn], Fmap[T0n], Fmap["F2"]
                ldx = ld3[{"F0": 0, "F5": 1, "F6": 2}[T0n]]
                dp.dma(nc.sync, ldx, lambda: nc.sync.dma_start(out=T0[:], in_=projR[rows, :]), writes=KK(T0n))
                dp.run(V, lambda: V.tensor_tensor(out=T1[:, 1:S], in0=T0[:, 0:S - 1], in1=T0[:, 1:S], op=ALU.subtract), reads=KK(T0n), writes=KK("F2"))
                dp.run(V, lambda: V.tensor_scalar(out=T1[:, 0:1], in0=T0[:, 0:1], scalar1=-1.0, scalar2=None, op0=ALU.mult), reads=KK(T0n), writes=KK("F2"))
                dp.run(V, lambda: V.scalar_tensor_tensor(out=Z[:], in0=T1[:], scalar=mu_ap, in1=T0[:], op0=ALU.mult, op1=ALU.add),
                       reads=KK("F2") + KK(T0n), writes=KK(Zn))

            Fmap = {f"F{i}": Fb[i] for i in range(8)}
            shift2(slice(ti * 128, ti * 128 + 128), vc(0), "F1", "F0")
            shift2(slice(RW + ti * 128, RW + ti * 128 + 128), vc(1), "F3", "F5")
            shift2(slice(2 * RW + ti * 128, 2 * RW + ti * 128 + 128), vc(2), "F4", "F6")
            with A(nc) as pl:
                pAs = [pl.ps("rk_pA0", [128, 512]), pl.ps("rk_pA1", [128, 512])]
                pi = [0]

                def nextp():
                    pi[0] += 1
                    i = pi[0] % 2
                    return pAs[i], [("pA", i)]

                for p4 in range(NP):
                    psl = slice(p4 * 512, (p4 + 1) * 512)
                    pA, pk = nextp()
                    dp.run(PE, lambda: PE.matmul(pA[:], lhsT=wup[:, csl], rhs=lxw[:, psl], start=True, stop=True), writes=pk)
                    dp.run(ACT_, lambda: ACT_.activation(out=F5[:, psl], in_=pA[:], func=AF.Sigmoid, bias=vc(3)), reads=pk, writes=KK("F5", p4))
                    pA, pk = nextp()
                    dp.run(PE, lambda: PE.matmul(pA[:], lhsT=aup[:, csl], rhs=lxa[:, psl], start=True, stop=True), writes=pk)
                    dp.run(ACT_, lambda: ACT_.activation(out=F6[:, psl], in_=pA[:], func=AF.Sigmoid, bias=vc(4)), reads=pk, writes=KK("F6", p4))
                    pA, pk = nextp()
                    dp.run(PE, lambda: PE.matmul(pA[:], lhsT=gup[:, 0, csl], rhs=lxg[:, 0, psl], start=True, stop=False), writes=pk)
                    dp.run(PE, lambda: PE.matmul(pA[:], lhsT=gup[:, 1, csl], rhs=lxg[:, 1, psl], start=False, stop=True), writes=pk)
                    dp.run(ACT_, lambda: ACT_.copy(out=F7[:, psl], in_=pA[:]), reads=pk, writes=KK("F7", p4))
                dp.run(V, lambda: V.tensor_scalar(out=F5[:], in0=F5[:], scalar1=-float(np.exp(-0.5)), scalar2=None, op0=ALU.mult),
                       reads=KK("F5"), writes=KK("F5"))
                dp.run(V, lambda: V.tensor_scalar(out=F0[:], in0=F3[:], scalar1=vc(5), scalar2=None, op0=ALU.mult), reads=KK("F3"), writes=KK("F0"))
                dp.run(ACT_, lambda: ACT_.activation(out=F2[:], in_=F0[:], func=AF.Square), reads=KK("F0"), writes=KK("F2"))
                for p4 in range(NP):
                    psl = slice(p4 * 512, (p4 + 1) * 512)
                    pA, pk = nextp()
                    dp.run(PE, lambda: PE.matmul(pA[:], lhsT=bones[:], rhs=F2[:, psl], start=True, stop=True), reads=KK("F2", p4), writes=pk)
                    dp.run(V, lambda: V.tensor_scalar(out=F2[:, psl], in0=pA[:], scalar1=1e-24, scalar2=None, op0=ALU.max), reads=pk, writes=KK("F2", p4))
                dp.run(ACT_, lambda: ACT_.activation(out=F2[:], in_=F2[:], func=AF.Ln), reads=KK("F2"), writes=KK("F2"))
                dp.run(ACT_, lambda: ACT_.activation(out=F2[:], in_=F2[:], func=AF.Exp, scale=-0.5), reads=KK("F2"), writes=KK("F2"))
                dp.run(V, lambda: V.tensor_tensor(out=F0[:], in0=F0[:], in1=F2[:], op=ALU.mult), reads=KK("F0") + KK("F2"), writes=KK("F0"))
                dp.run(V, lambda: V.tensor_scalar(out=F2[:], in0=F6[:], scalar1=vc(6), scalar2=omka[:, ti:ti + 1], op0=ALU.mult, op1=ALU.add),
                       reads=KK("F6"), writes=KK("F2"))
                dp.run(V, lambda: V.tensor_tensor(out=F3[:], in0=F3[:], in1=F2[:], op=ALU.mult), reads=KK("F3") + KK("F2"), writes=KK("F3"))
                dp.run(V, lambda: V.scalar_tensor_tensor(out=F2[:], in0=F1[:], scalar=vc(7), in1=F3[:], op0=ALU.mult, op1=ALU.mult),
                       reads=KK("F1") + KK("F3"), writes=KK("F2"))
                for p4 in range(NP):
                    psl = slice(p4 * 512, (p4 + 1) * 512)
                    pA, pk = nextp()
                    dp.run(PE, lambda: PE.matmul(pA[:], lhsT=bones[:], rhs=F2[:, psl], start=True, stop=True), reads=KK("F2", p4), writes=pk)
                    dp.run(V, lambda: V.tensor_tensor(out=F2[:, psl], in0=pA[:], in1=F4[:, psl], op=ALU.mult), reads=pk + KK("F4", p4), writes=KK("F2", p4))
                dp.run(V, lambda: V.tensor_tensor(out=F6[:], in0=F0[:], in1=F6[:], op=ALU.mult), reads=KK("F0") + KK("F6"), writes=KK("F6"))
                dp.run(ACT_, lambda: ACT_.copy(out=VB[:], in_=F4[:]), reads=KK("F4"), writes=[("VB", 0)])
                dp.run(V, lambda: V.tensor_tensor_scan(out=F4[:], data0=seg[:], data1=F5[:], initial=0.0, op0=ALU.mult, op1=ALU.add),
                       reads=KK("F5"), writes=KK("F4"))
                clv = F4[:].rearrange("p (c j) -> p c j", j=64)
                dp.run(ACT_, lambda: ACT_.activation(out=PC[:], in_=clv[:, :, 63], func=AF.Exp), reads=KK("F4"), writes=[("PC", 0)])
                dp.run(V, lambda: V.tensor_scalar(out=nPC[:], in0=PC[:], scalar1=-1.0, scalar2=None, op0=ALU.mult), reads=[("PC", 0)], writes=[("nPC", 0)])
                dp.run(V, lambda: V.tensor_tensor(out=F5[:], in0=F4[:], in1=F5[:], op=ALU.subtract), reads=KK("F4") + KK("F5"), writes=KK("F5"))
                dp.run(ACT_, lambda: ACT_.activation(out=F5[:], in_=F5[:], func=AF.Exp), reads=KK("F5"), writes=KK("F5"))
                dp.run(V, lambda: V.tensor_tensor(out=KR[:, :, 0, :], in0=F0[:].rearrange("p (c j) -> p c j", j=64),
                                                  in1=F5[:].rearrange("p (c j) -> p c j", j=64), op=ALU.mult), reads=KK("F0") + KK("F5"), writes=[("KR", 0)])
                dp.run(ACT_, lambda: ACT_.activation(out=F5[:], in_=F4[:], func=AF.Exp), reads=KK("F4"), writes=KK("F5"))
                dp.run(V, lambda: V.tensor_tensor(out=KR[:, :, 1, :], in0=F1[:].rearrange("p (c j) -> p c j", j=64),
                                                  in1=F5[:].rearrange("p (c j) -> p c j", j=64), op=ALU.mult), reads=KK("F1") + KK("F5"), writes=[("KR", 1)])
                dp.run(ACT_, lambda: ACT_.activation(out=F5[:], in_=F4[:], func=AF.Exp, scale=-1.0), reads=KK("F4"), writes=KK("F5"))
                dp.run(V, lambda: V.tensor_tensor(out=F3[:], in0=F3[:], in1=F5[:], op=ALU.mult), reads=KK("F3") + KK("F5"), writes=KK("F3"))
                dp.run(V, lambda: V.tensor_tensor(out=F6[:], in0=F6[:], in1=F5[:], op=ALU.mult), reads=KK("F6") + KK("F5"), writes=KK("F6"))
                dp.run(ACT_, lambda: ACT_.copy(out=KT[:], in_=F3[:]), reads=KK("F3"), writes=[("KT", 0)])
                dp.run(ACT_, lambda: ACT_.copy(out=BT[:], in_=F6[:]), reads=KK("F6"), writes=[("BT", 0)])
                dp.run(V, lambda: V.tensor_tensor(out=KH[:].rearrange("p (c j) -> p c j", j=64), in0=F3[:].rearrange("p (c j) -> p c j", j=64),
                                                  in1=PC[:].unsqueeze(2).to_broadcast([128, NCH, 64]), op=ALU.mult), reads=KK("F3") + [("PC", 0)], writes=[("KH", 0)])
                dp.run(V, lambda: V.tensor_tensor(out=NBH[:].rearrange("p (c j) -> p c j", j=64), in0=F6[:].rearrange("p (c j) -> p c j", j=64),
                                                  in1=nPC[:].unsqueeze(2).to_broadcast([128, NCH, 64]), op=ALU.mult), reads=KK("F6") + [("nPC", 0)], writes=[("NBH", 0)])
                dp.all_done([PE, ACT_, V])
            ser = Ser(cx, (cx.dve, cx.dve.n))
            prep_tok = ser.last
            YT = F1
            with A(nc) as pl:
                pT = [pl.ps("rk_pT0", [128, 8, 64]), pl.ps("rk_pT1", [128, 8, 64])]
                pT_free = [None, None]
                cx.wait(PE, prep_tok)
                u = 0
                for (src, dst) in ((VB, Vtok), (KH, Khat), (NBH, nBhat)):
                    for g8 in range(NCH // 8):
                        s = u % 2
                        u += 1
                        cx.wait(PE, pT_free[s])
                        for c8 in range(8):
                            c = g8 * 8 + c8
                            for hs in range(2):
                                hp = slice(hs * 64, hs * 64 + 64)
                                ins = PE.matmul(pT[s][hp, c8, :], lhsT=src[hp, c * 64:(c + 1) * 64], rhs=C["ident_b"][hp, hs * 64:hs * 64 + 64],
                                                start=True, stop=True)
                        p_tok = cx.pe.mark(ins)
                        cx.wait(ACT_, p_tok)
                        pT_free[s] = cx.act.mark(ACT_.copy(out=dst[:, g8 * 8:(g8 + 1) * 8, :], in_=pT[s][:]))
                tm_tok = pT_free[(u - 1) % 2]
                tm_tok2 = pT_free[u % 2]
            with A(nc) as pl:
                p1 = [pl.ps("rk_p1a", [128, 4, 128]), pl.ps("rk_p1b", [128, 4, 128])]
                p2 = [pl.ps("rk_p2a", [128, 4, 128]), pl.ps("rk_p2b", [128, 4, 128])]
                p3 = [pl.ps("rk_p3a", [128, 4, 64]), pl.ps("rk_p3b", [128, 4, 64])]
                pfree = [None, None]
                psum_fence(cx)
                cx.wait(V, tm_tok)
                cx.wait(V, tm_tok2)
                for g4 in range(NCH // 4):
                    s = g4 % 2
                    cx.wait(PE, pfree[s])
                    for c4 in range(4):
                        c = g4 * 4 + c4
                        tsl = slice(c * 64, (c + 1) * 64)
                        for hs in range(2):
                            hp = slice(hs * 64, hs * 64 + 64)
                            PE.matmul(p1[s][hp, c4, :], lhsT=BT[hp, tsl], rhs=KR[hp, c, :, :], start=True, stop=True)
                            PE.matmul(p2[s][hp, c4, :], lhsT=KT[hp, tsl], rhs=KR[hp, c, :, :], start=True, stop=True)
                            ins = PE.matmul(p3[s][hp, c4, :], lhsT=KR[hp, c, 0, :], rhs=BT[hp, tsl], start=True, stop=True)
                    p_tok = cx.pe.mark(ins)
                    cx.wait(V, p_tok)
                    gsl = slice(g4 * 4, g4 * 4 + 4)
                    V.tensor_tensor(out=NTB[:, gsl, :], in0=p1[s][:], in1=m1[:], op=ALU.mult)
                    V.tensor_tensor(out=AB2[:, gsl, :], in0=p2[s][:], in1=m2[:], op=ALU.mult)
                    pfree[s] = cx.dve.mark(V.tensor_tensor(out=Nb[0][:, gsl, :], in0=p3[s][:], in1=m3[:], op=ALU.mult))
                ab_tok = pfree[(NCH // 4 - 1) % 2]
            NG8 = NCH // 8
            with A(nc) as pl:
                pN = [pl.ps("rk_pNa", [128, 8, 64]), pl.ps("rk_pNb", [128, 8, 64])]
                pNT = [pl.ps("rk_pNTa", [128, 8, 64]), pl.ps("rk_pNTb", [128, 8, 64])]
                pTT = [pl.ps("rk_pTa", [128, 8, 64]), pl.ps("rk_pTb", [128, 8, 64])]
                psum_fence(cx)
                cx.wait(V, ab_tok)
                for g8 in range(NG8):
                    gsl = slice(g8 * 8, g8 * 8 + 8)
                    V.tensor_copy(out=NTb[0][:, gsl, :], in_=NTB[:, gsl, 0:64])
                    t0_tok = cx.dve.mark(V.tensor_tensor(out=TTb[0][:, gsl, :], in0=NTB[:, gsl, 0:64], in1=idrep[:], op=ALU.add))
                cur = 0
                lvl_tok = t0_tok
                pN_free, pNT_free, pTT_free = [None, None], [None, None], [None, None]
                for k in range(5):
                    nxt = 1 - cur
                    cx.wait(PE, lvl_tok)
                    n_toks = []
                    for g8 in range(NG8):
                        s = g8 % 2
                        gsl = slice(g8 * 8, g8 * 8 + 8)
                        cx.wait(PE, pN_free[s])
                        cx.wait(PE, pNT_free[s])
                        for c8 in range(8):
                            c = g8 * 8 + c8
                            for hs in range(2):
                                hp = slice(hs * 64, hs * 64 + 64)
                                ins = PE.matmul(pN[s][hp, c8, :], lhsT=NTb[cur][hp, c, :], rhs=Nb[cur][hp, c, :], start=True, stop=True)
                                if k < 4:
                                    ins = PE.matmul(pNT[s][hp, c8, :], lhsT=Nb[cur][hp, c, :], rhs=NTb[cur][hp, c, :], start=True, stop=True)
                        p_tok = cx.pe.mark(ins)
                        cx.wait(ACT_, p_tok)
                        a_tok = cx.act.mark(ACT_.copy(out=Nb[nxt][:, gsl, :], in_=pN[s][:]))
                        pN_free[s] = a_tok
                        n_toks.append(a_tok)
                        if k < 4:
                            cx.wait(V, p_tok)
                            pNT_free[s] = cx.dve.mark(V.tensor_copy(out=NTb[nxt][:, gsl, :], in_=pNT[s][:]))
                    for g8 in range(NG8):
                        s = g8 % 2
                        gsl = slice(g8 * 8, g8 * 8 + 8)
                        cx.wait(PE, n_toks[g8])
                        cx.wait(PE, pTT_free[s])
                        for c8 in range(8):
                            c = g8 * 8 + c8
                            for hs in range(2):
                                hp = slice(hs * 64, hs * 64 + 64)
                                ins = PE.matmul(pTT[s][hp, c8, :], lhsT=Nb[nxt][hp, c, :], rhs=TTb[cur][hp, c, :], start=True, stop=True)
                        p_tok = cx.pe.mark(ins)
                        cx.wait(V, p_tok)
                        pTT_free[s] = cx.dve.mark(V.tensor_tensor(out=TTb[nxt][:, gsl, :], in0=pTT[s][:], in1=TTb[cur][:, gsl, :], op=ALU.add))
                    lvl_tok = pTT_free[(NG8 - 1) % 2]
                    cur = nxt
                TT = TTb[cur]
                inv_tok = (cx.dve, cx.dve.n)
            with A(nc) as pl:
                pX, pU, pY, pM = pl.ps("rk_pX", [128, 64]), pl.ps("rk_pU", [128, 64]), pl.ps("rk_pY", [128, 64]), pl.ps("rk_pM", [128, 64])
                cx.wait(V, inv_tok)
                V.memset(M32[:], 0.0)
                m_tok = cx.dve.mark(V.memset(Mb[:], 0.0))
                cx.wait(PE, inv_tok)
                cx.wait(PE, (cx.act, cx.act.n))
                y_tok = None
                xs_tok = None
                us_tok = None
                mb_tok = m_tok
                m32_tok = None
                for c in range(NCH):
                    for hs in range(2):
                        hp = slice(hs * 64, hs * 64 + 64)
                        PE.matmul(pX[hp, :], lhsT=AB2[hp, c, 0:64], rhs=Vtok[hp, c, :], start=True, stop=False)
                    cx.wait(PE, mb_tok)
                    for hs in range(2):
                        hp = slice(hs * 64, hs * 64 + 64)
                        ins = PE.matmul(pX[hp, :], lhsT=KR[hp, c, 0, :], rhs=Mb[hp, :], start=False, stop=True)
                    x_tok = cx.pe.mark(ins)
                    cx.wait(ACT_, x_tok)
                    xs_tok = cx.act.mark(ACT_.copy(out=Xs[:], in_=pX[:]))
                    cx.wait(PE, xs_tok)
                    for hs in range(2):
                        hp = slice(hs * 64, hs * 64 + 64)
                        ins = PE.matmul(pU[hp, :], lhsT=TT[hp, c, :], rhs=Xs[hp, :], start=True, stop=True)
                    u_tok = cx.pe.mark(ins)
                    cx.wait(V, u_tok)
                    us_tok = cx.dve.mark(V.tensor_copy(out=Us[:], in_=pU[:]))
                    cx.wait(PE, y_tok)
                    for hs in range(2):
                        hp = slice(hs * 64, hs * 64 + 64)
                        PE.matmul(pY[hp, :], lhsT=Mb[hp, :], rhs=KR[hp, c, 1, :], start=True, stop=False)
                        PE.matmul(pY[hp, :], lhsT=Vtok[hp, c, :], rhs=AB2[hp, c, 64:128], start=False, stop=False)
                    cx.wait(PE, us_tok)
                    for hs in range(2):
                        hp = slice(hs * 64, hs * 64 + 64)
                        ins = PE.matmul(pY[hp, :], lhsT=Us[hp, :], rhs=NTB[hp, c, 64:128], start=False, stop=True)
                    yp_tok = cx.pe.mark(ins)
                    cx.wait(PE, m32_tok)
                    for hs in range(2):
                        hp = slice(hs * 64, hs * 64 + 64)
                        PE.matmul(pM[hp, :], lhsT=Khat[hp, c, :], rhs=Vtok[hp, c, :], start=True, stop=False)
                        ins = PE.matmul(pM[hp, :], lhsT=nBhat[hp, c, :], rhs=Us[hp, :], start=False, stop=True)
                    mp_tok = cx.pe.mark(ins)
                    cx.wait(ACT_, yp_tok)
                    y_tok = cx.act.mark(ACT_.copy(out=YT[:, c * 64:(c + 1) * 64], in_=pY[:]))
                    cx.wait(V, mp_tok)
                    mb_tok = cx.dve.mark(V.scalar_tensor_tensor(out=Mb[:], in0=M32[:], scalar=PC[:, c:c + 1], in1=pM[:], op0=ALU.mult, op1=ALU.add))
                    m32_tok = cx.dve.mark(V.scalar_tensor_tensor(out=M32[:], in0=M32[:], scalar=PC[:, c:c + 1], in1=pM[:], op0=ALU.mult, op1=ALU.add))
                loop_tok = m32_tok
            ser = Ser(cx, loop_tok)
            with A(nc) as pl:
                pA = pl.ps("rk_pB", [128, 512])
                F0, F3 = Fb[0], Fb[3]
                for p4 in range(NP):
                    psl = slice(p4 * 512, (p4 + 1) * 512)
                    ser.run(PE, lambda: PE.matmul(pA[:], lhsT=bones[:], rhs=YT[:, psl], start=True, stop=True), extra=[y_tok])
                    ser.run(V, lambda: V.scalar_tensor_tensor(out=YT[:, psl], in0=pA[:], scalar=-1.0 / 64, in1=YT[:, psl], op0=ALU.mult, op1=ALU.add))
                ser.run(ACT_, lambda: ACT_.activation(out=F0[:], in_=YT[:], func=AF.Square))
                for p4 in range(NP):
                    psl = slice(p4 * 512, (p4 + 1) * 512)
                    ser.run(PE, lambda: PE.matmul(pA[:], lhsT=bones[:], rhs=F0[:, psl], start=True, stop=True))
                    ser.run(V, lambda: V.tensor_scalar(out=F0[:, psl], in0=pA[:], scalar1=1.0 / 64, scalar2=gn_eps, op0=ALU.mult, op1=ALU.add))
                ser.run(ACT_, lambda: ACT_.activation(out=F0[:], in_=F0[:], func=AF.Ln))
                ser.run(ACT_, lambda: ACT_.activation(out=F0[:], in_=F0[:], func=AF.Exp, scale=-0.5))
                ser.run(V, lambda: V.tensor_tensor(out=YT[:], in0=YT[:], in1=F0[:], op=ALU.mult))
                ser.run(V, lambda: V.tensor_scalar(out=YT[:], in0=YT[:], scalar1=vc(8), scalar2=vc(9), op0=ALU.mult, op1=ALU.add))
                ser.run(V, lambda: V.tensor_tensor(out=YT[:], in0=YT[:], in1=Fb[2][:], op=ALU.add))
                ser.run(V, lambda: V.tensor_tensor(out=ysb[:], in0=YT[:], in1=Fb[7][:], op=ALU.mult), extra=[(sty, sty.n)])
                cx.wait(nc.sync, ser.last)
                sty.mark(nc.sync.dma_start(out=yT_dram[y_row0 + ti * 128:y_row0 + (ti + 1) * 128, :], in_=ysb[:]))
            barrier(cx)
            if after_round is not None:
                after_round(ti)


def setup_consts(cx):
    nc = cx.nc
    C = {}
    C["ident_f"] = nc.alloc_sbuf_tensor("ident_f", [128, 128], F32)
    C["ones_f"] = nc.alloc_sbuf_tensor("ones_f", [128, 128], F32)
    C["ident_b"] = nc.alloc_sbuf_tensor("ident_b", [128, 128], BF16)
    C["ones_b"] = nc.alloc_sbuf_tensor("ones_b", [128, 128], BF16)
    t = cx.dve.mark(nc.vector.memset(C["ones_f"][:], 1.0))
    nc.vector.memset(C["ones_b"][:], 1.0)
    cx.wait(nc.gpsimd, t)
    t2 = cx.pool.mark(nc.gpsimd.affine_select(out=C["ident_f"][:], in_=C["ones_f"][:], pattern=[[-1, 128]], compare_op=ALU.is_equal,
                                              fill=0.0, base=0, channel_multiplier=1))
    cx.wait(nc.vector, t2)
    cx.dve.mark(nc.vector.tensor_copy(out=C["ident_b"][:], in_=C["ident_f"][:]))
    return C


CFG_FULL = dict(D=4096, F=11008, T=1024, NFH=8, NRT=8, HX=4, NM=256)
GROUPS = [[0, 1], [2, 3], [4, 5], [6, 7]]


def build_full(cfg):
    D, F, T, NFH, NRT, HX, NM = (cfg[k] for k in ("D", "F", "T", "NFH", "NRT", "HX", "NM"))
    S = 2 * T
    DC, FC = D // 128, F // 128
    FW, RW = NFH * 128, NRT * 128
    NTF = 3 * NFH
    NTR = 3 * NRT + 4 + 1
    NCOL = (NTF + NTR) * 128
    YC = FW + RW
    assert 2 * YC == D
    eps = 1e-6
    nc = bass.Bass("TRN2", target_bir_lowering=False)
    ein = lambda n, sh, dt=F32: nc.dram_tensor(n, sh, dt, kind="ExternalInput").ap()
    x = ein("x", [T, D])
    mem = ein("mem", [NM, D])
    wg1, wu1, wd1 = ein("wg1", [D, F]), ein("wu1", [D, F]), ein("wd1", [F, D])
    wg2, wu2, wd2 = ein("wg2", [D, F]), ein("wu2", [D, F]), ein("wd2", [F, D])
    w_in = ein("w_in_c", [D, NCOL])
    w_out = ein("w_out_p", [D, D])
    wq, wk, wv, wo = ein("wq", [D, D]), ein("wk", [D, D]), ein("wv", [D, D]), ein("wo", [D, D])
    gains = ein("gains", [128, 9, DC])
    rvec_d = ein("rvec", [128, NRT * NVT + 4])
    wup_d, aup_d, gup_d = ein("wup", [128, RW]), ein("aup", [128, RW]), ein("gup", [256, RW])
    fbias_d = ein("fbias", [128, 1])
    sel_d = ein("sel", [128, 2])
    out = nc.dram_tensor("out", [T, D], F32, kind="ExternalOutput").ap()
    xT = nc.dram_tensor("xT_s", [D, T], F32).ap()
    uT = nc.dram_tensor("uT_s", [F, T], BF16).ap()
    rawT = nc.dram_tensor("rawT_s", [D, T], F32).ap()
    qT = nc.dram_tensor("qT_s", [D, T], BF16).ap()
    HCH = max(1, (D * T * 2) // (1 << 20))
    HR = D // HCH
    hsnd = nc.dram_tensor("hsnd_s", [D, T], BF16)
    hrcv = nc.dram_tensor("hrcv_s", [HCH, 2 * HR, T], BF16)
    YCH = max(1, (YC * S * 2) // (1 << 20))
    YR = YC // YCH
    assert HR % 128 == 0 and YR % 128 == 0
    ysnd = nc.dram_tensor("ysnd_s", [YC, S], BF16)
    yrcv = nc.dram_tensor("yrcv_s", [YCH, 2 * YR, S], BF16)
    projF = nc.dram_tensor("projF_s", [NTF * 128, S], BF16).ap()
    projR = nc.dram_tensor("projR_s", [NTR * 128, S], F32).ap()
    kT_d = nc.dram_tensor("kT_s", [128, DC, NM], BF16).ap()
    vtok_d = nc.dram_tensor("vtok_s", [128, NM // 128, D], BF16).ap()

    cx = Cx(nc)
    C = setup_consts(cx)
    g_sb = nc.alloc_sbuf_tensor("g_sb", [128, 9, DC], F32)
    fb_sb = nc.alloc_sbuf_tensor("fb_sb", [128, 1], F32)
    sel_sb = nc.alloc_sbuf_tensor("sel_sb", [128, 2], F32)
    gl = dma_ev(cx, "gl")
    gl.mark(nc.sync.dma_start(out=g_sb[:], in_=gains))
    gl.mark(nc.sync.dma_start(out=fb_sb[:], in_=fbias_d))
    tok = gl.mark(nc.sync.dma_start(out=sel_sb[:], in_=sel_d))
    cx.wait(nc.vector, tok)
    for gi in (1, 7):
        cx.dve.mark(nc.vector.tensor_scalar(out=g_sb[:, gi, :], in0=g_sb[:, gi, :], scalar1=0.5, scalar2=None, op0=ALU.mult))
    barrier(cx)
    G = lambda i: g_sb[:, i, :]
    wsems = [dma_ev(cx, "w0"), dma_ev(cx, "w1")]
    wstate = {}
    cc = Ev(nc, "cc_ev", 1)
    ACTN = max(DC * T, FC * min(T, 512))

    with A(nc) as s1:
        actbuf = s1.sb("actbuf", [128, ACTN], BF16)
        act = actbuf[:, 0:DC * T].rearrange("p (k t) -> p k t", k=DC)
        norm_in_tokmajor(cx, x, T, DC, G(0), act, xT, C, eps)
        wbufs = [s1.sb("wb0", [128, 16384], BF16), s1.sb("wb1", [128, 16384], BF16)]
        ffn_up(cx, act, DC, T, wg1, wu1, FC, uT, wbufs, wsems, wstate, Wd_next=wd1)
        with A(nc) as sp:
            stA, stB = sp.ps("stA", [128, T]), sp.ps("stB", [128, T])
            ffn_down(cx, actbuf, FC, T, wd1, DC, uT, rawT, stA, C, wbufs, wsems, wstate)
            residual_pass(cx, rawT, stA, xT, T, DC, G(1), G(2), act, C, eps, stB)
        hs = dma_ev(cx, "hsnd")
        t = hs.mark(nc.sync.dma_start(out=hsnd.ap().rearrange("(k p) t -> p k t", p=128), in_=act))
        cx.wait(nc.gpsimd, t)
        barrier(cx)
        for c in range(HCH):
            t_cc = cc.mark(nc.gpsimd.collective_compute("AllGather", ALU.bypass, replica_groups=GROUPS,
                                                        ins=[hsnd.ap()[c * HR:(c + 1) * HR, :].opt()], outs=[hrcv.ap()[c].opt()]))
    wstate = {}
    with A(nc) as s0:
        wbufs = [s0.sb("wb0", [128, 16384], BF16), s0.sb("wb1", [128, 16384], BF16)]
        xattn_kv_phase(cx, DC, NM, mem, G(8), wk, wv, kT_d, vtok_d, C, eps, wbufs, wsems, wstate)
    wstate = {}
    with A(nc) as s1b:
        actbuf = s1b.sb("actbuf", [128, ACTN], BF16)
        wbufs = [s1b.sb("wb0", [128, 16384], BF16), s1b.sb("wb1", [128, 16384], BF16)]
        gemm_prefetch(cx, DC, [w_in], NTF + NTR, max(1, min(NTF + NTR, (16384 // DC) // 128)), wbufs, wsems, wstate)
        for e_ in (nc.gpsimd, nc.sync, nc.tensor, nc.scalar, nc.vector):
            cx.wait(e_, t_cc)
        halves = [[hrcv.ap()[c][th * HR:(th + 1) * HR, :] for c in range(HCH)] for th in range(2)]
        inproj_phase(cx, actbuf, None, S, DC, w_in, NTF, NTR, projF, projR, wbufs, wsems, wstate, halves=halves)
    wstate = {}
    ystate = {"next": 0, "tok": None}

    def issue_y(rows_done):
        while ystate["next"] < YCH and (ystate["next"] + 1) * YR <= rows_done:
            c = ystate["next"]
            ystate["tok"] = cc.mark(nc.gpsimd.collective_compute("AllGather", ALU.bypass, replica_groups=GROUPS,
                                                                 ins=[ysnd.ap()[c * YR:(c + 1) * YR, :].opt()], outs=[yrcv.ap()[c].opt()]))
            ystate["next"] += 1

    fox_phase(cx, S, NFH, projF, projR[(NTR - 1) * 128:NTR * 128, :], fb_sb[:], ysnd.ap(), C)
    issue_y(FW)
    rwkv_phase(cx, S, NRT, projR, rvec_d, wup_d, aup_d, gup_d, ysnd.ap(), FW, C, 64e-5, after_round=lambda ti: issue_y(FW + (ti + 1) * 128))
    assert ystate["next"] == YCH
    t = ystate["tok"]
    for e_ in (nc.gpsimd, nc.sync, nc.tensor, nc.scalar, nc.vector):
        cx.wait(e_, t)
    barrier(cx)
    with A(nc) as s2:
        actbuf = s2.sb("actbuf", [128, ACTN], BF16)
        act = actbuf[:, 0:DC * T].rearrange("p (k t) -> p k t", k=DC)
        wbufs = [s2.sb("wb0", [128, 16384], BF16), s2.sb("wb1", [128, 16384], BF16)]
        NGT1 = max(1, min(DC, (16384 // DC) // 128))
        NGTU = max(1, min(FC, (16384 // (2 * DC)) // 128))
        gemm_prefetch(cx, DC, [w_out], DC, NGT1, wbufs, wsems, wstate)
        with A(nc) as sb:
            g0s = [sb.sb("bl_g00", [128, T], BF16), sb.sb("bl_g01", [128, T], BF16)]
            g1s = [sb.sb("bl_g10", [128, T], BF16), sb.sb("bl_g11", [128, T], BF16)]
            ldb = [dma_ev(cx, "bl0"), dma_ev(cx, "bl1")]
            free = [None, None]
            for k in range(DC):
                b = k % 2
                rank, loc = (k * 128) // YC, (k * 128) % YC
                cch, sub = loc // YR, loc % YR
                yk = yrcv.ap()[cch][rank * YR + sub:rank * YR + sub + 128, :]
                cx.wait(nc.sync, free[b])
                ldb[b].mark(nc.sync.dma_start(out=g0s[b][:], in_=yk[:, 0:T]))
                lt = ldb[b].mark(nc.sync.dma_start(out=g1s[b][:], in_=yk[:, T:2 * T]))
                cx.wait(nc.vector, lt)
                d = cx.dve.mark(nc.vector.tensor_scalar(out=g0s[b][:], in0=g0s[b][:], scalar1=sel_sb[:, 0:1], scalar2=None, op0=ALU.mult))
                cx.wait(nc.vector, d)
                free[b] = cx.dve.mark(nc.vector.scalar_tensor_tensor(out=act[:, k, :], in0=g1s[b][:], scalar=sel_sb[:, 1:2], in1=g0s[b][:],
                                                                     op0=ALU.mult, op1=ALU.add))
            barrier(cx)
        with A(nc) as sp:
            stA, stB = sp.ps("stA", [128, T]), sp.ps("stB", [128, T])
            proj_raw_stats(cx, actbuf, T, DC, w_out, rawT, stA, C, wbufs, wsems, wstate)
            gemm_prefetch(cx, DC, [wq], DC, NGT1, wbufs, wsems, wstate)
            residual_pass(cx, rawT, stA, xT, T, DC, G(3), G(4), act, C, eps, stB)
        xattn_phase(cx, actbuf, T, DC, HX, NM, kT_d, vtok_d, wq, qT, C, wbufs, wsems, wstate, wo_next=wo)
        with A(nc) as sp:
            stA, stB = sp.ps("stA", [128, T]), sp.ps("stB", [128, T])
            proj_raw_stats(cx, actbuf, T, DC, wo, rawT, stA, C, wbufs, wsems, wstate)
            gemm_prefetch(cx, DC, [wg2, wu2], FC, NGTU, wbufs, wsems, wstate)
            residual_pass(cx, rawT, stA, xT, T, DC, G(5), G(6), act, C, eps, stB)
        ffn_up(cx, act, DC, T, wg2, wu2, FC, uT, wbufs, wsems, wstate, Wd_next=wd2)
        with A(nc) as sp:
            stA, stB = sp.ps("stA", [128, T]), sp.ps("stB", [128, T])
            ffn_down(cx, actbuf, FC, T, wd2, DC, uT, rawT, stA, C, wbufs, wsems, wstate)
            residual_pass(cx, rawT, stA, xT, T, DC, G(7), G(0), act, C, eps, stB, out_dram=out)
    barrier(cx)
    return nc


def host_layout(cfg, inp):
    D, F, T, NFH, NRT, HX, NM = (cfg[k] for k in ("D", "F", "T", "NFH", "NRT", "HX", "NM"))
    DC = D // 128
    FW, RW = NFH * 128, NRT * 128
    FOXW, RWW = 2 * FW, 2 * RW
    f32 = lambda a: np.ascontiguousarray(np.asarray(a, dtype=np.float32))
    tl = lambda v: np.ascontiguousarray(f32(v).reshape(-1, 128).T)
    w_in = f32(inp["w_in"][0])
    R0 = 3 * FOXW + 2 * NFH
    gl = [inp[k][0] for k in ("ffn1_pre_g", "ffn1_post_g", "mix_pre_g", "mix_post_g", "xattn_pre_g", "xattn_post_g",
                              "ffn2_pre_g", "ffn2_post_g", "mem_norm_g")]
    gains = np.ascontiguousarray(np.stack([tl(g) for g in gl], axis=1))
    w_out = f32(inp["w_out"][0])
    w_out_p = np.ascontiguousarray(np.concatenate([w_out[0:FW], w_out[FOXW:FOXW + RW], w_out[FW:FOXW], w_out[FOXW + RW:FOXW + RWW]], 0))
    shared = dict(
        wg1=f32(inp["ffn1_w_gate"][0]), wu1=f32(inp["ffn1_w_up"][0]), wd1=f32(inp["ffn1_w_down"][0]),
        wg2=f32(inp["ffn2_w_gate"][0]), wu2=f32(inp["ffn2_w_up"][0]), wd2=f32(inp["ffn2_w_down"][0]),
        w_out_p=w_out_p, wq=f32(inp["xattn_wq"][0]), wk=f32(inp["xattn_wk"][0]), wv=f32(inp["xattn_wv"][0]), wo=f32(inp["xattn_wo"][0]),
        gains=gains)
    mu = f32(inp["rwkv_mu"][0])
    per_hh = []
    for hh in range(2):
        cols = []
        for blk in range(3):
            cols.append(w_in[:, blk * FOXW + hh * FW: blk * FOXW + (hh + 1) * FW])
        for blk in range(3):
            cols.append(w_in[:, R0 + blk * RWW + hh * RW: R0 + blk * RWW + (hh + 1) * RW])
        pad = lambda a, n: np.concatenate([a, np.zeros((a.shape[0], n - a.shape[1]), np.float32)], 1)
        L0 = R0 + 3 * RWW
        cols.append(pad(w_in[:, L0:L0 + 96], 128))
        cols.append(pad(w_in[:, L0 + 96:L0 + 192], 128))
        cols.append(w_in[:, L0 + 192:L0 + 448])
        cols.append(pad(w_in[:, 3 * FOXW + hh * NFH: 3 * FOXW + (hh + 1) * NFH], 128))
        w_in_c = np.ascontiguousarray(np.concatenate(cols, 1))
        ch = slice(hh * RW, (hh + 1) * RW)
        rvec = np.zeros((128, NRT * NVT + 4), np.float32)
        vecs = [mu[0:RWW][ch], mu[RWW:2 * RWW][ch], mu[2 * RWW:3 * RWW][ch], inp["rwkv_w0"][0][ch], inp["rwkv_a0"][0][ch],
                inp["rwkv_k_k"][0][ch], inp["rwkv_k_a"][0][ch], inp["rwkv_r_k"][0][ch], inp["rwkv_ln_w"][0][ch], inp["rwkv_ln_b"][0][ch]]
        for i, v in enumerate(vecs):
            rvec[:, i:NRT * NVT:NVT] = tl(v)
        GV = NRT * NVT
        M0 = 3 * RWW
        rvec[0:96, GV] = mu[M0:M0 + 96]
        rvec[0:96, GV + 1] = mu[M0 + 96:M0 + 192]
        rvec[:, GV + 2] = mu[M0 + 192:M0 + 320]
        rvec[:, GV + 3] = mu[M0 + 320:M0 + 448]
        wup = np.zeros((128, RW), np.float32)
        wup[:96] = f32(inp["rwkv_w_up"][0])[:, ch]
        aup = np.zeros((128, RW), np.float32)
        aup[:96] = f32(inp["rwkv_a_up"][0])[:, ch]
        gup = np.ascontiguousarray(f32(inp["rwkv_g_up"][0])[:, ch])
        fb = np.zeros((128, 1), np.float32)
        fb[0:NFH, 0] = f32(inp["fox_f_bias"][0])[hh * NFH:(hh + 1) * NFH]
        sel = np.zeros((128, 2), np.float32)
        sel[:, hh] = 1.0
        per_hh.append(dict(w_in_c=w_in_c, rvec=rvec, wup=wup, aup=aup, gup=gup, fbias=fb, sel=sel))
    xs = f32(inp["x"])
    mems = f32(inp["mem"])
    in_maps = []
    for core in range(8):
        b, hh = core // 2, core % 2
        m = dict(shared)
        m.update(per_hh[hh])
        m["x"] = np.ascontiguousarray(xs[b, hh * T:(hh + 1) * T, :])
        m["mem"] = np.ascontiguousarray(mems[b])
        in_maps.append(m)
    return in_maps


def run_cfg(cfg, inp, trace=False):
    nc = build_full(cfg)
    in_maps = host_layout(cfg, inp)
    res = run_bass_kernel_spmd(nc, in_maps, core_ids=list(range(8)), **({"trace": True} if trace else {}))
    T, D = cfg["T"], cfg["D"]
    outp = np.zeros((4, 2 * T, D), np.float32)
    for core in range(8):
        b, hh = core // 2, core % 2
        outp[b, hh * T:(hh + 1) * T, :] = res.results[core]["out"]
    return outp, res


def kernel(**inputs):
    outp, _ = run_cfg(CFG_FULL, inputs)
    return outp
```
